# Optimizing a Trainium2 kernel written in Bass

```python
import math
import jax, jax.numpy as jnp
from jax import lax
import numpy as np

D_MODEL = 1024
BATCH = 8
SEQ = 2048
DEPTH = 1

PLE_DIM = 256
Q_BLOCK = 128

NSA_HEADS = 8
NSA_GROUPS = 2
NSA_HPG = NSA_HEADS // NSA_GROUPS
HEAD_DIM = 64
CMP_BLOCK = 32
CMP_STRIDE = 16
CMP_HIDDEN = 256
SEL_BLOCK = 64
SEL_TOPK = 8
WINDOW = 512
SEL_FORCE = 1.0e4

DIFF_HEADS = 4
DIFF_DH = 64
DIFF_VDIM = 2 * DIFF_DH

PEER_HEADS = 8
PEER_NKEYS = 128
PEER_NEXPERTS = PEER_NKEYS * PEER_NKEYS
PEER_DKEY = 256
PEER_TOPK = 16
PEER_CHUNK = 128

NSA_WIDTH = NSA_HEADS * HEAD_DIM
NSA_KV = NSA_GROUPS * HEAD_DIM
DIFF_QK = DIFF_HEADS * DIFF_DH
DIFF_WIDTH = DIFF_HEADS * DIFF_VDIM
IN_SPLITS = (NSA_WIDTH, 6 * NSA_KV, 3 * NSA_HEADS, 2 * DIFF_QK, 2 * DIFF_QK, DIFF_WIDTH, D_MODEL, D_MODEL)
IN_COLS = NSA_WIDTH + 6 * NSA_KV + 3 * NSA_HEADS + 4 * DIFF_QK + DIFF_WIDTH + 2 * D_MODEL

DEEPNORM_ALPHA = (2.0 * DEPTH) ** 0.25
DEEPNORM_BETA = (8.0 * DEPTH) ** -0.25
NEG = -1.0e30

kernel_name = 'hybrid_nsa_diffattn_peer_block'


def alibi_slopes(n):
    return jnp.exp2(-8.0 * jnp.arange(1, n + 1, dtype=jnp.float32) / n)


def layer_norm(x, g, b, eps=1e-5):
    xf = x.astype(jnp.float32)
    mu = jnp.mean(xf, axis=-1, keepdims=True)
    var = jnp.mean(jnp.square(xf - mu), axis=-1, keepdims=True)
    return ((xf - mu) * lax.rsqrt(var + eps) * g.astype(jnp.float32) + b.astype(jnp.float32)).astype(x.dtype)


def masked_softmax(s, mask):
    s = jnp.where(mask, s, NEG)
    m = jnp.max(s, axis=-1, keepdims=True)
    e = jnp.where(mask, jnp.exp(s - m), 0.0)
    return e / jnp.maximum(jnp.sum(e, axis=-1, keepdims=True), 1e-30)


def compress_kv(kv, pos_emb, w1, w2):
    S = kv.shape[2]
    nc = (S - CMP_BLOCK) // CMP_STRIDE + 1
    idx = jnp.arange(nc)[:, None] * CMP_STRIDE + jnp.arange(CMP_BLOCK)[None, :]
    blocks = kv[:, :, idx] + pos_emb
    flat = blocks.reshape(blocks.shape[:3] + (CMP_BLOCK * HEAD_DIM,))
    return jax.nn.gelu(flat @ w1, approximate=False) @ w2


def cmp_to_sel_weights(nc, n_sel):
    c0 = jnp.arange(nc)[:, None] * CMP_STRIDE
    s0 = jnp.arange(n_sel)[None, :] * SEL_BLOCK
    ov = jnp.minimum(c0 + CMP_BLOCK, s0 + SEL_BLOCK) - jnp.maximum(c0, s0)
    return jnp.clip(ov, 0, None).astype(jnp.float32) / CMP_BLOCK


def token_mixers(h, w_in, cmp_pos_k, cmp_pos_v, cmp_k_w1, cmp_k_w2, cmp_v_w1, cmp_v_w2,
                 lam_q1, lam_k1, lam_q2, lam_k2, diff_norm_g, w_branch_nsa, w_branch_diff, w_out, layer_idx):
    B, S, _ = h.shape
    dt = h.dtype
    f32 = jnp.float32
    nq = S // Q_BLOCK
    proj = h @ w_in
    cuts = np.cumsum(IN_SPLITS)[:-1].tolist()
    q_n, kv_n, g_n, q_d, k_d, v_d, gate_a, gate_b = jnp.split(proj, cuts, axis=-1)

    q_n = q_n.reshape(B, S, NSA_GROUPS, NSA_HPG, HEAD_DIM).transpose(0, 2, 3, 1, 4) * (HEAD_DIM ** -0.5)
    kv_n = kv_n.reshape(B, S, 6, NSA_GROUPS, HEAD_DIM).transpose(2, 0, 3, 1, 4)
    k_c, v_c, k_s, v_s, k_w, v_w = [kv_n[j] for j in range(6)]
    kc = compress_kv(k_c, cmp_pos_k, cmp_k_w1, cmp_k_w2)
    vc = compress_kv(v_c, cmp_pos_v, cmp_v_w1, cmp_v_w2)
    nc = kc.shape[2]
    c_end = jnp.arange(nc) * CMP_STRIDE + (CMP_BLOCK - 1)
    n_sel = S // SEL_BLOCK
    k_sel = min(SEL_TOPK, n_sel)
    sel_w = cmp_to_sel_weights(nc, n_sel)
    ks_blk = k_s.reshape(B, NSA_GROUPS, n_sel, SEL_BLOCK, HEAD_DIM)
    vs_blk = v_s.reshape(B, NSA_GROUPS, n_sel, SEL_BLOCK, HEAD_DIM)
    kw_pad = jnp.pad(k_w, ((0, 0), (0, 0), (WINDOW, 0), (0, 0)))
    vw_pad = jnp.pad(v_w, ((0, 0), (0, 0), (WINDOW, 0), (0, 0)))
    gates_n = jax.nn.sigmoid(g_n.reshape(B, nq, Q_BLOCK, NSA_GROUPS, NSA_HPG, 3)).transpose(1, 0, 3, 4, 2, 5)
    slopes_n = alibi_slopes(NSA_HEADS).reshape(NSA_GROUPS, NSA_HPG)[None, :, :, None, None]
    q_n_blk = jnp.moveaxis(q_n.reshape(B, NSA_GROUPS, NSA_HPG, nq, Q_BLOCK, HEAD_DIM), 3, 0)
    bi = jnp.arange(B)[:, None, None, None]
    gi = jnp.arange(NSA_GROUPS)[None, :, None, None]

    q_d = q_d.reshape(B, S, 2, DIFF_HEADS, DIFF_DH) * (DIFF_DH ** -0.5)
    k_d = k_d.reshape(B, S, 2, DIFF_HEADS, DIFF_DH)
    q1_blk = jnp.moveaxis(q_d[:, :, 0].transpose(0, 2, 1, 3).reshape(B, DIFF_HEADS, nq, Q_BLOCK, DIFF_DH), 2, 0)
    q2_blk = jnp.moveaxis(q_d[:, :, 1].transpose(0, 2, 1, 3).reshape(B, DIFF_HEADS, nq, Q_BLOCK, DIFF_DH), 2, 0)
    k1 = k_d[:, :, 0].transpose(0, 2, 1, 3)
    k2 = k_d[:, :, 1].transpose(0, 2, 1, 3)
    v_d = v_d.reshape(B, S, DIFF_HEADS, DIFF_VDIM).transpose(0, 2, 1, 3)
    lam_init = 0.8 - 0.6 * math.exp(-0.3 * layer_idx)
    lam = (jnp.exp(jnp.sum(lam_q1.astype(f32) * lam_k1.astype(f32)))
           - jnp.exp(jnp.sum(lam_q2.astype(f32) * lam_k2.astype(f32))) + lam_init)
    slopes_d = alibi_slopes(DIFF_HEADS)[None, :, None, None]
    key_pos = jnp.arange(S)
    norm_g = diff_norm_g.astype(f32)

    def block(args):
        i, qn, gn, qa, qb = args
        t = i * Q_BLOCK + jnp.arange(Q_BLOCK)
        dist_c = (t[:, None] - c_end[None, :]).astype(f32)
        s = jnp.einsum('bghqd,bgcd->bghqc', qn, kc).astype(f32) - slopes_n * dist_c
        p_cmp = masked_softmax(s, dist_c >= 0)
        o_cmp = jnp.einsum('bghqc,bgcd->bghqd', p_cmp.astype(dt), vc)
        imp = jnp.einsum('bghqc,cj->bgqj', p_cmp, sel_w)
        jb = jnp.arange(n_sel)[None, :]
        cur = (t // SEL_BLOCK)[:, None]
        allowed = jb * SEL_BLOCK <= t[:, None]
        forced = (jb == 0) | (jb == cur) | (jb == cur - 1)
        score = jnp.where(forced, SEL_FORCE, jnp.where(allowed, imp, -SEL_FORCE))
        _, sel = lax.top_k(score, k_sel)
        L = k_sel * SEL_BLOCK
        ks = ks_blk[bi, gi, sel].reshape(B, NSA_GROUPS, Q_BLOCK, L, HEAD_DIM)
        vs = vs_blk[bi, gi, sel].reshape(B, NSA_GROUPS, Q_BLOCK, L, HEAD_DIM)
        pos = (sel[..., None] * SEL_BLOCK + jnp.arange(SEL_BLOCK)).reshape(B, NSA_GROUPS, Q_BLOCK, L)
        dist_s = (t[None, None, :, None] - pos).astype(f32)[:, :, None]
        s = jnp.einsum('bghqd,bgqld->bghql', qn, ks).astype(f32) - slopes_n * dist_s
        o_slc = jnp.einsum('bghql,bgqld->bghqd', masked_softmax(s, dist_s >= 0).astype(dt), vs)
        kw = lax.dynamic_slice_in_dim(kw_pad, i * Q_BLOCK, WINDOW + Q_BLOCK, axis=2)
        vw = lax.dynamic_slice_in_dim(vw_pad, i * Q_BLOCK, WINDOW + Q_BLOCK, axis=2)
        wpos = i * Q_BLOCK - WINDOW + jnp.arange(WINDOW + Q_BLOCK)
        dist_w = t[:, None] - wpos[None, :]
        mask_w = (dist_w >= 0) & (dist_w < WINDOW) & (wpos[None, :] >= 0)
        s = jnp.einsum('bghqd,bgld->bghql', qn, kw).astype(f32) - slopes_n * dist_w.astype(f32)
        o_win = jnp.einsum('bghql,bgld->bghqd', masked_softmax(s, mask_w).astype(dt), vw)
        o_nsa = gn[..., 0:1] * o_cmp + gn[..., 1:2] * o_slc + gn[..., 2:3] * o_win
        o_nsa = o_nsa.transpose(0, 3, 1, 2, 4).reshape(B, Q_BLOCK, NSA_WIDTH)
        dist_d = (t[:, None] - key_pos[None, :]).astype(f32)
        mask_d = dist_d >= 0
        bias = -slopes_d * dist_d
        a1 = masked_softmax(jnp.einsum('bhqd,bhkd->bhqk', qa, k1).astype(f32) + bias, mask_d)
        a2 = masked_softmax(jnp.einsum('bhqd,bhkd->bhqk', qb, k2).astype(f32) + bias, mask_d)
        o = jnp.einsum('bhqk,bhkv->bhqv', (a1 - lam * a2).astype(dt), v_d).astype(f32)
        o = o * lax.rsqrt(jnp.mean(o * o, axis=-1, keepdims=True) + 1e-5) * norm_g * (1.0 - lam_init)
        o_diff = o.astype(dt).transpose(0, 2, 1, 3).reshape(B, Q_BLOCK, DIFF_WIDTH)
        return o_nsa, o_diff

    o_nsa, o_diff = lax.map(block, (jnp.arange(nq), q_n_blk, gates_n, q1_blk, q2_blk))
    o_nsa = jnp.moveaxis(o_nsa, 0, 1).reshape(B, S, NSA_WIDTH)
    o_diff = jnp.moveaxis(o_diff, 0, 1).reshape(B, S, DIFF_WIDTH)
    merged = (jax.nn.sigmoid(gate_a) * (o_nsa @ w_branch_nsa)
              + jax.nn.sigmoid(gate_b) * (o_diff @ w_branch_diff))
    return merged @ w_out


def peer_ffn(h, wq, subkeys1, subkeys2, u_tab, v_tab):
    B, S, D = h.shape
    f32 = jnp.float32
    q = (h @ wq).reshape(B, S, PEER_HEADS, 2, PEER_DKEY // 2)
    s1 = jnp.einsum('bshc,kc->bshk', q[:, :, :, 0], subkeys1).astype(f32)
    s2 = jnp.einsum('bshc,kc->bshk', q[:, :, :, 1], subkeys2).astype(f32)
    v1, i1 = lax.top_k(s1, PEER_TOPK)
    v2, i2 = lax.top_k(s2, PEER_TOPK)
    cand = (v1[..., :, None] + v2[..., None, :]).reshape(B, S, PEER_HEADS, PEER_TOPK * PEER_TOPK)
    cid = (i1[..., :, None] * PEER_NKEYS + i2[..., None, :]).reshape(B, S, PEER_HEADS, PEER_TOPK * PEER_TOPK)
    sc, pick = lax.top_k(cand, PEER_TOPK)
    eid = jnp.take_along_axis(cid, pick, axis=-1)
    g = jax.nn.softmax(sc, axis=-1).astype(h.dtype)
    n_chunk = (B * S) // PEER_CHUNK
    xs = h.reshape(n_chunk, PEER_CHUNK, D)
    es = eid.reshape(n_chunk, PEER_CHUNK, PEER_HEADS, PEER_TOPK)
    gs = g.reshape(n_chunk, PEER_CHUNK, PEER_HEADS, PEER_TOPK)

    def chunk(args):
        xc, ec, gc = args
        u = u_tab[ec]
        v = v_tab[ec]
        a = jax.nn.gelu(jnp.einsum('cd,chkd->chk', xc, u), approximate=False)
        return jnp.einsum('chk,chkd->cd', gc * a, v)

    return lax.map(chunk, (xs, es, gs)).reshape(B, S, D)


def setup_inputs(seed: int = 0) -> dict:
    key = jax.random.key(seed)
    ks = jax.random.split(key, 32)
    f32 = jnp.float32
    L = DEPTH

    def nrm(k, shape, scale):
        return jax.random.normal(k, shape, f32) * scale

    return {
        'x': nrm(ks[0], (BATCH, SEQ, D_MODEL), 1.0),
        'p': nrm(ks[1], (DEPTH, BATCH, SEQ, PLE_DIM), 1.0),
        'w_in': nrm(ks[2], (L, D_MODEL, IN_COLS), D_MODEL ** -0.5),
        'cmp_pos_k': nrm(ks[3], (L, CMP_BLOCK, HEAD_DIM), 0.1),
        'cmp_pos_v': nrm(ks[4], (L, CMP_BLOCK, HEAD_DIM), 0.1),
        'cmp_k_w1': nrm(ks[5], (L, CMP_BLOCK * HEAD_DIM, CMP_HIDDEN), (CMP_BLOCK * HEAD_DIM) ** -0.5),
        'cmp_k_w2': nrm(ks[6], (L, CMP_HIDDEN, HEAD_DIM), CMP_HIDDEN ** -0.5),
        'cmp_v_w1': nrm(ks[7], (L, CMP_BLOCK * HEAD_DIM, CMP_HIDDEN), (CMP_BLOCK * HEAD_DIM) ** -0.5),
        'cmp_v_w2': nrm(ks[8], (L, CMP_HIDDEN, HEAD_DIM), CMP_HIDDEN ** -0.5),
        'lam_q1': nrm(ks[9], (L, DIFF_DH), 0.1),
        'lam_k1': nrm(ks[10], (L, DIFF_DH), 0.1),
        'lam_q2': nrm(ks[11], (L, DIFF_DH), 0.1),
        'lam_k2': nrm(ks[12], (L, DIFF_DH), 0.1),
        'diff_norm_g': 1.0 + nrm(ks[13], (L, DIFF_VDIM), 0.02),
        'w_branch_nsa': nrm(ks[14], (L, NSA_WIDTH, D_MODEL), NSA_WIDTH ** -0.5 * DEEPNORM_BETA),
        'w_branch_diff': nrm(ks[15], (L, DIFF_WIDTH, D_MODEL), DIFF_WIDTH ** -0.5 * DEEPNORM_BETA),
        'w_out': nrm(ks[16], (L, D_MODEL, D_MODEL), D_MODEL ** -0.5 * DEEPNORM_BETA),
        'ln1_g': 1.0 + nrm(ks[17], (L, D_MODEL), 0.02),
        'ln1_b': nrm(ks[18], (L, D_MODEL), 0.02),
        'peer_wq': nrm(ks[19], (L, D_MODEL, PEER_HEADS * PEER_DKEY), D_MODEL ** -0.5),
        'peer_subkeys1': nrm(ks[20], (L, PEER_NKEYS, PEER_DKEY // 2), (PEER_DKEY // 2) ** -0.5),
        'peer_subkeys2': nrm(ks[21], (L, PEER_NKEYS, PEER_DKEY // 2), (PEER_DKEY // 2) ** -0.5),
        'peer_u': nrm(ks[22], (L, PEER_NEXPERTS, D_MODEL), D_MODEL ** -0.5),
        'peer_v': nrm(ks[23], (L, PEER_NEXPERTS, D_MODEL), DEEPNORM_BETA * PEER_HEADS ** -0.5),
        'ln2_g': 1.0 + nrm(ks[24], (L, D_MODEL), 0.02),
        'ln2_b': nrm(ks[25], (L, D_MODEL), 0.02),
        'ple_w_proj': nrm(ks[26], (L, PLE_DIM, D_MODEL), PLE_DIM ** -0.5 * DEEPNORM_BETA),
        'ple_w_gate': nrm(ks[27], (L, D_MODEL, D_MODEL), D_MODEL ** -0.5),
    }


def reference(x, p, w_in, cmp_pos_k, cmp_pos_v, cmp_k_w1, cmp_k_w2, cmp_v_w1, cmp_v_w2,
              lam_q1, lam_k1, lam_q2, lam_k2, diff_norm_g, w_branch_nsa, w_branch_diff, w_out,
              ln1_g, ln1_b, peer_wq, peer_subkeys1, peer_subkeys2, peer_u, peer_v, ln2_g, ln2_b,
              ple_w_proj, ple_w_gate):
    h = x
    for i in range(DEPTH):
        mix = token_mixers(h, w_in[i], cmp_pos_k[i], cmp_pos_v[i], cmp_k_w1[i], cmp_k_w2[i],
                           cmp_v_w1[i], cmp_v_w2[i], lam_q1[i], lam_k1[i], lam_q2[i], lam_k2[i],
                           diff_norm_g[i], w_branch_nsa[i], w_branch_diff[i], w_out[i], i)
        h = layer_norm(DEEPNORM_ALPHA * h + mix, ln1_g[i], ln1_b[i])
        ffn = peer_ffn(h, peer_wq[i], peer_subkeys1[i], peer_subkeys2[i], peer_u[i], peer_v[i])
        h = layer_norm(DEEPNORM_ALPHA * h + ffn, ln2_g[i], ln2_b[i])
        h = h + jax.nn.sigmoid(h @ ple_w_gate[i]) * (p[i] @ ple_w_proj[i])
    return h
```

```python
import numpy as np
import concourse.bass as bass
import concourse.mybir as mybir
from concourse.bass_utils import run_bass_kernel_spmd

F32 = mybir.dt.float32
U32 = mybir.dt.uint32
AF = mybir.ActivationFunctionType
ALU = mybir.AluOpType
AX = mybir.AxisListType

S = 2048
D = 1024
NT = 16
BIGS = 1.0e5
SLOPE_N = [2.0 ** (-(h + 1)) for h in range(8)]
SLOPE_D = [2.0 ** (-2 * (h + 1)) for h in range(4)]
LAM_INIT = 0.2
ALPHA = 2.0 ** 0.25


class FW:
    def __init__(self, nc):
        self.nc = nc
        self.eng = {'pe': nc.tensor, 'dve': nc.vector, 'act': nc.scalar,
                    'pool': nc.gpsimd, 'sp': nc.sync}
        self.sems = {}
        self.cnt = {}
        self.waited = {e: {} for e in self.eng}
        self.state = {}
        self._cms = []
        for e in self.eng:
            self._mksem('E_' + e)
        self.n_inst = 0
        self._rr = {}

    def _mksem(self, name):
        cm = self.nc.semaphore(name)
        s = cm.__enter__()
        self._cms.append(cm)
        self.sems[name] = s
        self.cnt[name] = 0
        return s

    def close(self):
        for cm in reversed(self._cms):
            cm.__exit__(None, None, None)

    def _deps(self, reads, writes):
        deps = {}

        def add(d):
            if d is None:
                return
            s, v = d
            if deps.get(s, 0) < v:
                deps[s] = v
        for k in reads:
            st = self.state.get(k)
            if st:
                add(st['w'])
        for k in writes:
            st = self.state.get(k)
            if st:
                add(st['w'])
                for s, v in st['r'].items():
                    add((s, v))
        return deps

    def _wait(self, e, deps):
        own = 'E_' + e
        for s, v in deps.items():
            if e == 'pe' and s == own:
                continue
            if self.waited[e].get(s, 0) >= v:
                continue
            self.eng[e].wait_ge(self.sems[s], v)
            self.waited[e][s] = v

    def _record(self, reads, writes, tok):
        for k in reads:
            st = self.state.setdefault(k, {'w': None, 'r': {}})
            s, v = tok
            if st['r'].get(s, 0) < v:
                st['r'][s] = v
        for k in writes:
            self.state[k] = {'w': tok, 'r': {}}

    def op(self, e, fn, reads=(), writes=(), signal=True):
        deps = self._deps(reads, writes)
        self._wait(e, deps)
        inst = fn(self.eng[e])
        own = 'E_' + e
        if signal:
            self.cnt[own] += 1
            inst.then_inc(self.sems[own], 1)
            tok = (own, self.cnt[own])
        else:
            tok = (own, self.cnt[own] + 1)
        self._record(reads, writes, tok)
        self.n_inst += 1
        return inst

    POOLS = {'ld_w': 8}

    def dma(self, q, fn, sem, reads=(), writes=()):
        if sem in self.POOLS:
            n = self._rr.get(sem, 0)
            self._rr[sem] = n + 1
            sem = '%s_%d' % (sem, n % self.POOLS[sem])
        if sem not in self.sems:
            self._mksem(sem)
        deps = self._deps(reads, writes)
        if self.cnt[sem] > 0:
            deps[sem] = max(deps.get(sem, 0), self.cnt[sem])
        self._wait(q, deps)
        inst = fn(self.eng[q])
        self.cnt[sem] += 16
        inst.then_inc(self.sems[sem], 16)
        self._record(reads, writes, (sem, self.cnt[sem]))
        self.n_inst += 1
        return inst

    def wait_all(self, e, keys):
        self._wait(e, self._deps(keys, keys))

    def barrier(self):
        for e in self.eng:
            own = 'E_' + e
            for s, v in self.cnt.items():
                if v == 0 or (s == own and e in ('pe', 'sp')):
                    continue
                if self.waited[e].get(s, 0) >= v:
                    continue
                self.eng[e].wait_ge(self.sems[s], v)
                self.waited[e][s] = v


class T:
    def __init__(self, t, ncols, name):
        self.t, self.n, self.name = t, ncols, name

    def ap(self, col=0, dims=None, p0=0, np_=128):
        if dims is None:
            dims = [(1, self.n - col)]
        return bass.AP(self.t, p0 * self.n + col, [[self.n, np_]] + [[s, n] for s, n in dims])

    def c(self, col, n, p0=0, np_=128):
        return self.ap(col, [(1, n)], p0, np_)


def dram2(t, rowlen, row0, nrows, col, dims):
    return bass.AP(t, row0 * rowlen + col, [[rowlen, nrows]] + [[s, n] for s, n in dims])


def pack_fm(w, cols):
    sub = w[:, cols]
    M = sub.shape[1]
    return np.ascontiguousarray(sub.reshape(8, 128, M).transpose(1, 0, 2).reshape(128, 8 * M))


def pack_rows(w, kt):
    M = w.shape[1]
    return np.ascontiguousarray(w.reshape(kt, 128, M).transpose(1, 0, 2).reshape(128, kt * M))


def host_consts():
    c = {}
    c['ident'] = np.eye(128, dtype=np.float32)
    kl = np.arange(128)[:, None]
    m = np.arange(2048)[None, :]
    v = m - kl
    c['tstrip'] = np.where(v >= 0, -v, -BIGS).astype(np.float32)
    m2 = np.arange(1024)[None, :]
    v2 = m2 - kl
    c['wstrip'] = np.where((v2 >= 0) & (v2 < 512), -v2, -BIGS).astype(np.float32)
    cend = 16 * np.arange(128)[:, None] + 31
    vq = np.arange(2048)[None, :] - cend
    cb = np.where(vq >= 0, -vq, -BIGS).astype(np.float32)
    cb[127, :] = -BIGS
    c['cmpbias'] = cb
    e = np.zeros((32, 2048), np.float32)
    e[np.arange(2048) // 64, np.arange(2048)] = BIGS
    c['eblk'] = e
    t = (np.arange(16)[None, :, None] * 128 + np.arange(128)[:, None, None])
    jb = np.arange(32)[None, None, :]
    cur = t // 64
    allowed = jb <= cur
    forced = (jb == 0) | (jb == cur) | (jb == cur - 1)
    A = (allowed & ~forced).astype(np.float32)
    B = np.where(forced, 1.0e4, np.where(allowed, 0.0, -1.0e4)).astype(np.float32)
    c['selA'] = np.ascontiguousarray(A.reshape(128, 512))
    c['selB'] = np.ascontiguousarray(B.reshape(128, 512))
    c0 = np.arange(128)[:, None] * 16
    s0 = np.arange(32)[None, :] * 64
    ov = np.minimum(c0 + 32, s0 + 64) - np.maximum(c0, s0)
    sw = (np.clip(ov, 0, None) / 32.0).astype(np.float32)
    sw[127, :] = 0
    c['selw'] = sw
    qs = np.zeros((128, 8), np.float32)
    for i in range(4):
        for p in range(128):
            qs[p, i] = 0.125 / SLOPE_N[2 * i + p // 64]
            qs[p, 4 + i] = 0.125 / SLOPE_D[2 * (i % 2) + p // 64]
    c['qscale'] = qs
    bt = np.zeros((128, 8 * 36), np.float32)
    for si in range(8):
        for mi in range(36):
            bt[:, si * 36 + mi] = (2.0 ** (-(si + 1))) * (np.arange(128) - 64 * (mi - 2))
    c['btab'] = bt
    ql_ = np.arange(128)[None, :]
    tri = np.where(kl > ql_, -BIGS, 0.0).astype(np.float32)
    anti = np.where(ql_ >= kl, -BIGS, 0.0).astype(np.float32)
    c['masks'] = np.ascontiguousarray(np.concatenate([tri, anti], axis=1))
    c['e01'] = (e > 0).astype(np.float32)
    c['iota16'] = np.tile(np.arange(16, dtype=np.float32)[None, :], (128, 1))
    return c


def pack_inputs(inp, b):
    w_in = inp['w_in'][0]
    x = inp['x'][b]
    d = {}
    d['xT'] = pack_rows(np.ascontiguousarray(x.T), 8)
    d['xtm'] = np.ascontiguousarray(x)
    kv0 = 512
    tiles = [np.arange(128 * i, 128 * i + 128) for i in range(4)]
    tiles.append(kv0 + np.arange(128))
    tiles.append(kv0 + 128 + np.arange(128))
    for j in (2, 4):
        for g in range(2):
            cg = kv0 + j * 128 + g * 64 + np.arange(64)
            tiles.append(np.concatenate([cg, cg]))
    d['wA_fm'] = np.concatenate([pack_fm(w_in, t) for t in tiles], axis=1)
    tmcols = np.concatenate([kv0 + 3 * 128 + np.arange(128), kv0 + 5 * 128 + np.arange(128),
                             kv0 + 768 + np.arange(24)])
    d['wA_tm'] = pack_fm(w_in, tmcols)
    for nm, w1, w2, pos in (('k', 'cmp_k_w1', 'cmp_k_w2', 'cmp_pos_k'), ('v', 'cmp_v_w1', 'cmp_v_w2', 'cmp_pos_v')):
        W1 = inp[w1][0].reshape(32, 64, 256)
        W1p = W1.transpose(1, 0, 2).reshape(64, 32 * 256)
        d['w1' + nm] = np.ascontiguousarray(np.concatenate([W1p, W1p], axis=0))
        pT = inp[pos][0].T
        d['pos' + nm] = np.ascontiguousarray(np.concatenate([pT, pT], axis=0))
    w2k = inp['cmp_k_w2'][0]
    d['w2k'] = pack_rows(np.concatenate([w2k, w2k], axis=1), 2)
    d['w2v'] = pack_rows(inp['cmp_v_w2'][0], 2)
    q0 = 512 + 768 + 24
    k0 = q0 + 512
    v0 = k0 + 512
    tb = []
    for base in (q0, k0):
        for mp in range(2):
            for pr in range(2):
                tb.append(base + mp * 256 + pr * 128 + np.arange(128))
    d['wB_fm'] = np.concatenate([pack_fm(w_in, t) for t in tb], axis=1)
    d['wB_tm'] = pack_fm(w_in, v0 + np.arange(512))
    d['lam'] = np.stack([inp['lam_q1'][0], inp['lam_k1'][0], inp['lam_q2'][0], inp['lam_k2'][0]]).reshape(1, 256)
    d['normg'] = inp['diff_norm_g'][0].reshape(1, 128)
    ga0 = v0 + 512
    d['wC_g'] = np.concatenate([pack_fm(w_in, ga0 + 128 * i + np.arange(128)) for i in range(16)], axis=1)
    d['wbn'] = pack_rows(inp['w_branch_nsa'][0], 4)
    d['wbd'] = pack_rows(inp['w_branch_diff'][0], 4)
    d['wout'] = pack_rows(inp['w_out'][0], 8)
    d['ln'] = np.stack([inp['ln1_g'][0], inp['ln1_b'][0], inp['ln2_g'][0], inp['ln2_b'][0]]).reshape(1, 4096)
    d['wq'] = pack_rows(inp['peer_wq'][0], 8)
    d['sk'] = np.ascontiguousarray(np.concatenate([inp['peer_subkeys1'][0].T, inp['peer_subkeys2'][0].T], axis=1))
    d['puv'] = np.concatenate([inp['peer_u'][0], inp['peer_v'][0]], axis=1)
    d['pT'] = pack_rows(np.ascontiguousarray(inp['p'][0, b].T), 2)
    d['wpg'] = pack_rows(inp['ple_w_gate'][0], 8)
    d['wpp'] = pack_rows(inp['ple_w_proj'][0], 2)
    d.update(host_consts())
    return d


IN_SHAPES = {
    'xT': [128, 8 * 2048], 'xtm': [2048, 1024],
    'wA_fm': [128, 10 * 1024], 'wA_tm': [128, 8 * 280],
    'w1k': [128, 32 * 256], 'w1v': [128, 32 * 256], 'posk': [128, 32], 'posv': [128, 32],
    'w2k': [128, 256], 'w2v': [128, 128],
    'wB_fm': [128, 8 * 1024], 'wB_tm': [128, 8 * 512], 'lam': [1, 256], 'normg': [1, 128],
    'wC_g': [128, 16 * 1024], 'wbn': [128, 4096], 'wbd': [128, 4096], 'wout': [128, 8192],
    'ln': [1, 4096], 'wq': [128, 8 * 2048], 'sk': [128, 256],
    'puv': [16384, 2048], 'pT': [128, 2 * 2048],
    'wpg': [128, 8192], 'wpp': [128, 2048],
    'ident': [128, 128], 'tstrip': [128, 2048], 'wstrip': [128, 1024], 'cmpbias': [128, 2048],
    'eblk': [32, 2048], 'selA': [128, 512], 'selB': [128, 512], 'selw': [128, 32], 'qscale': [128, 8], 'iota16': [128, 16], 'btab': [128, 288], 'masks': [128, 256], 'e01': [32, 2048],
}


class Ctx:
    pass


def dump(fw, C, t, col, n, keys, dcol, p0=0, np_=128):
    if C.dbg is None:
        return
    fw.dma('sp', lambda q: q.dma_start(out=dram2(C.dbg, 32768, p0, np_, dcol, [(1, n)]), in_=t.c(col, n, p0, np_)),
           'st_dbg', reads=keys)


_UID = [0]


def alloc(nc, stack, name, ncols, dtype=F32, psum=False, nparts=128):
    _UID[0] += 1
    cm = (nc.psum_tensor if psum else nc.sbuf_tensor)('s%d_%s' % (_UID[0], name), [nparts, ncols], dtype)
    t = cm.__enter__()
    stack.append(cm)
    return T(t, ncols, name)


def free_to(stack, n, fw=None):
    while len(stack) > n:
        stack.pop().__exit__(None, None, None)
    if fw is not None:
        fw.barrier()


def load(fw, dst, dcol, src, scol, ncols, key, sem, nrows=128, p0=0):
    rowlen = src.shape[1]
    fw.dma('sp', lambda q: q.dma_start(out=dst.c(dcol, ncols, p0, nrows),
                                       in_=dram2(src, rowlen, 0, nrows, scol, [(1, ncols)])),
           sem, writes=[key])


def proj_phase(fw, C, wfm, nfm, wtm, ntm, fm_evac, tm_evac):
    nc = fw.nc
    n0 = len(C.stack)
    Wf = alloc(nc, C.stack, 'Wf', nfm * 1024)
    Wt = alloc(nc, C.stack, 'Wt', 8 * ntm)
    xb = [alloc(nc, C.stack, 'xb%d' % i, 8 * 512) for i in range(2)]
    for i in range(nfm):
        load(fw, Wf, i * 1024, wfm, i * 1024, 1024, 'Wf', 'ld_w')
    load(fw, Wt, 0, wtm, 0, 8 * ntm, 'Wt', 'ld_w')
    for c in range(4):
        X = xb[c % 2]
        fw.dma('sp', lambda q: q.dma_start(out=X.ap(0, [(512, 8), (1, 512)]),
                                           in_=dram2(C.d['xT'], 8 * 2048, 0, 128, c * 512, [(2048, 8), (1, 512)])),
               'ld_x%d' % (c % 2), writes=[X.name])
        for i in range(nfm):
            ps = C.ps[6 + (i % 2)]
            for kt in range(8):
                fw.op('pe', lambda e: e.matmul(ps.c(0, 512), Wf.c(i * 1024 + kt * 128, 128), X.c(kt * 512, 512),
                                               start=(kt == 0), stop=(kt == 7)),
                      reads=['Wf', X.name], writes=[ps.name], signal=(kt == 7))
            fm_evac(i, c, ps)
        for tl in range(4):
            ps = C.ps[6 + (tl % 2)]
            for kt in range(8):
                fw.op('pe', lambda e: e.matmul(ps.c(0, ntm), X.c(kt * 512 + tl * 128, 128), Wt.c(kt * ntm, ntm),
                                               start=(kt == 0), stop=(kt == 7)),
                      reads=['Wt', X.name], writes=[ps.name], signal=(kt == 7))
            tm_evac(c * 4 + tl, ps)
    free_to(C.stack, n0, fw)


class AttnItem:
    pass


def attn_stream(fw, C, items):
    prev = None
    n = 0
    for it in items:
        sb = C.ps[n % 3]
        pb = C.pbuf[n % 3]
        mms = [(it.kT, it.qT)] + it.extra
        for mi, (l, r) in enumerate(mms):
            fw.op('pe', lambda e: e.matmul(sb.c(0, it.N), l, r, start=(mi == 0), stop=(mi == len(mms) - 1)),
                  reads=it.reads, writes=[sb.name], signal=(mi == len(mms) - 1))
        fw.op('act', lambda e: e.activation(out=pb.c(0, it.N), in_=sb.c(0, it.N), func=AF.Exp, scale=it.slope),
              reads=[sb.name], writes=[pb.name])
        it.pb = pb
        if prev is not None:
            emit_pv(fw, C, prev)
        prev = it
        n += 1
    if prev is not None:
        emit_pv(fw, C, prev)


def exp_segments(btab, si, qs, qe, j):
    w = 128 if si == 0 else (256 if si == 1 else 512)
    segs = []
    s0 = (qs // w) * w
    while s0 < qe:
        a = max(s0, qs)
        b = min(s0 + w, qe)
        m = (s0 + w // 2 - 128 * j) // 64
        segs.append((a - qs, b - a, btab.c(si * 36 + m + 2, 1)))
        s0 += w
    return segs


def attn_stream2(fw, C, items):
    LAH = 2
    n = len(items)
    ident = C.ident
    for idx in range(n + LAH):
        if idx < n:
            it = items[idx]
            sb = C.ps[idx % 3]
            pb = C.pbuf[idx % 4]
            nm = len(it.masks)
            fw.op('pe', lambda e: e.matmul(sb.c(0, it.N), it.kT, it.qT, start=True, stop=(nm == 0)),
                  reads=it.reads, writes=[sb.name], signal=(nm == 0))
            for mi, (off, map_) in enumerate(it.masks):
                fw.op('pe', lambda e: e.matmul(sb.c(off, 128), ident.c(0, 128), map_, start=False, stop=(mi == nm - 1)),
                      reads=['ident', 'masks'], writes=[sb.name], signal=(mi == nm - 1))
            for (off, wd, bias_ap) in it.exps:
                fw.op('act', lambda e: e.activation(out=pb.c(off, wd), in_=sb.c(off, wd), func=AF.Exp, scale=it.slope, bias=bias_ap),
                      reads=[sb.name, 'btab'], writes=[pb.name])
            if it.mmask is not None:
                fw.op('dve', lambda e: e.tensor_tensor(out=pb.c(0, it.N), in0=pb.c(0, it.N), in1=it.mmask, op=ALU.mult),
                      reads=[pb.name, it.mkey], writes=[pb.name])
            it.pb = pb
        if idx >= LAH:
            it = items[idx - LAH]
            for pi, (out_ap, lhsT, start) in enumerate(it.pvs):
                fw.op('pe', lambda e: e.matmul(out_ap, lhsT, it.pb.c(0, it.N), start=start, stop=False, skip_group_check=True),
                      reads=[it.pb.name] + it.vreads, writes=it.obanks, signal=(pi == len(it.pvs) - 1))
            if it.after is not None:
                it.after()


def emit_pv(fw, C, it):
    for pi, (ps_ap, pcol, rhs, start) in enumerate(it.pv):
        fw.op('pe', lambda e: e.matmul(ps_ap, it.pb.c(pcol, 128), rhs, start=start, stop=False, skip_group_check=True),
              reads=[it.pb.name] + it.vreads, writes=it.obanks, signal=(pi == len(it.pv) - 1))
    if getattr(it, 'after', None) is not None:
        it.after()


def vcopy(fw, eng, out, in_, reads, writes):
    if eng == 'act':
        fw.op('act', lambda e: e.activation(out=out, in_=in_, func=AF.Copy), reads=reads, writes=writes)
    else:
        fw.op(eng, lambda e: e.tensor_copy(out, in_), reads=reads, writes=writes)


def transpose_out(fw, C, och, oT, dram_t, c, tag):
    for f in range(4):
        ps = C.ps[6 + (f % 2)]
        for ql in range(4):
            fw.op('pe', lambda e: e.transpose(ps.c(ql * 128, 128), och.c(ql * 512 + f * 128, 128), C.ident.c(0, 128)),
                  reads=[och.name, 'ident'], writes=[ps.name], signal=(ql == 3))
        vcopy(fw, 'act' if f % 2 else 'dve', oT.c(f * 512, 512), ps.c(0, 512), [ps.name], [oT.name + str(f)])
    fw.dma('sp', lambda q: q.dma_start(out=dram2(dram_t, 8192, 0, 128, c * 512, [(2048, 4), (1, 512)]),
                                       in_=oT.ap(0, [(512, 4), (1, 512)])),
           'st_' + tag, reads=[oT.name + str(f) for f in range(4)], writes=[tag + '_dram%d' % c])


def phase_A(fw, C):
    nc = fw.nc
    d = C.d
    n0 = len(C.stack)
    Qp = [alloc(nc, C.stack, 'Qp%d' % i, 2048) for i in range(4)]
    KsT = [alloc(nc, C.stack, 'KsT%d' % g, 2048) for g in range(2)]
    KwT = [alloc(nc, C.stack, 'KwT%d' % g, 2048) for g in range(2)]
    VsA = alloc(nc, C.stack, 'VsA', 16 * 130 + 64)
    VwA = alloc(nc, C.stack, 'VwA', 16 * 130 + 64)
    gN = alloc(nc, C.stack, 'gN', 16 * 24)
    kcT = [alloc(nc, C.stack, 'kcT%d' % g, 128) for g in range(2)]
    vcA = [alloc(nc, C.stack, 'vcA%d' % g, 97) for g in range(2)]
    n1 = len(C.stack)
    KcT = alloc(nc, C.stack, 'KcT', 2048)
    VcT = alloc(nc, C.stack, 'VcT', 2048)
    fw.op('pool', lambda e: e.memset(VsA.c(0, 16 * 130 + 64), 1.0), writes=['VsA'])
    fw.op('pool', lambda e: e.memset(VwA.c(0, 16 * 130 + 64), 1.0), writes=['VwA'])
    fmdst = [KcT, VcT, KsT[0], KsT[1], KwT[0], KwT[1]]

    def fm_evac(i, c, ps):
        if i < 4:
            fw.op('dve', lambda e: e.tensor_scalar(out=Qp[i].c(c * 512, 512), in0=ps.c(0, 512), scalar1=C.qscale.c(i, 1),
                                                   scalar2=None, op0=ALU.mult),
                  reads=[ps.name, 'qscale'], writes=[Qp[i].name])
        else:
            dst = fmdst[i - 4]
            vcopy(fw, 'act', dst.c(c * 512, 512), ps.c(0, 512), [ps.name], [dst.name])

    def tm_evac(tt, ps):
        fw.op('dve', lambda e: e.tensor_copy(VsA.ap(tt * 130, [(65, 2), (1, 64)]), ps.ap(0, [(64, 2), (1, 64)])),
              reads=[ps.name], writes=['VsA'])
        fw.op('dve', lambda e: e.tensor_copy(VwA.ap(tt * 130, [(65, 2), (1, 64)]), ps.ap(128, [(64, 2), (1, 64)])),
              reads=[ps.name], writes=['VwA'])
        fw.op('act', lambda e: e.activation(out=gN.c(tt * 24, 24), in_=ps.c(256, 24), func=AF.Sigmoid),
              reads=[ps.name], writes=['gN'])

    proj_phase(fw, C, d['wA_fm'], 10, d['wA_tm'], 280, fm_evac, tm_evac)

    dump(fw, C, Qp[0], 0, 2048, [Qp[0].name], 0)
    dump(fw, C, KsT[1], 0, 2048, [KsT[1].name], 2048)
    dump(fw, C, VsA, 0, 2080, ['VsA'], 4096)
    dump(fw, C, gN, 0, 384, ['gN'], 6176)
    dump(fw, C, KcT, 0, 2048, ['KcT'], 6560)
    n2 = len(C.stack)
    W1 = alloc(nc, C.stack, 'W1', 32 * 256)
    pos = alloc(nc, C.stack, 'pos', 32)
    w2k = alloc(nc, C.stack, 'w2k', 256)
    w2v = alloc(nc, C.stack, 'w2v', 128)
    hidT = [alloc(nc, C.stack, 'hidT%d' % h, 128) for h in range(2)]
    cb = alloc(nc, C.stack, 'cb', 2)
    load(fw, w2k, 0, d['w2k'], 0, 256, 'w2k', 'ld_w')
    load(fw, w2v, 0, d['w2v'], 0, 128, 'w2v', 'ld_w')
    for g in range(2):
        fw.op('pool', lambda e: e.memset(kcT[g].c(0, 128), 0.0), writes=[kcT[g].name])
        fw.op('pool', lambda e: e.memset(vcA[g].c(0, 97), 0.0), writes=[vcA[g].name])
        fw.op('pool', lambda e: e.memset(vcA[g].c(64, 1), 1.0), writes=[vcA[g].name])
        load(fw, vcA[g], 65, d['selw'], 0, 32, vcA[g].name, 'ld_w')
    for kv in range(2):
        src = KcT if kv == 0 else VcT
        for q4 in range(4):
            load(fw, W1, q4 * 2048, d['w1k' if kv == 0 else 'w1v'], q4 * 2048, 2048, 'W1', 'ld_w')
        load(fw, pos, 0, d['posk' if kv == 0 else 'posv'], 0, 32, 'pos', 'ld_w')
        for g in range(2):
            for half in range(2):
                ps = C.ps[3 + half]
                for l in range(32):
                    fw.op('pe', lambda e: e.matmul(ps.c(0, 127), W1.c(l * 256 + half * 128, 128, g * 64, 64),
                                                   src.ap(l, [(16, 127)], g * 64, 64), start=(l == 0), stop=(l == 31)),
                          reads=['W1', src.name], writes=[ps.name], signal=(l == 31))
                ps5 = C.ps[5]
                for l in range(32):
                    fw.op('pe', lambda e: e.matmul(ps5.c(half, 1), W1.c(l * 256 + half * 128, 128, g * 64, 64),
                                                   pos.c(l, 1, g * 64, 64), start=(l == 0), stop=(l == 31)),
                          reads=['W1', 'pos'], writes=[ps5.name], signal=(l == 31))
                fw.op('dve', lambda e: e.tensor_copy(cb.c(half, 1), ps5.c(half, 1)), reads=[ps5.name], writes=['cb%d' % half])
                fw.op('act', lambda e: e.activation(out=hidT[half].c(0, 127), in_=ps.c(0, 127), func=AF.Gelu,
                                                    bias=cb.c(half, 1)),
                      reads=[ps.name, 'cb%d' % half], writes=[hidT[half].name])
            if kv == 0:
                ps = C.ps[6]
                for half in range(2):
                    fw.op('pe', lambda e: e.matmul(ps.c(0, 127), w2k.c(half * 128, 128), hidT[half].c(0, 127),
                                                   start=(half == 0), stop=(half == 1)),
                          reads=['w2k', hidT[half].name], writes=[ps.name], signal=(half == 1))
                vcopy(fw, 'dve', kcT[g].c(0, 127), ps.c(0, 127), [ps.name], [kcT[g].name])
            else:
                ps = C.ps[7]
                for half in range(2):
                    fw.op('pe', lambda e: e.matmul(ps.c(0, 64, 0, 127), hidT[half].c(0, 127), w2v.c(half * 64, 64),
                                                   start=(half == 0), stop=(half == 1)),
                          reads=['w2v', hidT[half].name], writes=[ps.name], signal=(half == 1))
                vcopy(fw, 'dve', vcA[g].c(0, 64, 0, 127), ps.c(0, 64, 0, 127), [ps.name], [vcA[g].name])
    dump(fw, C, kcT[1], 0, 128, [kcT[1].name], 8608)
    dump(fw, C, vcA[1], 0, 97, [vcA[1].name], 8736)
    free_to(C.stack, n1, fw)

    cmpb = alloc(nc, C.stack, 'cmpb', 2048)
    e01 = alloc(nc, C.stack, 'e01', 2048)
    btab = alloc(nc, C.stack, 'btab', 288)
    masks = alloc(nc, C.stack, 'masks', 256)
    selA = alloc(nc, C.stack, 'selA', 512)
    selB = alloc(nc, C.stack, 'selB', 512)
    C.pbuf = [alloc(nc, C.stack, 'pb%d' % i, 512) for i in range(4)]
    och = alloc(nc, C.stack, 'och', 2048)
    oT = alloc(nc, C.stack, 'oT', 2048)
    Mst = alloc(nc, C.stack, 'Mst', 8192)
    Qz = [alloc(nc, C.stack, 'Qz%d' % i, 512) for i in range(8)]
    for i in range(8):
        fw.op('pool', lambda e: e.memset(Qz[i].c(0, 512), 0.0), writes=[Qz[i].name])
    ots = alloc(nc, C.stack, 'ots', 512)
    imp = alloc(nc, C.stack, 'imp', 128)
    score = alloc(nc, C.stack, 'score', 128)
    nm = alloc(nc, C.stack, 'nm', 128)
    negT = [alloc(nc, C.stack, 'negT%d' % g, 512) for g in range(2)]
    sm = alloc(nc, C.stack, 'sm', 32)
    load(fw, cmpb, 0, d['cmpbias'], 0, 2048, 'cmpb', 'ld_w')
    load(fw, e01, 0, d['e01'], 0, 2048, 'e01', 'ld_w', nrows=32)
    load(fw, btab, 0, d['btab'], 0, 288, 'btab', 'ld_w')
    load(fw, masks, 0, d['masks'], 0, 256, 'masks', 'ld_w')
    load(fw, selA, 0, d['selA'], 0, 512, 'selA', 'ld_w')
    load(fw, selB, 0, d['selB'], 0, 512, 'selB', 'ld_w')
    ident = C.ident

    def normalize(O, W, hh, br, c, first_in_group, clamp=1e-30):
        def f():
            fw.op('dve', lambda e: e.tensor_scalar(out=sm.c(0, 4), in0=O.ap(64, [(W, 4)]), scalar1=clamp, scalar2=None,
                                                   op0=ALU.max), reads=[O.name], writes=['sm_rs'])
            fw.op('dve', lambda e: e.reciprocal(sm.c(4, 4), sm.c(0, 4)), reads=['sm_rs'], writes=['sm_rcp'])
            fw.op('dve', lambda e: e.tensor_tensor(out=sm.c(8, 4), in0=sm.c(4, 4),
                                                   in1=gN.ap(4 * c * 24 + hh * 3 + br, [(24, 4)]), op=ALU.mult),
                  reads=['sm_rcp', 'gN'], writes=['sm_rg'])
            for ql in range(4):
                dst = och.c(ql * 512 + hh * 64, 64)
                if br == 0:
                    fw.op('dve', lambda e: e.tensor_scalar(out=dst, in0=O.c(ql * W, 64), scalar1=sm.c(8 + ql, 1), scalar2=None,
                                                           op0=ALU.mult), reads=[O.name, 'sm_rg'], writes=['och'])
                else:
                    fw.op('dve', lambda e: e.scalar_tensor_tensor(out=dst, in0=O.c(ql * W, 64), scalar=sm.c(8 + ql, 1), in1=dst,
                                                                  op0=ALU.mult, op1=ALU.add),
                          reads=[O.name, 'sm_rg', 'och'], writes=['och'])
                if br == 0:
                    idst = imp.c(ql * 32, 32)
                    if first_in_group:
                        fw.op('dve', lambda e: e.tensor_scalar(out=idst, in0=O.c(ql * W + 65, 32), scalar1=sm.c(4 + ql, 1),
                                                               scalar2=None, op0=ALU.mult),
                              reads=[O.name, 'sm_rcp'], writes=['imp'])
                    else:
                        fw.op('dve', lambda e: e.scalar_tensor_tensor(out=idst, in0=O.c(ql * W + 65, 32), scalar=sm.c(4 + ql, 1),
                                                                      in1=idst, op0=ALU.mult, op1=ALU.add),
                              reads=[O.name, 'sm_rcp', 'imp'], writes=['imp'])
        return f

    On = C.ps[5]

    def after_fm(OT, hh, br, c):
        def f():
            vcopy(fw, 'act', ots.c(0, 512, 0, 65), OT.c(0, 512, 0, 65), [OT.name], ['ots'])
            for ql in range(4):
                fw.op('pe', lambda e: e.transpose(On.c(ql * 65, 65), ots.c(ql * 128, 128, 0, 65), ident.c(0, 65, 0, 65)),
                      reads=['ots', 'ident'], writes=[On.name], signal=(ql == 3))
            normalize(On, 65, hh, br, c, False, clamp=1e-37)()
        return f

    def build_items(c, br, heads):
        q0 = c * 512
        items = []
        for hh in heads:
            g = hh // 4
            i, ph = hh // 2, hh % 2
            OT = C.ps[3 + (hh % 2)]
            VA = VsA if br == 1 else VwA
            KT = KsT[g] if br == 1 else KwT[g]
            jlo = 0 if br == 1 else max(0, 4 * c - 4)
            jhi = 4 * c + 3
            first = True
            for j in range(jlo, jhi + 1):
                tlo = max(j, 4 * c)
                thi = 4 * c + 3 if br == 1 else min(j + 4, 4 * c + 3)
                qs_ = tlo * 128
                N = (thi - tlo + 1) * 128
                it = AttnItem()
                it.kT = KT.c(j * 128, 128)
                it.qT = Qz[hh].c(qs_ - q0, N)
                it.reads = [KT.name, Qz[hh].name]
                it.N = N
                it.slope = SLOPE_N[hh]
                it.masks = []
                for tq in range(tlo, thi + 1):
                    Dd = tq - j
                    off = (tq - tlo) * 128
                    if Dd == 0:
                        it.masks.append((off, masks.c(0, 128)))
                    if br == 2 and Dd == 4:
                        it.masks.append((off, masks.c(128, 128)))
                it.exps = exp_segments(btab, hh, qs_, qs_ + N, j)
                if br == 1:
                    it.mmask = Mst.c(j * 512 + qs_ - q0, N)
                    it.mkey = 'Mst%d' % j
                else:
                    it.mmask = None
                it.pvs = [(OT.c(qs_ - q0, N), VA.c(j * 130 + g * 65, 128), first)]
                first = False
                it.vreads = [VA.name]
                it.obanks = [OT.name]
                it.after = after_fm(OT, hh, br, c) if j == jhi else None
                items.append(it)
        return items

    for c in range(4):
        q0 = c * 512
        for hh in range(8):
            fw.op('pool', lambda e: e.tensor_copy(Qz[hh].c(0, 512, (hh % 2) * 64, 64), Qp[hh // 2].c(q0, 512, (hh % 2) * 64, 64)),
                  reads=[Qp[hh // 2].name], writes=[Qz[hh].name])
        for g in range(2):
            items = []
            for hl in range(4):
                hh = 4 * g + hl
                i, ph = hh // 2, hh % 2
                O = C.ps[3 + hl]
                it = AttnItem()
                it.kT = kcT[g].c(0, 128, ph * 64, 64)
                it.qT = Qp[i].c(q0, 512, ph * 64, 64)
                it.extra = [(ident.c(0, 128), cmpb.c(q0, 512))]
                it.N = 512
                it.slope = SLOPE_N[hh]
                it.pv = [(O.c(ql * 97, 97), ql * 128, vcA[g].c(0, 97), ql == 0) for ql in range(4)]
                it.reads = [kcT[g].name, Qp[i].name, 'ident', 'cmpb']
                it.vreads = [vcA[g].name]
                it.obanks = [O.name]
                it.after = normalize(O, 97, hh, 0, c, hl == 0)
                items.append(it)
            attn_stream(fw, C, items)
            fw.op('dve', lambda e: e.tensor_tensor(out=score.c(0, 128), in0=imp.c(0, 128), in1=selA.c(c * 128, 128), op=ALU.mult),
                  reads=['imp', 'selA'], writes=['score'])
            fw.op('dve', lambda e: e.tensor_tensor(out=score.c(0, 128), in0=score.c(0, 128), in1=selB.c(c * 128, 128), op=ALU.add),
                  reads=['score', 'selB'], writes=['score'])
            ps5 = C.ps[7]
            for ql in range(4):
                fw.op('dve', lambda e: e.max(out=sm.c(16, 8), in_=score.c(ql * 32, 32)), reads=['score'], writes=['sm_top'])
                fw.op('dve', lambda e: e.tensor_scalar(out=nm.c(ql * 32, 32), in0=score.c(ql * 32, 32), scalar1=sm.c(23, 1),
                                                       scalar2=None, op0=ALU.is_ge),
                      reads=['score', 'sm_top'], writes=['nm'])
                fw.op('pe', lambda e: e.transpose(ps5.c(ql * 128, 128, 0, 32), nm.c(ql * 32, 32), ident.c(0, 128)),
                      reads=['nm', 'ident'], writes=[ps5.name])
            vcopy(fw, 'dve', negT[g].c(0, 512, 0, 32), ps5.c(0, 512, 0, 32), [ps5.name], [negT[g].name])
            for j in range(4 * c + 4):
                mp_ = C.ps[6 + (j % 2)]
                fw.op('pe', lambda e: e.matmul(mp_.c(0, 512), e01.c(j * 128, 128, 0, 32), negT[g].c(0, 512, 0, 32), start=True, stop=True),
                      reads=['e01', negT[g].name], writes=[mp_.name])
                vcopy(fw, 'dve', Mst.c(j * 512, 512), mp_.c(0, 512), [mp_.name], ['Mst%d' % j])
            items = build_items(c, 1, range(4 * g, 4 * g + 4))
            attn_stream2(fw, C, items)
        attn_stream2(fw, C, build_items(c, 2, range(8)))
        transpose_out(fw, C, och, oT, C.onT, c, 'onT')
    free_to(C.stack, n0, fw)


def phase_B(fw, C):
    nc = fw.nc
    d = C.d
    n0 = len(C.stack)
    Qd = [alloc(nc, C.stack, 'Qd%d' % i, 2048) for i in range(4)]
    Kd = [alloc(nc, C.stack, 'Kd%d' % i, 2048) for i in range(4)]
    Vd = alloc(nc, C.stack, 'Vd', 16 * 512)

    def fm_evac(i, c, ps):
        if i < 4:
            fw.op('dve', lambda e: e.tensor_scalar(out=Qd[i].c(c * 512, 512), in0=ps.c(0, 512), scalar1=C.qscale.c(4 + i, 1),
                                                   scalar2=None, op0=ALU.mult),
                  reads=[ps.name, 'qscale'], writes=[Qd[i].name])
        else:
            vcopy(fw, 'act', Kd[i - 4].c(c * 512, 512), ps.c(0, 512), [ps.name], [Kd[i - 4].name])

    def tm_evac(tt, ps):
        vcopy(fw, 'dve', Vd.c(tt * 512, 512), ps.c(0, 512), [ps.name], ['Vd'])

    proj_phase(fw, C, d['wB_fm'], 8, d['wB_tm'], 512, fm_evac, tm_evac)

    btab = alloc(nc, C.stack, 'btab', 288)
    masks = alloc(nc, C.stack, 'masks', 256)
    C.pbuf = [alloc(nc, C.stack, 'pb%d' % i, 512) for i in range(4)]
    och = alloc(nc, C.stack, 'och', 2048)
    oT = alloc(nc, C.stack, 'oT', 2048)
    o1 = alloc(nc, C.stack, 'o1', 512)
    otv = alloc(nc, C.stack, 'otv', 512)
    otsr = alloc(nc, C.stack, 'otsr', 512)
    onesc = alloc(nc, C.stack, 'onesc', 128)
    Qz = [alloc(nc, C.stack, 'Qz%d' % i, 512) for i in range(8)]
    for i in range(8):
        fw.op('pool', lambda e: e.memset(Qz[i].c(0, 512), 0.0), writes=[Qz[i].name])
    junk = alloc(nc, C.stack, 'junk', 128)
    gbc = alloc(nc, C.stack, 'gbc', 128)
    lt = alloc(nc, C.stack, 'lt', 640)
    neglam = alloc(nc, C.stack, 'neglam', 1)
    sm = alloc(nc, C.stack, 'smd', 32)
    load(fw, btab, 0, d['btab'], 0, 288, 'btab', 'ld_w')
    load(fw, masks, 0, d['masks'], 0, 256, 'masks', 'ld_w')
    fw.op('pool', lambda e: e.memset(onesc.c(0, 128), 1.0), writes=['onesc'])
    ident = C.ident
    load(fw, lt, 0, d['lam'], 0, 256, 'lt', 'ld_w', nrows=1)
    fw.op('pool', lambda e: e.memset(lt.c(400, 128, 0, 1), 1.0), writes=['lt_ones'])
    fw.op('dve', lambda e: e.tensor_tensor(out=lt.ap(256, [(64, 2), (1, 64)], 0, 1), in0=lt.ap(0, [(128, 2), (1, 64)], 0, 1),
                                           in1=lt.ap(64, [(128, 2), (1, 64)], 0, 1), op=ALU.mult), reads=['lt'], writes=['lt_p'])
    fw.op('dve', lambda e: e.tensor_reduce(out=lt.c(384, 2, 0, 1), in_=lt.ap(256, [(64, 2), (1, 64)], 0, 1), axis=AX.X, op=ALU.add),
          reads=['lt_p'], writes=['lt_s'])
    fw.op('act', lambda e: e.activation(out=lt.c(386, 2, 0, 1), in_=lt.c(384, 2, 0, 1), func=AF.Exp), reads=['lt_s'], writes=['lt_e'])
    fw.op('dve', lambda e: e.tensor_tensor(out=lt.c(388, 1, 0, 1), in0=lt.c(386, 1, 0, 1), in1=lt.c(387, 1, 0, 1), op=ALU.subtract),
          reads=['lt_e'], writes=['lt_d'])
    fw.op('dve', lambda e: e.tensor_scalar(out=lt.c(389, 1, 0, 1), in0=lt.c(388, 1, 0, 1), scalar1=LAM_INIT, scalar2=-1.0,
                                           op0=ALU.add, op1=ALU.mult), reads=['lt_d'], writes=['lt_n'])
    ps7 = C.ps[7]
    fw.op('pe', lambda e: e.matmul(ps7.c(0, 1), lt.c(400, 128, 0, 1), lt.c(389, 1, 0, 1), start=True, stop=True),
          reads=['lt_n', 'lt_ones'], writes=[ps7.name])
    vcopy(fw, 'dve', neglam.c(0, 1), ps7.c(0, 1), [ps7.name], ['neglam'])
    fw.dma('sp', lambda q: q.dma_start(out=gbc.c(0, 128), in_=bass.AP(d['normg'], 0, [[0, 128], [1, 128]])), 'ld_w', writes=['gbc0'])
    fw.op('dve', lambda e: e.tensor_scalar(out=gbc.c(0, 128), in0=gbc.c(0, 128), scalar1=1.0 - LAM_INIT, scalar2=None, op0=ALU.mult),
          reads=['gbc0'], writes=['gbc'])

    OTv = [C.ps[3], C.ps[4]]
    OTs = [C.ps[5], C.ps[6]]
    On = C.ps[7]

    def after0(h, mp):
        def f():
            base = 0 if mp == 0 else 8
            vcopy(fw, 'act', otv.c(0, 512), OTv[mp].c(0, 512), [OTv[mp].name], ['otv'])
            vcopy(fw, 'dve', otsr.c(0, 512, 0, 1), OTs[mp].c(0, 512, 0, 1), [OTs[mp].name], ['otsr'])
            for ql in range(4):
                fw.op('pe', lambda e: e.transpose(On.c(ql * 128, 128), otv.c(ql * 128, 128), ident.c(0, 128)),
                      reads=['otv', 'ident'], writes=[On.name], signal=(ql == 3))
            for ql in range(4):
                fw.op('pe', lambda e: e.transpose(OTs[mp].c(ql, 1), otsr.c(ql * 128, 128, 0, 1), ident.c(0, 1, 0, 1)),
                      reads=['otsr', 'ident'], writes=[OTs[mp].name], signal=(ql == 3))
            fw.op('dve', lambda e: e.tensor_scalar(out=sm.c(base, 4), in0=OTs[mp].c(0, 4), scalar1=1e-37,
                                                   scalar2=None, op0=ALU.max), reads=[OTs[mp].name], writes=['smd_rs%d' % mp])
            fw.op('dve', lambda e: e.reciprocal(sm.c(base + 4, 4), sm.c(base, 4)), reads=['smd_rs%d' % mp], writes=['smd_rcp%d' % mp])
            if mp == 0:
                for ql in range(4):
                    fw.op('dve', lambda e: e.tensor_scalar(out=o1.c(ql * 128, 128), in0=On.c(ql * 128, 128),
                                                           scalar1=sm.c(4 + ql, 1), scalar2=None, op0=ALU.mult),
                          reads=[On.name, 'smd_rcp0'], writes=['o1'])
                return
            fw.op('dve', lambda e: e.tensor_scalar(out=sm.c(16, 4), in0=sm.c(12, 4), scalar1=neglam.c(0, 1), scalar2=None, op0=ALU.mult),
                  reads=['smd_rcp1', 'neglam'], writes=['smd_nlr'])
            fw.op('dve', lambda e: e.memset(sm.c(20, 4), 0.0), writes=['smd_ss'])
            for ql in range(4):
                fw.op('dve', lambda e: e.scalar_tensor_tensor(out=o1.c(ql * 128, 128), in0=On.c(ql * 128, 128),
                                                              scalar=sm.c(16 + ql, 1), in1=o1.c(ql * 128, 128), op0=ALU.mult, op1=ALU.add),
                      reads=[On.name, 'smd_nlr', 'o1'], writes=['o1'])
                fw.op('dve', lambda e: e.scalar_tensor_tensor(out=junk.c(0, 128), in0=o1.c(ql * 128, 128), scalar=1.0,
                                                              in1=o1.c(ql * 128, 128), op0=ALU.mult, op1=ALU.mult,
                                                              accum_out=sm.c(20 + ql, 1)),
                      reads=['o1', 'smd_ss'], writes=['junk', 'smd_ss'])
            fw.op('dve', lambda e: e.tensor_scalar(out=sm.c(24, 4), in0=sm.c(20, 4), scalar1=1.0 / 128, scalar2=1e-5,
                                                   op0=ALU.mult, op1=ALU.add), reads=['smd_ss'], writes=['smd_var'])
            fw.op('act', lambda e: e.activation(out=sm.c(24, 4), in_=sm.c(24, 4), func=AF.Sqrt), reads=['smd_var'], writes=['smd_var'])
            fw.op('dve', lambda e: e.reciprocal(sm.c(28, 4), sm.c(24, 4)), reads=['smd_var'], writes=['smd_rstd'])
            for ql in range(4):
                fw.op('dve', lambda e: e.scalar_tensor_tensor(out=och.c(ql * 512 + h * 128, 128), in0=o1.c(ql * 128, 128),
                                                              scalar=sm.c(28 + ql, 1), in1=gbc.c(0, 128), op0=ALU.mult, op1=ALU.mult),
                      reads=['o1', 'smd_rstd', 'gbc'], writes=['och'])
        return f

    for c in range(4):
        q0 = c * 512
        for h in range(4):
            for mp in range(2):
                fw.op('pool', lambda e: e.tensor_copy(Qz[h * 2 + mp].c(0, 512, (h % 2) * 64, 64),
                                                      Qd[mp * 2 + h // 2].c(q0, 512, (h % 2) * 64, 64)),
                      reads=[Qd[mp * 2 + h // 2].name], writes=[Qz[h * 2 + mp].name])
        items = []
        for h in range(4):
            ph = h % 2
            si = 2 * h + 1
            for mp in range(2):
                ti = mp * 2 + h // 2
                first = True
                jhi = 4 * c + 3
                for j in range(0, jhi + 1):
                    tlo = max(j, 4 * c)
                    thi = 4 * c + 3
                    qs_ = tlo * 128
                    N = (thi - tlo + 1) * 128
                    it = AttnItem()
                    it.kT = Kd[ti].c(j * 128, 128)
                    it.qT = Qz[h * 2 + mp].c(qs_ - q0, N)
                    it.reads = [Kd[ti].name, Qz[h * 2 + mp].name]
                    it.N = N
                    it.slope = SLOPE_D[h]
                    it.masks = []
                    for tq in range(tlo, thi + 1):
                        Dd = tq - j
                        off = (tq - tlo) * 128
                        if Dd == 0:
                            it.masks.append((off, masks.c(0, 128)))
                    it.exps = exp_segments(btab, si, qs_, qs_ + N, j)
                    it.mmask = None
                    it.pvs = [(OTv[mp].c(qs_ - q0, N), Vd.c(j * 512 + h * 128, 128), first),
                              (OTs[mp].c(qs_ - q0, N), onesc.c(0, 128), first)]
                    first = False
                    it.vreads = ['Vd', 'onesc']
                    it.obanks = [OTv[mp].name, OTs[mp].name]
                    it.after = after0(h, mp) if j == jhi else None
                    items.append(it)
        attn_stream2(fw, C, items)
        transpose_out(fw, C, och, oT, C.odT, c, 'odT')
    free_to(C.stack, n0, fw)


def layer_norm_tile(fw, C, y, gb, goff, out, sm, tag, eng='pool'):
    for hf in range(2):
        fw.op('dve', lambda e: e.bn_stats(sm.c(hf * 6, 6), y.c(hf * 512, 512)), reads=[y.name], writes=[tag + 'st%d' % hf])
    fw.op('dve', lambda e: e.bn_aggr(sm.c(12, 2), sm.c(0, 12)), reads=[tag + 'st0', tag + 'st1'], writes=[tag + 'mv'])
    fw.op('dve', lambda e: e.tensor_scalar(out=sm.c(14, 1), in0=sm.c(13, 1), scalar1=1e-5, scalar2=None, op0=ALU.add),
          reads=[tag + 'mv'], writes=[tag + 've'])
    fw.op('act', lambda e: e.activation(out=sm.c(15, 1), in_=sm.c(14, 1), func=AF.Sqrt), reads=[tag + 've'], writes=[tag + 'sd'])
    fw.op('dve', lambda e: e.reciprocal(sm.c(16, 1), sm.c(15, 1)), reads=[tag + 'sd'], writes=[tag + 'rstd'])
    fw.op('dve', lambda e: e.tensor_scalar(out=out.c(0, 1024), in0=y.c(0, 1024), scalar1=sm.c(12, 1), scalar2=sm.c(16, 1),
                                           op0=ALU.subtract, op1=ALU.mult), reads=[y.name, tag + 'mv', tag + 'rstd'], writes=[out.name])
    fw.op(eng, lambda e: e.tensor_tensor(out=out.c(0, 1024), in0=out.c(0, 1024), in1=gb.c(goff, 1024), op=ALU.mult),
          reads=[out.name, 'gb'], writes=[out.name])
    fw.op(eng, lambda e: e.tensor_tensor(out=out.c(0, 1024), in0=out.c(0, 1024), in1=gb.c(goff + 1024, 1024), op=ALU.add),
          reads=[out.name, 'gb'], writes=[out.name])


def phase_C(fw, C):
    nc = fw.nc
    d = C.d
    n0 = len(C.stack)
    wbn = alloc(nc, C.stack, 'wbn', 4096)
    wbd = alloc(nc, C.stack, 'wbd', 4096)
    wout = alloc(nc, C.stack, 'wout', 8192)
    Wg = [alloc(nc, C.stack, 'Wg%d' % i, 2048) for i in range(2)]
    onc = alloc(nc, C.stack, 'onc', 2048)
    odc = alloc(nc, C.stack, 'odc', 2048)
    xc = alloc(nc, C.stack, 'xc', 4096)
    mT = alloc(nc, C.stack, 'mT', 4096)
    sg = [alloc(nc, C.stack, 'sg%d' % i, 512) for i in range(2)]
    tmp = alloc(nc, C.stack, 'tmpc', 512)
    gb = alloc(nc, C.stack, 'gb', 2048)
    xt = alloc(nc, C.stack, 'xt', 1024)
    y = alloc(nc, C.stack, 'y', 1024)
    ho = [alloc(nc, C.stack, 'ho%d' % i, 1024) for i in range(2)]
    sm = alloc(nc, C.stack, 'smc', 32)
    for q4 in range(2):
        load(fw, wbn, q4 * 2048, d['wbn'], q4 * 2048, 2048, 'wbn', 'ld_w')
        load(fw, wbd, q4 * 2048, d['wbd'], q4 * 2048, 2048, 'wbd', 'ld_w')
    for q4 in range(4):
        load(fw, wout, q4 * 2048, d['wout'], q4 * 2048, 2048, 'wout', 'ld_w')
    fw.dma('sp', lambda q: q.dma_start(out=gb.c(0, 2048), in_=bass.AP(d['ln'], 0, [[0, 128], [1, 2048]])), 'ld_w', writes=['gb'])
    for c in range(4):
        fw.dma('sp', lambda q: q.dma_start(out=onc.ap(0, [(512, 4), (1, 512)]),
                                           in_=dram2(C.onT, 8192, 0, 128, c * 512, [(2048, 4), (1, 512)])),
               'ld_c', reads=['onT_dram%d' % c], writes=['onc'])
        fw.dma('sp', lambda q: q.dma_start(out=odc.ap(0, [(512, 4), (1, 512)]),
                                           in_=dram2(C.odT, 8192, 0, 128, c * 512, [(2048, 4), (1, 512)])),
               'ld_c', reads=['odT_dram%d' % c], writes=['odc'])
        fw.dma('sp', lambda q: q.dma_start(out=xc.ap(0, [(512, 8), (1, 512)]),
                                           in_=dram2(d['xT'], 8 * 2048, 0, 128, c * 512, [(2048, 8), (1, 512)])),
               'ld_c', writes=['xc'])
        for m in range(8):
            W = Wg[m % 2]
            load(fw, W, 0, d['wC_g'], m * 1024, 1024, W.name, 'ld_g%d' % (m % 2))
            load(fw, W, 1024, d['wC_g'], (8 + m) * 1024, 1024, W.name, 'ld_g%d' % (m % 2))
            pb = [C.ps[(m % 2) * 4 + i] for i in range(4)]
            for gi in range(2):
                for kt in range(8):
                    fw.op('pe', lambda e: e.matmul(pb[gi].c(0, 512), W.c(gi * 1024 + kt * 128, 128), xc.c(kt * 512, 512),
                                                   start=(kt == 0), stop=(kt == 7)),
                          reads=[W.name, 'xc'], writes=[pb[gi].name], signal=(kt == 7))
            for bi, (wb, oc) in enumerate(((wbn, onc), (wbd, odc))):
                for ft in range(4):
                    fw.op('pe', lambda e: e.matmul(pb[2 + bi].c(0, 512), wb.c(ft * 1024 + m * 128, 128), oc.c(ft * 512, 512),
                                                   start=(ft == 0), stop=(ft == 3)),
                          reads=[wb.name, oc.name], writes=[pb[2 + bi].name], signal=(ft == 3))
            for gi in range(2):
                fw.op('act', lambda e: e.activation(out=sg[gi].c(0, 512), in_=pb[gi].c(0, 512), func=AF.Sigmoid),
                      reads=[pb[gi].name], writes=[sg[gi].name])
            fw.op('dve', lambda e: e.tensor_tensor(out=mT.c(m * 512, 512), in0=sg[0].c(0, 512), in1=pb[2].c(0, 512), op=ALU.mult),
                  reads=[sg[0].name, pb[2].name], writes=['mT%d' % m])
            fw.op('dve', lambda e: e.tensor_tensor(out=tmp.c(0, 512), in0=sg[1].c(0, 512), in1=pb[3].c(0, 512), op=ALU.mult),
                  reads=[sg[1].name, pb[3].name], writes=['tmpc'])
            fw.op('pool', lambda e: e.tensor_tensor(out=mT.c(m * 512, 512), in0=mT.c(m * 512, 512), in1=tmp.c(0, 512), op=ALU.add),
                  reads=['mT%d' % m, 'tmpc'], writes=['mT%d' % m])
        for tl in range(4):
            tt = c * 4 + tl
            fw.dma('sp', lambda q: q.dma_start(out=xt.c(0, 1024), in_=dram2(d['xtm'], 1024, tt * 128, 128, 0, [(1, 1024)])),
                   'ld_xt', writes=['xt'])
            for hf in range(2):
                ps = C.ps[hf]
                for ft in range(8):
                    fw.op('pe', lambda e: e.matmul(ps.c(0, 512), mT.c(ft * 512 + tl * 128, 128), wout.c(ft * 1024 + hf * 512, 512),
                                                   start=(ft == 0), stop=(ft == 7)),
                          reads=['mT%d' % ft, 'wout'], writes=[ps.name], signal=(ft == 7))
                fw.op('dve', lambda e: e.scalar_tensor_tensor(out=y.c(hf * 512, 512), in0=xt.c(hf * 512, 512), scalar=ALPHA,
                                                              in1=ps.c(0, 512), op0=ALU.mult, op1=ALU.add),
                      reads=['xt', ps.name], writes=['y'])
            h = ho[tt % 2]
            layer_norm_tile(fw, C, y, gb, 0, h, sm, 'c')
            fw.dma('sp', lambda q: q.dma_start(out=dram2(C.h1, 1024, tt * 128, 128, 0, [(1, 1024)]), in_=h.c(0, 1024)),
                   'st_h1', reads=[h.name], writes=['h1_dram%d' % tt])
    free_to(C.stack, n0, fw)


def phase_D(fw, C):
    nc = fw.nc
    d = C.d
    n0 = len(C.stack)
    ident = C.ident
    eid = alloc(nc, C.stack, 'eid', 2048, dtype=U32)
    gates = alloc(nc, C.stack, 'gates', 2048)
    n1 = len(C.stack)
    wq = alloc(nc, C.stack, 'wq', 8 * 2048)
    sk = alloc(nc, C.stack, 'sk', 256)
    iota = alloc(nc, C.stack, 'iota', 16)
    h1T = alloc(nc, C.stack, 'h1T', 4096)
    qT = alloc(nc, C.stack, 'qT', 16 * 512)
    ht = [alloc(nc, C.stack, 'ht%d' % i, 1024) for i in range(2)]
    scbs = [alloc(nc, C.stack, 'scb%s' % x, 2048) for x in ('A', 'B')]
    work = alloc(nc, C.stack, 'work', 512)
    tv = alloc(nc, C.stack, 'tv', 256)
    ti = alloc(nc, C.stack, 'ti', 256, dtype=U32)
    tif = alloc(nc, C.stack, 'tif', 256)
    cand = alloc(nc, C.stack, 'cand', 2048)
    cv = alloc(nc, C.stack, 'cv', 128)
    cpos = alloc(nc, C.stack, 'cpos', 128, dtype=U32)
    apos = alloc(nc, C.stack, 'apos', 256, dtype=U32)
    abf = alloc(nc, C.stack, 'abf', 256)
    isel = alloc(nc, C.stack, 'isel', 256)
    eidf = alloc(nc, C.stack, 'eidf', 128)
    gsh = alloc(nc, C.stack, 'gsh', 128)
    gs = alloc(nc, C.stack, 'gs', 16)
    for q8 in range(8):
        load(fw, wq, q8 * 2048, d['wq'], q8 * 2048, 2048, 'wq', 'ld_w')
    load(fw, sk, 0, d['sk'], 0, 256, 'sk', 'ld_w')
    load(fw, iota, 0, d['iota16'], 0, 16, 'iota', 'ld_w')
    for c in range(4):
        for tl in range(4):
            tt = c * 4 + tl
            H = ht[tl % 2]
            fw.dma('sp', lambda q: q.dma_start(out=H.c(0, 1024), in_=dram2(C.h1, 1024, tt * 128, 128, 0, [(1, 1024)])),
                   'ld_h%d' % (tl % 2), reads=['h1_dram%d' % tt], writes=[H.name])
            for a in range(2):
                ps = C.ps[a]
                for k4 in range(4):
                    kt = a * 4 + k4
                    fw.op('pe', lambda e: e.transpose(ps.c(k4 * 128, 128), H.c(kt * 128, 128), ident.c(0, 128)),
                          reads=[H.name, 'ident'], writes=[ps.name], signal=(k4 == 3))
                vcopy(fw, 'act', h1T.ap(a * 4 * 512 + tl * 128, [(512, 4), (1, 128)]), ps.ap(0, [(128, 4), (1, 128)]),
                      [ps.name], ['h1T'])
        for n in range(16):
            ps = C.ps[2 + (n % 2)]
            for kt in range(8):
                fw.op('pe', lambda e: e.matmul(ps.c(0, 512), wq.c(kt * 2048 + n * 128, 128), h1T.c(kt * 512, 512),
                                               start=(kt == 0), stop=(kt == 7)),
                      reads=['wq', 'h1T'], writes=[ps.name], signal=(kt == 7))
            vcopy(fw, 'act', qT.c(n * 512, 512), ps.c(0, 512), [ps.name], ['qT'])
        for tl in range(4):
            tt = c * 4 + tl
            for nb in range(4):
                ps = C.ps[4 + (nb % 2)]
                for ni in range(4):
                    n = nb * 4 + ni
                    fw.op('pe', lambda e: e.matmul(ps.c(ni * 128, 128), qT.c(n * 512 + tl * 128, 128), sk.c((n % 2) * 128, 128),
                                                   start=True, stop=True, skip_group_check=True),
                          reads=['qT', 'sk'], writes=[ps.name], signal=(ni == 3))
                vcopy(fw, 'act', scbs[tl % 2].c(nb * 512, 512), ps.c(0, 512), [ps.name], ['scb%d_%d' % (tl % 2, nb)])
            for n0_ in range(0, 16, 2):
                pr = (n0_, n0_ + 1)
                sc2 = [scbs[tl % 2].c(n * 128, 128) for n in pr]
                rk2 = [['scb%d_%d' % (tl % 2, n // 4)] for n in pr]
                wk2 = [work.c(k * 128, 128) for k in range(2)]
                for k, n in enumerate(pr):
                    fw.op('dve', lambda e: e.max(out=tv.c(n * 16, 8), in_=sc2[k]), reads=rk2[k], writes=['tva%d' % k])
                for k, n in enumerate(pr):
                    fw.op('dve', lambda e: e.max_index(out=ti.c(n * 16, 8), in_max=tv.c(n * 16, 8), in_values=sc2[k]),
                          reads=rk2[k] + ['tva%d' % k], writes=['ti%d' % k])
                for k, n in enumerate(pr):
                    fw.op('dve', lambda e: e.match_replace(out=wk2[k], in_to_replace=tv.c(n * 16, 8), in_values=sc2[k], imm_value=-1e30),
                          reads=rk2[k] + ['tva%d' % k], writes=['work%d' % k])
                for k, n in enumerate(pr):
                    fw.op('dve', lambda e: e.max(out=tv.c(n * 16 + 8, 8), in_=wk2[k]), reads=['work%d' % k], writes=['tvb%d' % k])
                for k, n in enumerate(pr):
                    fw.op('dve', lambda e: e.max_index(out=ti.c(n * 16 + 8, 8), in_max=tv.c(n * 16 + 8, 8), in_values=wk2[k]),
                          reads=['work%d' % k, 'tvb%d' % k], writes=['ti%d' % k])
            fw.op('dve', lambda e: e.tensor_copy(tif.c(0, 256), ti.c(0, 256)), reads=['ti0', 'ti1'], writes=['tif'])
            fw.op('dve', lambda e: e.tensor_tensor(out=cand.ap(0, [(256, 8), (16, 16), (1, 16)]),
                                                   in0=tv.ap(0, [(32, 8), (1, 16), (0, 16)]),
                                                   in1=tv.ap(16, [(32, 8), (0, 16), (1, 16)]), op=ALU.add),
                  reads=['tva0', 'tva1', 'tvb0', 'tvb1'], writes=['cand'])
            for h0_ in range(0, 8, 2):
                pr = (h0_, h0_ + 1)
                cd2 = [cand.c(h * 256, 256) for h in pr]
                wk2 = [work.c(k * 256, 256) for k in range(2)]
                for k, h in enumerate(pr):
                    fw.op('dve', lambda e: e.max(out=cv.c(h * 16, 8), in_=cd2[k]), reads=['cand'], writes=['cva%d' % k])
                for k, h in enumerate(pr):
                    fw.op('dve', lambda e: e.max_index(out=cpos.c(h * 16, 8), in_max=cv.c(h * 16, 8), in_values=cd2[k]),
                          reads=['cand', 'cva%d' % k], writes=['cpos%d' % k])
                for k, h in enumerate(pr):
                    fw.op('dve', lambda e: e.match_replace(out=wk2[k], in_to_replace=cv.c(h * 16, 8), in_values=cd2[k], imm_value=-1e30),
                          reads=['cand', 'cva%d' % k], writes=['work%d' % k])
                for k, h in enumerate(pr):
                    fw.op('dve', lambda e: e.max(out=cv.c(h * 16 + 8, 8), in_=wk2[k]), reads=['work%d' % k], writes=['cvb%d' % k])
                for k, h in enumerate(pr):
                    fw.op('dve', lambda e: e.max_index(out=cpos.c(h * 16 + 8, 8), in_max=cv.c(h * 16 + 8, 8), in_values=wk2[k]),
                          reads=['work%d' % k, 'cvb%d' % k], writes=['cpos%d' % k])
            v3 = [(16, 8), (1, 16)]
            fw.op('dve', lambda e: e.tensor_tensor(out=gsh.ap(0, v3), in0=cv.ap(0, v3), in1=cv.ap(0, [(16, 8), (0, 16)]), op=ALU.subtract),
                  reads=['cva0', 'cva1', 'cvb0', 'cvb1'], writes=['gsh'])
            fw.op('act', lambda e: e.activation(out=gsh.c(0, 128), in_=gsh.c(0, 128), func=AF.Exp), reads=['gsh'], writes=['gsh'])
            fw.op('dve', lambda e: e.tensor_reduce(out=gs.c(0, 8), in_=gsh.ap(0, v3), axis=AX.X, op=ALU.add), reads=['gsh'], writes=['gs'])
            fw.op('dve', lambda e: e.reciprocal(gs.c(8, 8), gs.c(0, 8)), reads=['gs'], writes=['gr'])
            fw.op('dve', lambda e: e.tensor_tensor(out=gates.ap(tt * 128, v3), in0=gsh.ap(0, v3), in1=gs.ap(8, [(1, 8), (0, 16)]), op=ALU.mult),
                  reads=['gsh', 'gr'], writes=['gates'])
            fw.op('dve', lambda e: e.tensor_scalar(out=apos.c(0, 128), in0=cpos.c(0, 128), scalar1=4, scalar2=None,
                                                   op0=ALU.logical_shift_right), reads=['cpos0', 'cpos1'], writes=['apos'])
            fw.op('dve', lambda e: e.tensor_scalar(out=apos.c(128, 128), in0=cpos.c(0, 128), scalar1=15, scalar2=None,
                                                   op0=ALU.bitwise_and), reads=['cpos0', 'cpos1'], writes=['bpos'])
            fw.op('dve', lambda e: e.tensor_copy(abf.c(0, 256), apos.c(0, 256)), reads=['apos', 'bpos'], writes=['abf'])
            v4 = [(256, 8), (16, 16), (1, 16)]
            for ab in range(2):
                fw.op('dve', lambda e: e.tensor_tensor(out=cand.ap(0, v4), in0=abf.ap(ab * 128, [(16, 8), (1, 16), (0, 16)]),
                                                       in1=iota.ap(0, [(0, 8), (0, 16), (1, 16)]), op=ALU.is_equal),
                      reads=['abf', 'iota', 'cand'], writes=['cand'])
                fw.op('dve', lambda e: e.tensor_tensor(out=cand.ap(0, v4), in0=cand.ap(0, v4),
                                                       in1=tif.ap(ab * 16, [(32, 8), (0, 16), (1, 16)]), op=ALU.mult),
                      reads=['cand', 'tif'], writes=['cand'])
                fw.op('dve', lambda e: e.tensor_reduce(out=isel.ap(ab * 128, v3), in_=cand.ap(0, v4), axis=AX.X, op=ALU.add),
                      reads=['cand'], writes=['isel%d' % ab])
            fw.op('dve', lambda e: e.scalar_tensor_tensor(out=eidf.c(0, 128), in0=isel.c(0, 128), scalar=128.0, in1=isel.c(128, 128),
                                                          op0=ALU.mult, op1=ALU.add), reads=['isel0', 'isel1'], writes=['eidf'])
            fw.op('dve', lambda e: e.tensor_copy(eid.c(tt * 128, 128), eidf.c(0, 128)), reads=['eidf'], writes=['eid'])
    dump(fw, C, gates, 0, 2048, ['gates'], 10000)
    dump(fw, C, eidf, 0, 128, ['eidf'], 12048)
    free_to(C.stack, n1, fw)
    NR = 12
    uv = [alloc(nc, C.stack, 'uv%d' % i, 2048) for i in range(NR)]
    dg = [alloc(nc, C.stack, 'dg%d' % i, 128) for i in range(4)]
    wpg = alloc(nc, C.stack, 'wpg', 8192)
    wpp = alloc(nc, C.stack, 'wpp', 2048)
    gb = alloc(nc, C.stack, 'gb', 2048)
    htd = [alloc(nc, C.stack, 'htd%d' % i, 1024) for i in range(2)]
    pTt = alloc(nc, C.stack, 'pTt', 256)
    y = alloc(nc, C.stack, 'y', 1024)
    h2 = alloc(nc, C.stack, 'h2', 1024)
    h2T = alloc(nc, C.stack, 'h2T', 1024)
    dots = alloc(nc, C.stack, 'dots', 264)
    accS = alloc(nc, C.stack, 'accS', 1024)
    sgp = alloc(nc, C.stack, 'sgp', 512)
    tmp = alloc(nc, C.stack, 'tmpd', 512)
    outt = [alloc(nc, C.stack, 'outt%d' % i, 1024) for i in range(1)]
    sm = alloc(nc, C.stack, 'smdd', 32)
    for q4 in range(4):
        load(fw, wpg, q4 * 2048, d['wpg'], q4 * 2048, 2048, 'wpg', 'ld_w')
    load(fw, wpp, 0, d['wpp'], 0, 2048, 'wpp', 'ld_w')
    fw.dma('sp', lambda q: q.dma_start(out=gb.c(0, 2048), in_=bass.AP(d['ln'], 2048, [[0, 128], [1, 2048]])), 'ld_w', writes=['gb'])
    accb = [C.ps[6], C.ps[7]]
    LA = NR - 2
    total = NT * 128
    DVE_EVERY = 8
    LAST_PE = 126

    def gather(n):
        tt, ex = divmod(n, 128)
        UV = uv[n % NR]
        fw.dma('pool', lambda q: q.indirect_dma_start(out=UV.c(0, 2048), out_offset=None, in_=d['puv'].ap(),
                                                      in_offset=bass.IndirectOffsetOnAxis(ap=eid.c(tt * 128 + ex, 1), axis=0)),
               'g_' + UV.name, reads=['eid'], writes=[UV.name])

    def front(n):
        tt, ex = divmod(n, 128)
        UV = uv[n % NR]
        H = htd[tt % 2]
        if ex == 0:
            fw.dma('sp', lambda q: q.dma_start(out=H.c(0, 1024), in_=dram2(C.h1, 1024, tt * 128, 128, 0, [(1, 1024)])),
                   'ld_h%d' % (tt % 2), reads=['h1_dram%d' % tt], writes=[H.name])
        fw.op('dve', lambda e: e.scalar_tensor_tensor(out=UV.c(0, 1024), in0=UV.c(0, 1024), scalar=1.0, in1=H.c(0, 1024),
                                                      op0=ALU.mult, op1=ALU.mult, accum_out=dots.c(ex, 1)),
              reads=[UV.name, H.name], writes=[UV.name + 'u', 'dot%d' % (n % 4)])
        fw.op('act', lambda e: e.activation(out=dots.c(128 + ex, 1), in_=dots.c(ex, 1), func=AF.Gelu),
              reads=['dot%d' % (n % 4)], writes=['gel%d' % (n % 4)])
        fw.op('act', lambda e: e.activation(out=dots.c(256 + (n % 4), 1), in_=dots.c(128 + ex, 1), func=AF.Copy,
                                            scale=gates.c(tt * 128 + ex, 1)),
              reads=['gel%d' % (n % 4), 'gates'], writes=['gav%d' % (n % 4)])
        if ex % DVE_EVERY != DVE_EVERY - 1:
            DG = dg[n % 4]
            fw.op('act', lambda e: e.activation(out=DG.c(0, 128), in_=ident.c(0, 128), func=AF.Copy, scale=dots.c(256 + (n % 4), 1)),
                  reads=['ident', 'gav%d' % (n % 4)], writes=[DG.name])

    def back(n):
        tt, ex = divmod(n, 128)
        UV = uv[n % NR]
        if ex % DVE_EVERY == DVE_EVERY - 1:
            if ex == DVE_EVERY - 1:
                fw.op('dve', lambda e: e.tensor_scalar(out=accS.c(0, 1024), in0=UV.c(1024, 1024), scalar1=dots.c(256 + (n % 4), 1),
                                                       scalar2=None, op0=ALU.mult),
                      reads=[UV.name, 'gav%d' % (n % 4)], writes=['accS'])
            else:
                fw.op('dve', lambda e: e.scalar_tensor_tensor(out=accS.c(0, 1024), in0=UV.c(1024, 1024), scalar=dots.c(256 + (n % 4), 1),
                                                              in1=accS.c(0, 1024), op0=ALU.mult, op1=ALU.add),
                      reads=[UV.name, 'gav%d' % (n % 4), 'accS'], writes=['accS'])
            return
        DG = dg[n % 4]
        for hf in range(2):
            fw.op('pe', lambda e: e.matmul(accb[hf].c(0, 512), DG.c(0, 128), UV.c(1024 + hf * 512, 512),
                                           start=(ex == 0), stop=(ex == LAST_PE)),
                  reads=[DG.name, UV.name], writes=[accb[hf].name], signal=(hf == 1))

    for n in range(LA):
        gather(n)
    for n in range(total + 1):
        if n + LA < total:
            gather(n + LA)
        if n < total:
            front(n)
        if n >= 1:
            back(n - 1)
        if n < 1 or (n - 1) % 128 != 127:
            continue
        tt = (n - 1) // 128
        H = htd[tt % 2]
        fw.dma('sp', lambda q: q.dma_start(out=pTt.ap(0, [(128, 2), (1, 128)]),
                                           in_=dram2(d['pT'], 4096, 0, 128, tt * 128, [(2048, 2), (1, 128)])),
               'ld_p', writes=['pTt'])
        for hf in range(2):
            fw.op('dve', lambda e: e.scalar_tensor_tensor(out=y.c(hf * 512, 512), in0=H.c(hf * 512, 512), scalar=ALPHA,
                                                          in1=accb[hf].c(0, 512), op0=ALU.mult, op1=ALU.add),
                  reads=[H.name, accb[hf].name], writes=['y'])
        fw.op('dve', lambda e: e.tensor_tensor(out=y.c(0, 1024), in0=y.c(0, 1024), in1=accS.c(0, 1024), op=ALU.add),
              reads=['y', 'accS'], writes=['y'])
        layer_norm_tile(fw, C, y, gb, 0, h2, sm, 'd', eng='dve')
        for a in range(2):
            ps = C.ps[a]
            for k4 in range(4):
                kt = a * 4 + k4
                fw.op('pe', lambda e: e.transpose(ps.c(k4 * 128, 128), h2.c(kt * 128, 128), ident.c(0, 128)),
                      reads=['h2', 'ident'], writes=[ps.name], signal=(k4 == 3))
            vcopy(fw, 'act', h2T.c(a * 512, 512), ps.c(0, 512), [ps.name], ['h2T%d' % a])
        O = outt[0]
        for hf in range(2):
            psg = C.ps[2 + hf]
            psp = C.ps[4 + hf]
            for kt in range(8):
                fw.op('pe', lambda e: e.matmul(psg.c(0, 512), h2T.c(kt * 128, 128), wpg.c(kt * 1024 + hf * 512, 512),
                                               start=(kt == 0), stop=(kt == 7)),
                      reads=['h2T0', 'h2T1', 'wpg'], writes=[psg.name], signal=(kt == 7))
            for kt in range(2):
                fw.op('pe', lambda e: e.matmul(psp.c(0, 512), pTt.c(kt * 128, 128), wpp.c(kt * 1024 + hf * 512, 512),
                                               start=(kt == 0), stop=(kt == 1)),
                      reads=['pTt', 'wpp'], writes=[psp.name], signal=(kt == 1))
            fw.op('act', lambda e: e.activation(out=sgp.c(0, 512), in_=psg.c(0, 512), func=AF.Sigmoid), reads=[psg.name], writes=['sgp'])
            fw.op('dve', lambda e: e.tensor_tensor(out=tmp.c(0, 512), in0=sgp.c(0, 512), in1=psp.c(0, 512), op=ALU.mult),
                  reads=['sgp', psp.name], writes=['tmpd'])
            fw.op('dve', lambda e: e.tensor_tensor(out=O.c(hf * 512, 512), in0=tmp.c(0, 512), in1=h2.c(hf * 512, 512), op=ALU.add),
                  reads=['tmpd', 'h2'], writes=[O.name])
        fw.dma('sp', lambda q: q.dma_start(out=dram2(C.out, 1024, tt * 128, 128, 0, [(1, 1024)]), in_=O.c(0, 1024)),
               'st_out', reads=[O.name])
    free_to(C.stack, n0, fw)


def build(stage=99, debug=False):
    nc = bass.Bass("TRN2", target_bir_lowering=False)
    C = Ctx()
    C.d = {k: nc.dram_tensor(k, shp, F32, kind="ExternalInput") for k, shp in IN_SHAPES.items()}
    dbg_kind = "ExternalOutput" if debug else "Internal"
    C.onT = nc.dram_tensor("onT", [128, 8192], F32, kind=dbg_kind)
    C.odT = nc.dram_tensor("odT", [128, 8192], F32, kind=dbg_kind)
    C.h1 = nc.dram_tensor("h1", [2048, 1024], F32, kind=dbg_kind)
    C.out = nc.dram_tensor("out", [2048, 1024], F32, kind="ExternalOutput")
    fw = FW(nc)
    C.dbg = nc.dram_tensor("dbg", [128, 32768], F32, kind="ExternalOutput") if debug else None
    C.stack = []
    C.ps = [alloc(nc, C.stack, 'ps%d' % i, 512, psum=True) for i in range(8)]
    C.ident = alloc(nc, C.stack, 'ident', 128)
    C.qscale = alloc(nc, C.stack, 'qscale', 8)
    load(fw, C.ident, 0, C.d['ident'], 0, 128, 'ident', 'ld_w')
    load(fw, C.qscale, 0, C.d['qscale'], 0, 8, 'qscale', 'ld_w')
    phase_A(fw, C)
    if stage >= 2:
        phase_B(fw, C)
    if stage >= 3:
        phase_C(fw, C)
    if stage >= 4:
        phase_D(fw, C)
    for s in list(fw.sems):
        if fw.cnt[s] > 0 and s != 'E_sp':
            fw.eng['sp'].wait_ge(fw.sems[s], fw.cnt[s])
    free_to(C.stack, 0)
    fw.close()
    print("instructions:", fw.n_inst)
    return nc


def kernel(**inputs):
    inp = {k: np.asarray(v) for k, v in inputs.items()}
    nc = build(stage=99, debug=False)
    in_maps = []
    for b in range(8):
        dmap = pack_inputs(inp, b)
        in_maps.append({k: np.ascontiguousarray(dmap[k], dtype=np.float32) for k in IN_SHAPES})
    res = run_bass_kernel_spmd(nc, in_maps, core_ids=list(range(8)))
    out = np.stack([np.asarray(res.results[b]['out'], dtype=np.float32) for b in range(8)], axis=0)
    return out
```

```python
import numpy as np
import concourse.bass as bass
import concourse.mybir as mybir
from concourse.bass_utils import run_bass_kernel_spmd

F32 = mybir.dt.float32
U32 = mybir.dt.uint32
AF = mybir.ActivationFunctionType
ALU = mybir.AluOpType
AX = mybir.AxisListType

S = 2048
D = 1024
NT = 16
BIGS = 1.0e5
SLOPE_N = [2.0 ** (-(h + 1)) for h in range(8)]
SLOPE_D = [2.0 ** (-2 * (h + 1)) for h in range(4)]
LAM_INIT = 0.2
ALPHA = 2.0 ** 0.25


class FW:
    def __init__(self, nc):
        self.nc = nc
        self.eng = {'pe': nc.tensor, 'dve': nc.vector, 'act': nc.scalar,
                    'pool': nc.gpsimd, 'sp': nc.sync}
        self.sems = {}
        self.cnt = {}
        self.waited = {e: {} for e in self.eng}
        self.state = {}
        self._cms = []
        for e in self.eng:
            self._mksem('E_' + e)
        self.n_inst = 0
        self._rr = {}

    def _mksem(self, name):
        cm = self.nc.semaphore(name)
        s = cm.__enter__()
        self._cms.append(cm)
        self.sems[name] = s
        self.cnt[name] = 0
        return s

    def close(self):
        for cm in reversed(self._cms):
            cm.__exit__(None, None, None)

    def _deps(self, reads, writes):
        deps = {}

        def add(d):
            if d is None:
                return
            s, v = d
            if deps.get(s, 0) < v:
                deps[s] = v
        for k in reads:
            st = self.state.get(k)
            if st:
                add(st['w'])
        for k in writes:
            st = self.state.get(k)
            if st:
                add(st['w'])
                for s, v in st['r'].items():
                    add((s, v))
        return deps

    def _wait(self, e, deps):
        own = 'E_' + e
        for s, v in deps.items():
            if e == 'pe' and s == own:
                continue
            if self.waited[e].get(s, 0) >= v:
                continue
            self.eng[e].wait_ge(self.sems[s], v)
            self.waited[e][s] = v

    def _record(self, reads, writes, tok):
        for k in reads:
            st = self.state.setdefault(k, {'w': None, 'r': {}})
            s, v = tok
            if st['r'].get(s, 0) < v:
                st['r'][s] = v
        for k in writes:
            self.state[k] = {'w': tok, 'r': {}}

    def op(self, e, fn, reads=(), writes=(), signal=True):
        deps = self._deps(reads, writes)
        self._wait(e, deps)
        inst = fn(self.eng[e])
        own = 'E_' + e
        if signal:
            self.cnt[own] += 1
            inst.then_inc(self.sems[own], 1)
            tok = (own, self.cnt[own])
        else:
            tok = (own, self.cnt[own] + 1)
        self._record(reads, writes, tok)
        self.n_inst += 1
        return inst

    POOLS = {'ld_w': 8}

    def dma(self, q, fn, sem, reads=(), writes=()):
        if sem in self.POOLS:
            n = self._rr.get(sem, 0)
            self._rr[sem] = n + 1
            sem = '%s_%d' % (sem, n % self.POOLS[sem])
        if sem not in self.sems:
            self._mksem(sem)
        deps = self._deps(reads, writes)
        if self.cnt[sem] > 0:
            deps[sem] = max(deps.get(sem, 0), self.cnt[sem])
        self._wait(q, deps)
        inst = fn(self.eng[q])
        self.cnt[sem] += 16
        inst.then_inc(self.sems[sem], 16)
        self._record(reads, writes, (sem, self.cnt[sem]))
        self.n_inst += 1
        return inst

    def wait_all(self, e, keys):
        self._wait(e, self._deps(keys, keys))

    def barrier(self):
        for e in self.eng:
            own = 'E_' + e
            for s, v in self.cnt.items():
                if v == 0 or (s == own and e in ('pe', 'sp')):
                    continue
                if self.waited[e].get(s, 0) >= v:
                    continue
                self.eng[e].wait_ge(self.sems[s], v)
                self.waited[e][s] = v


class T:
    def __init__(self, t, ncols, name):
        self.t, self.n, self.name = t, ncols, name

    def ap(self, col=0, dims=None, p0=0, np_=128):
        if dims is None:
            dims = [(1, self.n - col)]
        return bass.AP(self.t, p0 * self.n + col, [[self.n, np_]] + [[s, n] for s, n in dims])

    def c(self, col, n, p0=0, np_=128):
        return self.ap(col, [(1, n)], p0, np_)


def dram2(t, rowlen, row0, nrows, col, dims):
    return bass.AP(t, row0 * rowlen + col, [[rowlen, nrows]] + [[s, n] for s, n in dims])


def pack_fm(w, cols):
    sub = w[:, cols]
    M = sub.shape[1]
    return np.ascontiguousarray(sub.reshape(8, 128, M).transpose(1, 0, 2).reshape(128, 8 * M))


def pack_rows(w, kt):
    M = w.shape[1]
    return np.ascontiguousarray(w.reshape(kt, 128, M).transpose(1, 0, 2).reshape(128, kt * M))


def host_consts():
    c = {}
    c['ident'] = np.eye(128, dtype=np.float32)
    kl = np.arange(128)[:, None]
    m = np.arange(2048)[None, :]
    v = m - kl
    c['tstrip'] = np.where(v >= 0, -v, -BIGS).astype(np.float32)
    m2 = np.arange(1024)[None, :]
    v2 = m2 - kl
    c['wstrip'] = np.where((v2 >= 0) & (v2 < 512), -v2, -BIGS).astype(np.float32)
    cend = 16 * np.arange(128)[:, None] + 31
    vq = np.arange(2048)[None, :] - cend
    cb = np.where(vq >= 0, -vq, -BIGS).astype(np.float32)
    cb[127, :] = -BIGS
    c['cmpbias'] = cb
    e = np.zeros((32, 2048), np.float32)
    e[np.arange(2048) // 64, np.arange(2048)] = BIGS
    c['eblk'] = e
    t = (np.arange(16)[None, :, None] * 128 + np.arange(128)[:, None, None])
    jb = np.arange(32)[None, None, :]
    cur = t // 64
    allowed = jb <= cur
    forced = (jb == 0) | (jb == cur) | (jb == cur - 1)
    A = (allowed & ~forced).astype(np.float32)
    B = np.where(forced, 1.0e4, np.where(allowed, 0.0, -1.0e4)).astype(np.float32)
    c['selA'] = np.ascontiguousarray(A.reshape(128, 512))
    c['selB'] = np.ascontiguousarray(B.reshape(128, 512))
    c0 = np.arange(128)[:, None] * 16
    s0 = np.arange(32)[None, :] * 64
    ov = np.minimum(c0 + 32, s0 + 64) - np.maximum(c0, s0)
    sw = (np.clip(ov, 0, None) / 32.0).astype(np.float32)
    sw[127, :] = 0
    c['selw'] = sw
    qs = np.zeros((128, 8), np.float32)
    for i in range(4):
        for p in range(128):
            qs[p, i] = 0.125 / SLOPE_N[2 * i + p // 64]
            qs[p, 4 + i] = 0.125 / SLOPE_D[2 * (i % 2) + p // 64]
    c['qscale'] = qs
    bt = np.zeros((128, 8 * 36), np.float32)
    for si in range(8):
        for mi in range(36):
            bt[:, si * 36 + mi] = (2.0 ** (-(si + 1))) * (np.arange(128) - 64 * (mi - 2))
    c['btab'] = bt
    ql_ = np.arange(128)[None, :]
    tri = np.where(kl > ql_, -BIGS, 0.0).astype(np.float32)
    anti = np.where(ql_ >= kl, -BIGS, 0.0).astype(np.float32)
    c['masks'] = np.ascontiguousarray(np.concatenate([tri, anti], axis=1))
    e01 = np.zeros((128, 2048), np.float32)
    e01[:32] = (e > 0)
    c['e01'] = e01
    c['iota16'] = np.tile(np.arange(16, dtype=np.float32)[None, :], (128, 1))
    return c


def pack_inputs(inp, b):
    w_in = inp['w_in'][0]
    x = inp['x'][b]
    d = {}
    d['xT'] = pack_rows(np.ascontiguousarray(x.T), 8)
    d['xtm'] = np.ascontiguousarray(x)
    kv0 = 512
    tiles = [np.arange(128 * i, 128 * i + 128) for i in range(4)]
    tiles.append(kv0 + np.arange(128))
    tiles.append(kv0 + 128 + np.arange(128))
    for j in (2, 4):
        for g in range(2):
            cg = kv0 + j * 128 + g * 64 + np.arange(64)
            tiles.append(np.concatenate([cg, cg]))
    d['wA_fm'] = np.concatenate([pack_fm(w_in, t) for t in tiles], axis=1)
    tmcols = np.concatenate([kv0 + 3 * 128 + np.arange(128), kv0 + 5 * 128 + np.arange(128),
                             kv0 + 768 + np.arange(24)])
    d['wA_tm'] = pack_fm(w_in, tmcols)
    for nm, w1, w2, pos in (('k', 'cmp_k_w1', 'cmp_k_w2', 'cmp_pos_k'), ('v', 'cmp_v_w1', 'cmp_v_w2', 'cmp_pos_v')):
        W1 = inp[w1][0].reshape(32, 64, 256)
        W1p = W1.transpose(1, 0, 2).reshape(64, 32 * 256)
        d['w1' + nm] = np.ascontiguousarray(np.concatenate([W1p, W1p], axis=0))
        pT = inp[pos][0].T
        d['pos' + nm] = np.ascontiguousarray(np.concatenate([pT, pT], axis=0))
    w2k = inp['cmp_k_w2'][0]
    d['w2k'] = pack_rows(np.concatenate([w2k, w2k], axis=1), 2)
    d['w2v'] = pack_rows(inp['cmp_v_w2'][0], 2)
    q0 = 512 + 768 + 24
    k0 = q0 + 512
    v0 = k0 + 512
    tb = []
    for base in (q0, k0):
        for mp in range(2):
            for pr in range(2):
                tb.append(base + mp * 256 + pr * 128 + np.arange(128))
    d['wB_fm'] = np.concatenate([pack_fm(w_in, t) for t in tb], axis=1)
    d['wB_tm'] = pack_fm(w_in, v0 + np.arange(512))
    d['lam'] = np.stack([inp['lam_q1'][0], inp['lam_k1'][0], inp['lam_q2'][0], inp['lam_k2'][0]]).reshape(1, 256)
    d['normg'] = inp['diff_norm_g'][0].reshape(1, 128)
    ga0 = v0 + 512
    d['wC_g'] = np.concatenate([pack_fm(w_in, ga0 + 128 * i + np.arange(128)) for i in range(16)], axis=1)
    d['wbn'] = pack_rows(inp['w_branch_nsa'][0], 4)
    d['wbd'] = pack_rows(inp['w_branch_diff'][0], 4)
    d['wout'] = pack_rows(inp['w_out'][0], 8)
    d['ln'] = np.stack([inp['ln1_g'][0], inp['ln1_b'][0], inp['ln2_g'][0], inp['ln2_b'][0]]).reshape(1, 4096)
    d['wq'] = pack_rows(inp['peer_wq'][0], 8)
    d['sk'] = np.ascontiguousarray(np.concatenate([inp['peer_subkeys1'][0].T, inp['peer_subkeys2'][0].T], axis=1))
    d['puv'] = np.concatenate([inp['peer_u'][0], inp['peer_v'][0]], axis=1)
    d['pT'] = pack_rows(np.ascontiguousarray(inp['p'][0, b].T), 2)
    d['wpg'] = pack_rows(inp['ple_w_gate'][0], 8)
    d['wpp'] = pack_rows(inp['ple_w_proj'][0], 2)
    d.update(host_consts())
    return d


IN_SHAPES = {
    'xT': [128, 8 * 2048], 'xtm': [2048, 1024],
    'wA_fm': [128, 10 * 1024], 'wA_tm': [128, 8 * 280],
    'w1k': [128, 32 * 256], 'w1v': [128, 32 * 256], 'posk': [128, 32], 'posv': [128, 32],
    'w2k': [128, 256], 'w2v': [128, 128],
    'wB_fm': [128, 8 * 1024], 'wB_tm': [128, 8 * 512], 'lam': [1, 256], 'normg': [1, 128],
    'wC_g': [128, 16 * 1024], 'wbn': [128, 4096], 'wbd': [128, 4096], 'wout': [128, 8192],
    'ln': [1, 4096], 'wq': [128, 8 * 2048], 'sk': [128, 256],
    'puv': [16384, 2048], 'pT': [128, 2 * 2048],
    'wpg': [128, 8192], 'wpp': [128, 2048],
    'ident': [128, 128], 'tstrip': [128, 2048], 'wstrip': [128, 1024], 'cmpbias': [128, 2048],
    'eblk': [32, 2048], 'selA': [128, 512], 'selB': [128, 512], 'selw': [128, 32], 'qscale': [128, 8], 'iota16': [128, 16], 'btab': [128, 288], 'masks': [128, 256], 'e01': [128, 2048],
}


class Ctx:
    pass


def dump(fw, C, t, col, n, keys, dcol, p0=0, np_=128):
    if C.dbg is None:
        return
    fw.dma('sp', lambda q: q.dma_start(out=dram2(C.dbg, 32768, p0, np_, dcol, [(1, n)]), in_=t.c(col, n, p0, np_)),
           'st_dbg', reads=keys)


_UID = [0]


def alloc(nc, stack, name, ncols, dtype=F32, psum=False, nparts=128):
    _UID[0] += 1
    cm = (nc.psum_tensor if psum else nc.sbuf_tensor)('s%d_%s' % (_UID[0], name), [nparts, ncols], dtype)
    t = cm.__enter__()
    stack.append(cm)
    return T(t, ncols, name)


def free_to(stack, n, fw=None):
    while len(stack) > n:
        stack.pop().__exit__(None, None, None)
    if fw is not None:
        fw.barrier()


def load(fw, dst, dcol, src, scol, ncols, key, sem, nrows=128, p0=0):
    rowlen = src.shape[1]
    fw.dma('sp', lambda q: q.dma_start(out=dst.c(dcol, ncols, p0, nrows),
                                       in_=dram2(src, rowlen, 0, nrows, scol, [(1, ncols)])),
           sem, writes=[key])


def proj_phase(fw, C, wfm, nfm, wtm, ntm, fm_evac, tm_evac):
    nc = fw.nc
    n0 = len(C.stack)
    Wf = alloc(nc, C.stack, 'Wf', nfm * 1024)
    Wt = alloc(nc, C.stack, 'Wt', 8 * ntm)
    xb = [alloc(nc, C.stack, 'xb%d' % i, 8 * 512) for i in range(2)]
    for i in range(nfm):
        load(fw, Wf, i * 1024, wfm, i * 1024, 1024, 'Wf', 'ld_w')
    load(fw, Wt, 0, wtm, 0, 8 * ntm, 'Wt', 'ld_w')
    for c in range(4):
        X = xb[c % 2]
        fw.dma('sp', lambda q: q.dma_start(out=X.ap(0, [(512, 8), (1, 512)]),
                                           in_=dram2(C.d['xT'], 8 * 2048, 0, 128, c * 512, [(2048, 8), (1, 512)])),
               'ld_x%d' % (c % 2), writes=[X.name])
        for i in range(nfm):
            ps = C.ps[6 + (i % 2)]
            for kt in range(8):
                fw.op('pe', lambda e: e.matmul(ps.c(0, 512), Wf.c(i * 1024 + kt * 128, 128), X.c(kt * 512, 512),
                                               start=(kt == 0), stop=(kt == 7)),
                      reads=['Wf', X.name], writes=[ps.name], signal=(kt == 7))
            fm_evac(i, c, ps)
        for tl in range(4):
            ps = C.ps[6 + (tl % 2)]
            for kt in range(8):
                fw.op('pe', lambda e: e.matmul(ps.c(0, ntm), X.c(kt * 512 + tl * 128, 128), Wt.c(kt * ntm, ntm),
                                               start=(kt == 0), stop=(kt == 7)),
                      reads=['Wt', X.name], writes=[ps.name], signal=(kt == 7))
            tm_evac(c * 4 + tl, ps)
    free_to(C.stack, n0, fw)


class AttnItem:
    pass


def attn_stream(fw, C, items):
    prev = None
    n = 0
    for it in items:
        sb = C.ps[n % 3]
        pb = C.pbuf[n % 3]
        mms = [(it.kT, it.qT)] + it.extra
        for mi, (l, r) in enumerate(mms):
            fw.op('pe', lambda e: e.matmul(sb.c(0, it.N), l, r, start=(mi == 0), stop=(mi == len(mms) - 1)),
                  reads=it.reads, writes=[sb.name], signal=(mi == len(mms) - 1))
        fw.op('act', lambda e: e.activation(out=pb.c(0, it.N), in_=sb.c(0, it.N), func=AF.Exp, scale=it.slope),
              reads=[sb.name], writes=[pb.name])
        it.pb = pb
        if prev is not None:
            emit_pv(fw, C, prev)
        prev = it
        n += 1
    if prev is not None:
        emit_pv(fw, C, prev)


def exp_segments(btab, si, qs, qe, j):
    w = 128 if si == 0 else (256 if si == 1 else 512)
    segs = []
    s0 = (qs // w) * w
    while s0 < qe:
        a = max(s0, qs)
        b = min(s0 + w, qe)
        m = (s0 + w // 2 - 128 * j) // 64
        segs.append((a - qs, b - a, btab.c(si * 36 + m + 2, 1)))
        s0 += w
    return segs


def attn_stream2(fw, C, items):
    LAH = 2
    n = len(items)
    ident = C.ident
    for idx in range(n + LAH):
        if idx < n:
            it = items[idx]
            sb = C.ps[idx % 3]
            pb = C.pbuf[idx % 4]
            nm = len(it.masks)
            fw.op('pe', lambda e: e.matmul(sb.c(0, it.N), it.kT, it.qT, start=True, stop=(nm == 0)),
                  reads=it.reads, writes=[sb.name], signal=(nm == 0))
            for mi, (off, map_) in enumerate(it.masks):
                fw.op('pe', lambda e: e.matmul(sb.c(off, 128), ident.c(0, 128), map_, start=False, stop=(mi == nm - 1)),
                      reads=['ident', 'masks'], writes=[sb.name], signal=(mi == nm - 1))
            for (off, wd, bias_ap) in it.exps:
                fw.op('act', lambda e: e.activation(out=pb.c(off, wd), in_=sb.c(off, wd), func=AF.Exp, scale=it.slope, bias=bias_ap),
                      reads=[sb.name, 'btab'], writes=[pb.name])
            if it.mmask is not None:
                fw.op('dve', lambda e: e.tensor_tensor(out=pb.c(0, it.N), in0=pb.c(0, it.N), in1=it.mmask, op=ALU.mult),
                      reads=[pb.name, it.mkey], writes=[pb.name])
            it.pb = pb
        if idx >= LAH:
            it = items[idx - LAH]
            for pi, (out_ap, lhsT, start) in enumerate(it.pvs):
                fw.op('pe', lambda e: e.matmul(out_ap, lhsT, it.pb.c(0, it.N), start=start, stop=False, skip_group_check=True),
                      reads=[it.pb.name] + it.vreads, writes=it.obanks, signal=(pi == len(it.pvs) - 1))
            if it.after is not None:
                it.after()


def emit_pv(fw, C, it):
    for pi, (ps_ap, pcol, rhs, start) in enumerate(it.pv):
        fw.op('pe', lambda e: e.matmul(ps_ap, it.pb.c(pcol, 128), rhs, start=start, stop=False, skip_group_check=True),
              reads=[it.pb.name] + it.vreads, writes=it.obanks, signal=(pi == len(it.pv) - 1))
    if getattr(it, 'after', None) is not None:
        it.after()


def vcopy(fw, eng, out, in_, reads, writes):
    if eng == 'act':
        fw.op('act', lambda e: e.activation(out=out, in_=in_, func=AF.Copy), reads=reads, writes=writes)
    else:
        fw.op(eng, lambda e: e.tensor_copy(out, in_), reads=reads, writes=writes)


def transpose_out(fw, C, och, oT, dram_t, c, tag):
    for f in range(4):
        ps = C.ps[6 + (f % 2)]
        for ql in range(4):
            fw.op('pe', lambda e: e.transpose(ps.c(ql * 128, 128), och.c(ql * 512 + f * 128, 128), C.ident.c(0, 128)),
                  reads=[och.name, 'ident'], writes=[ps.name], signal=(ql == 3))
        vcopy(fw, 'act' if f % 2 else 'dve', oT.c(f * 512, 512), ps.c(0, 512), [ps.name], [oT.name + str(f)])
    fw.dma('sp', lambda q: q.dma_start(out=dram2(dram_t, 8192, 0, 128, c * 512, [(2048, 4), (1, 512)]),
                                       in_=oT.ap(0, [(512, 4), (1, 512)])),
           'st_' + tag, reads=[oT.name + str(f) for f in range(4)], writes=[tag + '_dram%d' % c])


def phase_A(fw, C):
    nc = fw.nc
    d = C.d
    n0 = len(C.stack)
    Qp = [alloc(nc, C.stack, 'Qp%d' % i, 2048) for i in range(4)]
    KsT = [alloc(nc, C.stack, 'KsT%d' % g, 2048) for g in range(2)]
    KwT = [alloc(nc, C.stack, 'KwT%d' % g, 2048) for g in range(2)]
    VsA = alloc(nc, C.stack, 'VsA', 16 * 130 + 64)
    VwA = alloc(nc, C.stack, 'VwA', 16 * 130 + 64)
    gN = alloc(nc, C.stack, 'gN', 16 * 24)
    kcT = [alloc(nc, C.stack, 'kcT%d' % g, 128) for g in range(2)]
    vcA = [alloc(nc, C.stack, 'vcA%d' % g, 97) for g in range(2)]
    n1 = len(C.stack)
    KcT = alloc(nc, C.stack, 'KcT', 2048)
    VcT = alloc(nc, C.stack, 'VcT', 2048)
    fw.op('pool', lambda e: e.memset(VsA.c(0, 16 * 130 + 64), 1.0), writes=['VsA'])
    fw.op('pool', lambda e: e.memset(VwA.c(0, 16 * 130 + 64), 1.0), writes=['VwA'])
    fmdst = [KcT, VcT, KsT[0], KsT[1], KwT[0], KwT[1]]

    def fm_evac(i, c, ps):
        if i < 4:
            fw.op('dve', lambda e: e.tensor_scalar(out=Qp[i].c(c * 512, 512), in0=ps.c(0, 512), scalar1=C.qscale.c(i, 1),
                                                   scalar2=None, op0=ALU.mult),
                  reads=[ps.name, 'qscale'], writes=[Qp[i].name])
        else:
            dst = fmdst[i - 4]
            vcopy(fw, 'act', dst.c(c * 512, 512), ps.c(0, 512), [ps.name], [dst.name])

    def tm_evac(tt, ps):
        fw.op('dve', lambda e: e.tensor_copy(VsA.ap(tt * 130, [(65, 2), (1, 64)]), ps.ap(0, [(64, 2), (1, 64)])),
              reads=[ps.name], writes=['VsA'])
        fw.op('dve', lambda e: e.tensor_copy(VwA.ap(tt * 130, [(65, 2), (1, 64)]), ps.ap(128, [(64, 2), (1, 64)])),
              reads=[ps.name], writes=['VwA'])
        fw.op('act', lambda e: e.activation(out=gN.c(tt * 24, 24), in_=ps.c(256, 24), func=AF.Sigmoid),
              reads=[ps.name], writes=['gN'])

    proj_phase(fw, C, d['wA_fm'], 10, d['wA_tm'], 280, fm_evac, tm_evac)

    dump(fw, C, Qp[0], 0, 2048, [Qp[0].name], 0)
    dump(fw, C, KsT[1], 0, 2048, [KsT[1].name], 2048)
    dump(fw, C, VsA, 0, 2080, ['VsA'], 4096)
    dump(fw, C, gN, 0, 384, ['gN'], 6176)
    dump(fw, C, KcT, 0, 2048, ['KcT'], 6560)
    n2 = len(C.stack)
    W1 = alloc(nc, C.stack, 'W1', 32 * 256)
    pos = alloc(nc, C.stack, 'pos', 32)
    w2k = alloc(nc, C.stack, 'w2k', 256)
    w2v = alloc(nc, C.stack, 'w2v', 128)
    hidT = [alloc(nc, C.stack, 'hidT%d' % h, 128) for h in range(2)]
    crow = alloc(nc, C.stack, 'crow', 256)
    onesr = alloc(nc, C.stack, 'onesr', 128)
    hid = alloc(nc, C.stack, 'hid', 256)
    fw.op('pool', lambda e: e.memset(onesr.c(0, 128), 1.0), writes=['onesr'])
    load(fw, w2k, 0, d['w2k'], 0, 256, 'w2k', 'ld_w')
    load(fw, w2v, 0, d['w2v'], 0, 128, 'w2v', 'ld_w')
    for g in range(2):
        fw.op('pool', lambda e: e.memset(kcT[g].c(0, 128), 0.0), writes=[kcT[g].name])
        fw.op('pool', lambda e: e.memset(vcA[g].c(0, 97), 0.0), writes=[vcA[g].name])
        fw.op('pool', lambda e: e.memset(vcA[g].c(64, 1), 1.0), writes=[vcA[g].name])
        load(fw, vcA[g], 65, d['selw'], 0, 32, vcA[g].name, 'ld_w')
    for kv in range(2):
        src = KcT if kv == 0 else VcT
        for q4 in range(4):
            load(fw, W1, q4 * 2048, d['w1k' if kv == 0 else 'w1v'], q4 * 2048, 2048, 'W1', 'ld_w')
        load(fw, pos, 0, d['posk' if kv == 0 else 'posv'], 0, 32, 'pos', 'ld_w')
        psc = C.ps[5]
        for l in range(32):
            fw.op('pe', lambda e: e.matmul(psc.c(0, 256, 0, 1), pos.c(l, 1, 0, 64), W1.c(l * 256, 256, 0, 64),
                                           start=(l == 0), stop=(l == 31)),
                  reads=['W1', 'pos'], writes=[psc.name], signal=(l == 31))
        vcopy(fw, 'dve', crow.c(0, 256, 0, 1), psc.c(0, 256, 0, 1), [psc.name], ['crow'])
        for g in range(2):
            ps = C.ps[3 + g]
            for l in range(32):
                fw.op('pe', lambda e: e.matmul(ps.c(0, 256, 0, 127), src.ap(l, [(16, 127)], g * 64, 64),
                                               W1.c(l * 256, 256, g * 64, 64), start=(l == 0), stop=False),
                      reads=['W1', src.name], writes=[ps.name], signal=False)
            fw.op('pe', lambda e: e.matmul(ps.c(0, 256, 0, 127), onesr.c(0, 127, 0, 1), crow.c(0, 256, 0, 1), start=False, stop=True),
                  reads=['onesr', 'crow'], writes=[ps.name])
            fw.op('act', lambda e: e.activation(out=hid.c(0, 256, 0, 127), in_=ps.c(0, 256, 0, 127), func=AF.Gelu),
                  reads=[ps.name], writes=['hid'])
            pst = C.ps[6 + g]
            for half in range(2):
                fw.op('pe', lambda e: e.transpose(pst.c(half * 128, 127), hid.c(half * 128, 128, 0, 127), C.ident.c(0, 127, 0, 127)),
                      reads=['hid', 'ident'], writes=[pst.name], signal=(half == 1))
            for half in range(2):
                vcopy(fw, 'dve', hidT[half].c(0, 127), pst.c(half * 128, 127), [pst.name], [hidT[half].name])
            if kv == 0:
                ps = C.ps[1]
                for half in range(2):
                    fw.op('pe', lambda e: e.matmul(ps.c(0, 127), w2k.c(half * 128, 128), hidT[half].c(0, 127),
                                                   start=(half == 0), stop=(half == 1)),
                          reads=['w2k', hidT[half].name], writes=[ps.name], signal=(half == 1))
                vcopy(fw, 'dve', kcT[g].c(0, 127), ps.c(0, 127), [ps.name], [kcT[g].name])
            else:
                ps = C.ps[2]
                for half in range(2):
                    fw.op('pe', lambda e: e.matmul(ps.c(0, 64, 0, 127), hidT[half].c(0, 127), w2v.c(half * 64, 64),
                                                   start=(half == 0), stop=(half == 1)),
                          reads=['w2v', hidT[half].name], writes=[ps.name], signal=(half == 1))
                vcopy(fw, 'dve', vcA[g].c(0, 64, 0, 127), ps.c(0, 64, 0, 127), [ps.name], [vcA[g].name])
    dump(fw, C, kcT[1], 0, 128, [kcT[1].name], 8608)
    dump(fw, C, vcA[1], 0, 97, [vcA[1].name], 8736)
    free_to(C.stack, n1, fw)

    cmpb = alloc(nc, C.stack, 'cmpb', 2048)
    e01 = alloc(nc, C.stack, 'e01', 2048)
    btab = alloc(nc, C.stack, 'btab', 288)
    masks = alloc(nc, C.stack, 'masks', 256)
    selA = alloc(nc, C.stack, 'selA', 512)
    selB = alloc(nc, C.stack, 'selB', 512)
    C.pbuf = [alloc(nc, C.stack, 'pb%d' % i, 512) for i in range(4)]
    och = alloc(nc, C.stack, 'och', 2048)
    oT = alloc(nc, C.stack, 'oT', 2048)
    Mst = alloc(nc, C.stack, 'Mst', 8192)
    Qz = [alloc(nc, C.stack, 'Qz%d' % i, 512) for i in range(8)]
    for i in range(8):
        fw.op('pool', lambda e: e.memset(Qz[i].c(0, 512), 0.0), writes=[Qz[i].name])
    ots = alloc(nc, C.stack, 'ots', 512)
    imp = alloc(nc, C.stack, 'imp', 128)
    score = alloc(nc, C.stack, 'score', 128)
    nm = alloc(nc, C.stack, 'nm', 128)
    negT = [alloc(nc, C.stack, 'negT%d' % g, 512) for g in range(2)]
    sm = alloc(nc, C.stack, 'sm', 32)
    load(fw, cmpb, 0, d['cmpbias'], 0, 2048, 'cmpb', 'ld_w')
    load(fw, e01, 0, d['e01'], 0, 2048, 'e01', 'ld_w')
    for g in range(2):
        fw.op('pool', lambda e: e.memset(negT[g].c(0, 512), 0.0), writes=[negT[g].name])
    load(fw, btab, 0, d['btab'], 0, 288, 'btab', 'ld_w')
    load(fw, masks, 0, d['masks'], 0, 256, 'masks', 'ld_w')
    load(fw, selA, 0, d['selA'], 0, 512, 'selA', 'ld_w')
    load(fw, selB, 0, d['selB'], 0, 512, 'selB', 'ld_w')
    ident = C.ident

    def normalize(O, W, hh, br, c, first_in_group, clamp=1e-30):
        def f():
            fw.op('dve', lambda e: e.tensor_scalar(out=sm.c(0, 4), in0=O.ap(64, [(W, 4)]), scalar1=clamp, scalar2=None,
                                                   op0=ALU.max), reads=[O.name], writes=['sm_rs'])
            fw.op('dve', lambda e: e.reciprocal(sm.c(4, 4), sm.c(0, 4)), reads=['sm_rs'], writes=['sm_rcp'])
            fw.op('dve', lambda e: e.tensor_tensor(out=sm.c(8, 4), in0=sm.c(4, 4),
                                                   in1=gN.ap(4 * c * 24 + hh * 3 + br, [(24, 4)]), op=ALU.mult),
                  reads=['sm_rcp', 'gN'], writes=['sm_rg'])
            for ql in range(4):
                dst = och.c(ql * 512 + hh * 64, 64)
                if br == 0:
                    fw.op('dve', lambda e: e.tensor_scalar(out=dst, in0=O.c(ql * W, 64), scalar1=sm.c(8 + ql, 1), scalar2=None,
                                                           op0=ALU.mult), reads=[O.name, 'sm_rg'], writes=['och'])
                else:
                    fw.op('dve', lambda e: e.scalar_tensor_tensor(out=dst, in0=O.c(ql * W, 64), scalar=sm.c(8 + ql, 1), in1=dst,
                                                                  op0=ALU.mult, op1=ALU.add),
                          reads=[O.name, 'sm_rg', 'och'], writes=['och'])
                if br == 0:
                    idst = imp.c(ql * 32, 32)
                    if first_in_group:
                        fw.op('dve', lambda e: e.tensor_scalar(out=idst, in0=O.c(ql * W + 65, 32), scalar1=sm.c(4 + ql, 1),
                                                               scalar2=None, op0=ALU.mult),
                              reads=[O.name, 'sm_rcp'], writes=['imp'])
                    else:
                        fw.op('dve', lambda e: e.scalar_tensor_tensor(out=idst, in0=O.c(ql * W + 65, 32), scalar=sm.c(4 + ql, 1),
                                                                      in1=idst, op0=ALU.mult, op1=ALU.add),
                              reads=[O.name, 'sm_rcp', 'imp'], writes=['imp'])
        return f

    On = C.ps[5]

    def after_fm(OT, hh, br, c):
        def f():
            vcopy(fw, 'act', ots.c(0, 512, 0, 65), OT.c(0, 512, 0, 65), [OT.name], ['ots'])
            for ql in range(4):
                fw.op('pe', lambda e: e.transpose(On.c(ql * 65, 65), ots.c(ql * 128, 128, 0, 65), ident.c(0, 65, 0, 65)),
                      reads=['ots', 'ident'], writes=[On.name], signal=(ql == 3))
            normalize(On, 65, hh, br, c, False, clamp=1e-37)()
        return f

    def build_items(c, br, heads):
        q0 = c * 512
        items = []
        for hh in heads:
            g = hh // 4
            i, ph = hh // 2, hh % 2
            OT = C.ps[3 + (hh % 2)]
            VA = VsA if br == 1 else VwA
            KT = KsT[g] if br == 1 else KwT[g]
            jlo = 0 if br == 1 else max(0, 4 * c - 4)
            jhi = 4 * c + 3
            first = True
            for j in range(jlo, jhi + 1):
                tlo = max(j, 4 * c)
                thi = 4 * c + 3 if br == 1 else min(j + 4, 4 * c + 3)
                qs_ = tlo * 128
                N = (thi - tlo + 1) * 128
                it = AttnItem()
                it.kT = KT.c(j * 128, 128)
                it.qT = Qz[hh].c(qs_ - q0, N)
                it.reads = [KT.name, Qz[hh].name]
                it.N = N
                it.slope = SLOPE_N[hh]
                it.masks = []
                for tq in range(tlo, thi + 1):
                    Dd = tq - j
                    off = (tq - tlo) * 128
                    if Dd == 0:
                        it.masks.append((off, masks.c(0, 128)))
                    if br == 2 and Dd == 4:
                        it.masks.append((off, masks.c(128, 128)))
                it.exps = exp_segments(btab, hh, qs_, qs_ + N, j)
                if br == 1:
                    it.mmask = Mst.c(j * 512 + qs_ - q0, N)
                    it.mkey = 'Mst%d' % j
                else:
                    it.mmask = None
                it.pvs = [(OT.c(qs_ - q0, N), VA.c(j * 130 + g * 65, 128), first)]
                first = False
                it.vreads = [VA.name]
                it.obanks = [OT.name]
                it.after = after_fm(OT, hh, br, c) if j == jhi else None
                items.append(it)
        return items

    for c in range(4):
        q0 = c * 512
        for hh in range(8):
            fw.op('pool', lambda e: e.tensor_copy(Qz[hh].c(0, 512, (hh % 2) * 64, 64), Qp[hh // 2].c(q0, 512, (hh % 2) * 64, 64)),
                  reads=[Qp[hh // 2].name], writes=[Qz[hh].name])
        for g in range(2):
            items = []
            for hl in range(4):
                hh = 4 * g + hl
                i, ph = hh // 2, hh % 2
                O = C.ps[3 + hl]
                it = AttnItem()
                it.kT = kcT[g].c(0, 128, ph * 64, 64)
                it.qT = Qp[i].c(q0, 512, ph * 64, 64)
                it.extra = [(ident.c(0, 128), cmpb.c(q0, 512))]
                it.N = 512
                it.slope = SLOPE_N[hh]
                it.pv = [(O.c(ql * 97, 97), ql * 128, vcA[g].c(0, 97), ql == 0) for ql in range(4)]
                it.reads = [kcT[g].name, Qp[i].name, 'ident', 'cmpb']
                it.vreads = [vcA[g].name]
                it.obanks = [O.name]
                it.after = normalize(O, 97, hh, 0, c, hl == 0)
                items.append(it)
            attn_stream(fw, C, items)
            fw.op('dve', lambda e: e.tensor_tensor(out=score.c(0, 128), in0=imp.c(0, 128), in1=selA.c(c * 128, 128), op=ALU.mult),
                  reads=['imp', 'selA'], writes=['score'])
            fw.op('dve', lambda e: e.tensor_tensor(out=score.c(0, 128), in0=score.c(0, 128), in1=selB.c(c * 128, 128), op=ALU.add),
                  reads=['score', 'selB'], writes=['score'])
            ps5 = C.ps[7]
            for ql in range(4):
                fw.op('dve', lambda e: e.max(out=sm.c(16, 8), in_=score.c(ql * 32, 32)), reads=['score'], writes=['sm_top'])
                fw.op('dve', lambda e: e.tensor_scalar(out=nm.c(ql * 32, 32), in0=score.c(ql * 32, 32), scalar1=sm.c(23, 1),
                                                       scalar2=None, op0=ALU.is_ge),
                      reads=['score', 'sm_top'], writes=['nm'])
                fw.op('pe', lambda e: e.transpose(ps5.c(ql * 128, 128, 0, 32), nm.c(ql * 32, 32), ident.c(0, 128)),
                      reads=['nm', 'ident'], writes=[ps5.name])
            vcopy(fw, 'dve', negT[g].c(0, 512, 0, 32), ps5.c(0, 512, 0, 32), [ps5.name], [negT[g].name])
            for j in range(4 * c + 4):
                mp_ = C.ps[6 + (j % 2)]
                fw.op('pe', lambda e: e.matmul(mp_.c(0, 512), e01.c(j * 128, 128), negT[g].c(0, 512), start=True, stop=True),
                      reads=['e01', negT[g].name], writes=[mp_.name])
                vcopy(fw, 'dve', Mst.c(j * 512, 512), mp_.c(0, 512), [mp_.name], ['Mst%d' % j])
            items = build_items(c, 1, range(4 * g, 4 * g + 4))
            attn_stream2(fw, C, items)
        attn_stream2(fw, C, build_items(c, 2, range(8)))
        transpose_out(fw, C, och, oT, C.onT, c, 'onT')
    free_to(C.stack, n0, fw)


def phase_B(fw, C):
    nc = fw.nc
    d = C.d
    n0 = len(C.stack)
    Qd = [alloc(nc, C.stack, 'Qd%d' % i, 2048) for i in range(4)]
    Kd = [alloc(nc, C.stack, 'Kd%d' % i, 2048) for i in range(4)]
    Vd = alloc(nc, C.stack, 'Vd', 16 * 512)

    def fm_evac(i, c, ps):
        if i < 4:
            fw.op('dve', lambda e: e.tensor_scalar(out=Qd[i].c(c * 512, 512), in0=ps.c(0, 512), scalar1=C.qscale.c(4 + i, 1),
                                                   scalar2=None, op0=ALU.mult),
                  reads=[ps.name, 'qscale'], writes=[Qd[i].name])
        else:
            vcopy(fw, 'act', Kd[i - 4].c(c * 512, 512), ps.c(0, 512), [ps.name], [Kd[i - 4].name])

    def tm_evac(tt, ps):
        vcopy(fw, 'dve', Vd.c(tt * 512, 512), ps.c(0, 512), [ps.name], ['Vd'])

    proj_phase(fw, C, d['wB_fm'], 8, d['wB_tm'], 512, fm_evac, tm_evac)

    btab = alloc(nc, C.stack, 'btab', 288)
    masks = alloc(nc, C.stack, 'masks', 256)
    C.pbuf = [alloc(nc, C.stack, 'pb%d' % i, 512) for i in range(4)]
    och = alloc(nc, C.stack, 'och', 2048)
    oT = alloc(nc, C.stack, 'oT', 2048)
    o1 = alloc(nc, C.stack, 'o1', 512)
    otv = alloc(nc, C.stack, 'otv', 512)
    otsr = alloc(nc, C.stack, 'otsr', 512)
    onesc = alloc(nc, C.stack, 'onesc', 128)
    Qz = [alloc(nc, C.stack, 'Qz%d' % i, 512) for i in range(8)]
    for i in range(8):
        fw.op('pool', lambda e: e.memset(Qz[i].c(0, 512), 0.0), writes=[Qz[i].name])
    junk = alloc(nc, C.stack, 'junk', 128)
    gbc = alloc(nc, C.stack, 'gbc', 128)
    lt = alloc(nc, C.stack, 'lt', 640)
    neglam = alloc(nc, C.stack, 'neglam', 1)
    sm = alloc(nc, C.stack, 'smd', 32)
    load(fw, btab, 0, d['btab'], 0, 288, 'btab', 'ld_w')
    load(fw, masks, 0, d['masks'], 0, 256, 'masks', 'ld_w')
    fw.op('pool', lambda e: e.memset(onesc.c(0, 128), 1.0), writes=['onesc'])
    ident = C.ident
    load(fw, lt, 0, d['lam'], 0, 256, 'lt', 'ld_w', nrows=1)
    fw.op('pool', lambda e: e.memset(lt.c(400, 128, 0, 1), 1.0), writes=['lt_ones'])
    fw.op('dve', lambda e: e.tensor_tensor(out=lt.ap(256, [(64, 2), (1, 64)], 0, 1), in0=lt.ap(0, [(128, 2), (1, 64)], 0, 1),
                                           in1=lt.ap(64, [(128, 2), (1, 64)], 0, 1), op=ALU.mult), reads=['lt'], writes=['lt_p'])
    fw.op('dve', lambda e: e.tensor_reduce(out=lt.c(384, 2, 0, 1), in_=lt.ap(256, [(64, 2), (1, 64)], 0, 1), axis=AX.X, op=ALU.add),
          reads=['lt_p'], writes=['lt_s'])
    fw.op('act', lambda e: e.activation(out=lt.c(386, 2, 0, 1), in_=lt.c(384, 2, 0, 1), func=AF.Exp), reads=['lt_s'], writes=['lt_e'])
    fw.op('dve', lambda e: e.tensor_tensor(out=lt.c(388, 1, 0, 1), in0=lt.c(386, 1, 0, 1), in1=lt.c(387, 1, 0, 1), op=ALU.subtract),
          reads=['lt_e'], writes=['lt_d'])
    fw.op('dve', lambda e: e.tensor_scalar(out=lt.c(389, 1, 0, 1), in0=lt.c(388, 1, 0, 1), scalar1=LAM_INIT, scalar2=-1.0,
                                           op0=ALU.add, op1=ALU.mult), reads=['lt_d'], writes=['lt_n'])
    ps7 = C.ps[7]
    fw.op('pe', lambda e: e.matmul(ps7.c(0, 1), lt.c(400, 128, 0, 1), lt.c(389, 1, 0, 1), start=True, stop=True),
          reads=['lt_n', 'lt_ones'], writes=[ps7.name])
    vcopy(fw, 'dve', neglam.c(0, 1), ps7.c(0, 1), [ps7.name], ['neglam'])
    fw.dma('sp', lambda q: q.dma_start(out=gbc.c(0, 128), in_=bass.AP(d['normg'], 0, [[0, 128], [1, 128]])), 'ld_w', writes=['gbc0'])
    fw.op('dve', lambda e: e.tensor_scalar(out=gbc.c(0, 128), in0=gbc.c(0, 128), scalar1=1.0 - LAM_INIT, scalar2=None, op0=ALU.mult),
          reads=['gbc0'], writes=['gbc'])

    OTv = [C.ps[3], C.ps[4]]
    OTs = [C.ps[5], C.ps[6]]
    On = C.ps[7]

    def after0(h, mp):
        def f():
            base = 0 if mp == 0 else 8
            vcopy(fw, 'act', otv.c(0, 512), OTv[mp].c(0, 512), [OTv[mp].name], ['otv'])
            vcopy(fw, 'dve', otsr.c(0, 512, 0, 1), OTs[mp].c(0, 512, 0, 1), [OTs[mp].name], ['otsr'])
            for ql in range(4):
                fw.op('pe', lambda e: e.transpose(On.c(ql * 128, 128), otv.c(ql * 128, 128), ident.c(0, 128)),
                      reads=['otv', 'ident'], writes=[On.name], signal=(ql == 3))
            for ql in range(4):
                fw.op('pe', lambda e: e.transpose(OTs[mp].c(ql, 1), otsr.c(ql * 128, 128, 0, 1), ident.c(0, 1, 0, 1)),
                      reads=['otsr', 'ident'], writes=[OTs[mp].name], signal=(ql == 3))
            fw.op('dve', lambda e: e.tensor_scalar(out=sm.c(base, 4), in0=OTs[mp].c(0, 4), scalar1=1e-37,
                                                   scalar2=None, op0=ALU.max), reads=[OTs[mp].name], writes=['smd_rs%d' % mp])
            fw.op('dve', lambda e: e.reciprocal(sm.c(base + 4, 4), sm.c(base, 4)), reads=['smd_rs%d' % mp], writes=['smd_rcp%d' % mp])
            if mp == 0:
                for ql in range(4):
                    fw.op('dve', lambda e: e.tensor_scalar(out=o1.c(ql * 128, 128), in0=On.c(ql * 128, 128),
                                                           scalar1=sm.c(4 + ql, 1), scalar2=None, op0=ALU.mult),
                          reads=[On.name, 'smd_rcp0'], writes=['o1'])
                return
            fw.op('dve', lambda e: e.tensor_scalar(out=sm.c(16, 4), in0=sm.c(12, 4), scalar1=neglam.c(0, 1), scalar2=None, op0=ALU.mult),
                  reads=['smd_rcp1', 'neglam'], writes=['smd_nlr'])
            fw.op('dve', lambda e: e.memset(sm.c(20, 4), 0.0), writes=['smd_ss'])
            for ql in range(4):
                fw.op('dve', lambda e: e.scalar_tensor_tensor(out=o1.c(ql * 128, 128), in0=On.c(ql * 128, 128),
                                                              scalar=sm.c(16 + ql, 1), in1=o1.c(ql * 128, 128), op0=ALU.mult, op1=ALU.add),
                      reads=[On.name, 'smd_nlr', 'o1'], writes=['o1'])
                fw.op('dve', lambda e: e.scalar_tensor_tensor(out=junk.c(0, 128), in0=o1.c(ql * 128, 128), scalar=1.0,
                                                              in1=o1.c(ql * 128, 128), op0=ALU.mult, op1=ALU.mult,
                                                              accum_out=sm.c(20 + ql, 1)),
                      reads=['o1', 'smd_ss'], writes=['junk', 'smd_ss'])
            fw.op('dve', lambda e: e.tensor_scalar(out=sm.c(24, 4), in0=sm.c(20, 4), scalar1=1.0 / 128, scalar2=1e-5,
                                                   op0=ALU.mult, op1=ALU.add), reads=['smd_ss'], writes=['smd_var'])
            fw.op('act', lambda e: e.activation(out=sm.c(24, 4), in_=sm.c(24, 4), func=AF.Sqrt), reads=['smd_var'], writes=['smd_var'])
            fw.op('dve', lambda e: e.reciprocal(sm.c(28, 4), sm.c(24, 4)), reads=['smd_var'], writes=['smd_rstd'])
            for ql in range(4):
                fw.op('dve', lambda e: e.scalar_tensor_tensor(out=och.c(ql * 512 + h * 128, 128), in0=o1.c(ql * 128, 128),
                                                              scalar=sm.c(28 + ql, 1), in1=gbc.c(0, 128), op0=ALU.mult, op1=ALU.mult),
                      reads=['o1', 'smd_rstd', 'gbc'], writes=['och'])
        return f

    for c in range(4):
        q0 = c * 512
        for h in range(4):
            for mp in range(2):
                fw.op('pool', lambda e: e.tensor_copy(Qz[h * 2 + mp].c(0, 512, (h % 2) * 64, 64),
                                                      Qd[mp * 2 + h // 2].c(q0, 512, (h % 2) * 64, 64)),
                      reads=[Qd[mp * 2 + h // 2].name], writes=[Qz[h * 2 + mp].name])
        items = []
        for h in range(4):
            ph = h % 2
            si = 2 * h + 1
            for mp in range(2):
                ti = mp * 2 + h // 2
                first = True
                jhi = 4 * c + 3
                for j in range(0, jhi + 1):
                    tlo = max(j, 4 * c)
                    thi = 4 * c + 3
                    qs_ = tlo * 128
                    N = (thi - tlo + 1) * 128
                    it = AttnItem()
                    it.kT = Kd[ti].c(j * 128, 128)
                    it.qT = Qz[h * 2 + mp].c(qs_ - q0, N)
                    it.reads = [Kd[ti].name, Qz[h * 2 + mp].name]
                    it.N = N
                    it.slope = SLOPE_D[h]
                    it.masks = []
                    for tq in range(tlo, thi + 1):
                        Dd = tq - j
                        off = (tq - tlo) * 128
                        if Dd == 0:
                            it.masks.append((off, masks.c(0, 128)))
                    it.exps = exp_segments(btab, si, qs_, qs_ + N, j)
                    it.mmask = None
                    it.pvs = [(OTv[mp].c(qs_ - q0, N), Vd.c(j * 512 + h * 128, 128), first),
                              (OTs[mp].c(qs_ - q0, N), onesc.c(0, 128), first)]
                    first = False
                    it.vreads = ['Vd', 'onesc']
                    it.obanks = [OTv[mp].name, OTs[mp].name]
                    it.after = after0(h, mp) if j == jhi else None
                    items.append(it)
        attn_stream2(fw, C, items)
        transpose_out(fw, C, och, oT, C.odT, c, 'odT')
    free_to(C.stack, n0, fw)


def layer_norm_tile(fw, C, y, gb, goff, out, sm, tag, eng='pool'):
    for hf in range(2):
        fw.op('dve', lambda e: e.bn_stats(sm.c(hf * 6, 6), y.c(hf * 512, 512)), reads=[y.name], writes=[tag + 'st%d' % hf])
    fw.op('dve', lambda e: e.bn_aggr(sm.c(12, 2), sm.c(0, 12)), reads=[tag + 'st0', tag + 'st1'], writes=[tag + 'mv'])
    fw.op('dve', lambda e: e.tensor_scalar(out=sm.c(14, 1), in0=sm.c(13, 1), scalar1=1e-5, scalar2=None, op0=ALU.add),
          reads=[tag + 'mv'], writes=[tag + 've'])
    fw.op('act', lambda e: e.activation(out=sm.c(15, 1), in_=sm.c(14, 1), func=AF.Sqrt), reads=[tag + 've'], writes=[tag + 'sd'])
    fw.op('dve', lambda e: e.reciprocal(sm.c(16, 1), sm.c(15, 1)), reads=[tag + 'sd'], writes=[tag + 'rstd'])
    fw.op('dve', lambda e: e.tensor_scalar(out=out.c(0, 1024), in0=y.c(0, 1024), scalar1=sm.c(12, 1), scalar2=sm.c(16, 1),
                                           op0=ALU.subtract, op1=ALU.mult), reads=[y.name, tag + 'mv', tag + 'rstd'], writes=[out.name])
    fw.op(eng, lambda e: e.tensor_tensor(out=out.c(0, 1024), in0=out.c(0, 1024), in1=gb.c(goff, 1024), op=ALU.mult),
          reads=[out.name, 'gb'], writes=[out.name])
    fw.op(eng, lambda e: e.tensor_tensor(out=out.c(0, 1024), in0=out.c(0, 1024), in1=gb.c(goff + 1024, 1024), op=ALU.add),
          reads=[out.name, 'gb'], writes=[out.name])


def phase_C(fw, C):
    nc = fw.nc
    d = C.d
    n0 = len(C.stack)
    wbn = alloc(nc, C.stack, 'wbn', 4096)
    wbd = alloc(nc, C.stack, 'wbd', 4096)
    wout = alloc(nc, C.stack, 'wout', 8192)
    Wg = [alloc(nc, C.stack, 'Wg%d' % i, 2048) for i in range(2)]
    onc = alloc(nc, C.stack, 'onc', 2048)
    odc = alloc(nc, C.stack, 'odc', 2048)
    xc = alloc(nc, C.stack, 'xc', 4096)
    mT = alloc(nc, C.stack, 'mT', 4096)
    sg = [alloc(nc, C.stack, 'sg%d' % i, 512) for i in range(2)]
    tmp = alloc(nc, C.stack, 'tmpc', 512)
    gb = alloc(nc, C.stack, 'gb', 2048)
    xt = alloc(nc, C.stack, 'xt', 1024)
    y = alloc(nc, C.stack, 'y', 1024)
    ho = [alloc(nc, C.stack, 'ho%d' % i, 1024) for i in range(2)]
    sm = alloc(nc, C.stack, 'smc', 32)
    for q4 in range(2):
        load(fw, wbn, q4 * 2048, d['wbn'], q4 * 2048, 2048, 'wbn', 'ld_w')
        load(fw, wbd, q4 * 2048, d['wbd'], q4 * 2048, 2048, 'wbd', 'ld_w')
    for q4 in range(4):
        load(fw, wout, q4 * 2048, d['wout'], q4 * 2048, 2048, 'wout', 'ld_w')
    fw.dma('sp', lambda q: q.dma_start(out=gb.c(0, 2048), in_=bass.AP(d['ln'], 0, [[0, 128], [1, 2048]])), 'ld_w', writes=['gb'])
    for c in range(4):
        fw.dma('sp', lambda q: q.dma_start(out=onc.ap(0, [(512, 4), (1, 512)]),
                                           in_=dram2(C.onT, 8192, 0, 128, c * 512, [(2048, 4), (1, 512)])),
               'ld_c', reads=['onT_dram%d' % c], writes=['onc'])
        fw.dma('sp', lambda q: q.dma_start(out=odc.ap(0, [(512, 4), (1, 512)]),
                                           in_=dram2(C.odT, 8192, 0, 128, c * 512, [(2048, 4), (1, 512)])),
               'ld_c', reads=['odT_dram%d' % c], writes=['odc'])
        fw.dma('sp', lambda q: q.dma_start(out=xc.ap(0, [(512, 8), (1, 512)]),
                                           in_=dram2(d['xT'], 8 * 2048, 0, 128, c * 512, [(2048, 8), (1, 512)])),
               'ld_c', writes=['xc'])
        for m in range(8):
            W = Wg[m % 2]
            load(fw, W, 0, d['wC_g'], m * 1024, 1024, W.name, 'ld_g%d' % (m % 2))
            load(fw, W, 1024, d['wC_g'], (8 + m) * 1024, 1024, W.name, 'ld_g%d' % (m % 2))
            pb = [C.ps[(m % 2) * 4 + i] for i in range(4)]
            for gi in range(2):
                for kt in range(8):
                    fw.op('pe', lambda e: e.matmul(pb[gi].c(0, 512), W.c(gi * 1024 + kt * 128, 128), xc.c(kt * 512, 512),
                                                   start=(kt == 0), stop=(kt == 7)),
                          reads=[W.name, 'xc'], writes=[pb[gi].name], signal=(kt == 7))
            for bi, (wb, oc) in enumerate(((wbn, onc), (wbd, odc))):
                for ft in range(4):
                    fw.op('pe', lambda e: e.matmul(pb[2 + bi].c(0, 512), wb.c(ft * 1024 + m * 128, 128), oc.c(ft * 512, 512),
                                                   start=(ft == 0), stop=(ft == 3)),
                          reads=[wb.name, oc.name], writes=[pb[2 + bi].name], signal=(ft == 3))
            for gi in range(2):
                fw.op('act', lambda e: e.activation(out=sg[gi].c(0, 512), in_=pb[gi].c(0, 512), func=AF.Sigmoid),
                      reads=[pb[gi].name], writes=[sg[gi].name])
            fw.op('dve', lambda e: e.tensor_tensor(out=mT.c(m * 512, 512), in0=sg[0].c(0, 512), in1=pb[2].c(0, 512), op=ALU.mult),
                  reads=[sg[0].name, pb[2].name], writes=['mT%d' % m])
            fw.op('dve', lambda e: e.tensor_tensor(out=tmp.c(0, 512), in0=sg[1].c(0, 512), in1=pb[3].c(0, 512), op=ALU.mult),
                  reads=[sg[1].name, pb[3].name], writes=['tmpc'])
            fw.op('pool', lambda e: e.tensor_tensor(out=mT.c(m * 512, 512), in0=mT.c(m * 512, 512), in1=tmp.c(0, 512), op=ALU.add),
                  reads=['mT%d' % m, 'tmpc'], writes=['mT%d' % m])
        for tl in range(4):
            tt = c * 4 + tl
            fw.dma('sp', lambda q: q.dma_start(out=xt.c(0, 1024), in_=dram2(d['xtm'], 1024, tt * 128, 128, 0, [(1, 1024)])),
                   'ld_xt', writes=['xt'])
            for hf in range(2):
                ps = C.ps[hf]
                for ft in range(8):
                    fw.op('pe', lambda e: e.matmul(ps.c(0, 512), mT.c(ft * 512 + tl * 128, 128), wout.c(ft * 1024 + hf * 512, 512),
                                                   start=(ft == 0), stop=(ft == 7)),
                          reads=['mT%d' % ft, 'wout'], writes=[ps.name], signal=(ft == 7))
                fw.op('dve', lambda e: e.scalar_tensor_tensor(out=y.c(hf * 512, 512), in0=xt.c(hf * 512, 512), scalar=ALPHA,
                                                              in1=ps.c(0, 512), op0=ALU.mult, op1=ALU.add),
                      reads=['xt', ps.name], writes=['y'])
            h = ho[tt % 2]
            layer_norm_tile(fw, C, y, gb, 0, h, sm, 'c')
            fw.dma('sp', lambda q: q.dma_start(out=dram2(C.h1, 1024, tt * 128, 128, 0, [(1, 1024)]), in_=h.c(0, 1024)),
                   'st_h1', reads=[h.name], writes=['h1_dram%d' % tt])
    free_to(C.stack, n0, fw)


def phase_D(fw, C):
    nc = fw.nc
    d = C.d
    n0 = len(C.stack)
    ident = C.ident
    eid = alloc(nc, C.stack, 'eid', 2048, dtype=U32)
    gates = alloc(nc, C.stack, 'gates', 2048)
    n1 = len(C.stack)
    wq = alloc(nc, C.stack, 'wq', 8 * 2048)
    sk = alloc(nc, C.stack, 'sk', 256)
    iota = alloc(nc, C.stack, 'iota', 16)
    h1T = alloc(nc, C.stack, 'h1T', 4096)
    qT = alloc(nc, C.stack, 'qT', 16 * 512)
    ht = [alloc(nc, C.stack, 'ht%d' % i, 1024) for i in range(2)]
    scbs = [alloc(nc, C.stack, 'scb%s' % x, 2048) for x in ('A', 'B')]
    work = alloc(nc, C.stack, 'work', 512)
    tv = alloc(nc, C.stack, 'tv', 256)
    ti = alloc(nc, C.stack, 'ti', 256, dtype=U32)
    tif = alloc(nc, C.stack, 'tif', 256)
    cand = alloc(nc, C.stack, 'cand', 2048)
    cv = alloc(nc, C.stack, 'cv', 128)
    cpos = alloc(nc, C.stack, 'cpos', 128, dtype=U32)
    apos = alloc(nc, C.stack, 'apos', 256, dtype=U32)
    abf = alloc(nc, C.stack, 'abf', 256)
    isel = alloc(nc, C.stack, 'isel', 256)
    eidf = alloc(nc, C.stack, 'eidf', 128)
    gsh = alloc(nc, C.stack, 'gsh', 128)
    gs = alloc(nc, C.stack, 'gs', 16)
    for q8 in range(8):
        load(fw, wq, q8 * 2048, d['wq'], q8 * 2048, 2048, 'wq', 'ld_w')
    load(fw, sk, 0, d['sk'], 0, 256, 'sk', 'ld_w')
    load(fw, iota, 0, d['iota16'], 0, 16, 'iota', 'ld_w')
    for c in range(4):
        for tl in range(4):
            tt = c * 4 + tl
            H = ht[tl % 2]
            fw.dma('sp', lambda q: q.dma_start(out=H.c(0, 1024), in_=dram2(C.h1, 1024, tt * 128, 128, 0, [(1, 1024)])),
                   'ld_h%d' % (tl % 2), reads=['h1_dram%d' % tt], writes=[H.name])
            for a in range(2):
                ps = C.ps[a]
                for k4 in range(4):
                    kt = a * 4 + k4
                    fw.op('pe', lambda e: e.transpose(ps.c(k4 * 128, 128), H.c(kt * 128, 128), ident.c(0, 128)),
                          reads=[H.name, 'ident'], writes=[ps.name], signal=(k4 == 3))
                vcopy(fw, 'act', h1T.ap(a * 4 * 512 + tl * 128, [(512, 4), (1, 128)]), ps.ap(0, [(128, 4), (1, 128)]),
                      [ps.name], ['h1T'])
        for n in range(16):
            ps = C.ps[2 + (n % 2)]
            for kt in range(8):
                fw.op('pe', lambda e: e.matmul(ps.c(0, 512), wq.c(kt * 2048 + n * 128, 128), h1T.c(kt * 512, 512),
                                               start=(kt == 0), stop=(kt == 7)),
                      reads=['wq', 'h1T'], writes=[ps.name], signal=(kt == 7))
            vcopy(fw, 'act', qT.c(n * 512, 512), ps.c(0, 512), [ps.name], ['qT'])
        for tl in range(4):
            tt = c * 4 + tl
            for nb in range(4):
                ps = C.ps[4 + (nb % 2)]
                for ni in range(4):
                    n = nb * 4 + ni
                    fw.op('pe', lambda e: e.matmul(ps.c(ni * 128, 128), qT.c(n * 512 + tl * 128, 128), sk.c((n % 2) * 128, 128),
                                                   start=True, stop=True, skip_group_check=True),
                          reads=['qT', 'sk'], writes=[ps.name], signal=(ni == 3))
                vcopy(fw, 'act', scbs[tl % 2].c(nb * 512, 512), ps.c(0, 512), [ps.name], ['scb%d_%d' % (tl % 2, nb)])
            for n0_ in range(0, 16, 2):
                pr = (n0_, n0_ + 1)
                sc2 = [scbs[tl % 2].c(n * 128, 128) for n in pr]
                rk2 = [['scb%d_%d' % (tl % 2, n // 4)] for n in pr]
                wk2 = [work.c(k * 128, 128) for k in range(2)]
                for k, n in enumerate(pr):
                    fw.op('dve', lambda e: e.max(out=tv.c(n * 16, 8), in_=sc2[k]), reads=rk2[k], writes=['tva%d' % k])
                for k, n in enumerate(pr):
                    fw.op('dve', lambda e: e.max_index(out=ti.c(n * 16, 8), in_max=tv.c(n * 16, 8), in_values=sc2[k]),
                          reads=rk2[k] + ['tva%d' % k], writes=['ti%d' % k])
                for k, n in enumerate(pr):
                    fw.op('dve', lambda e: e.match_replace(out=wk2[k], in_to_replace=tv.c(n * 16, 8), in_values=sc2[k], imm_value=-1e30),
                          reads=rk2[k] + ['tva%d' % k], writes=['work%d' % k])
                for k, n in enumerate(pr):
                    fw.op('dve', lambda e: e.max(out=tv.c(n * 16 + 8, 8), in_=wk2[k]), reads=['work%d' % k], writes=['tvb%d' % k])
                for k, n in enumerate(pr):
                    fw.op('dve', lambda e: e.max_index(out=ti.c(n * 16 + 8, 8), in_max=tv.c(n * 16 + 8, 8), in_values=wk2[k]),
                          reads=['work%d' % k, 'tvb%d' % k], writes=['ti%d' % k])
            fw.op('dve', lambda e: e.tensor_copy(tif.c(0, 256), ti.c(0, 256)), reads=['ti0', 'ti1'], writes=['tif'])
            fw.op('dve', lambda e: e.tensor_tensor(out=cand.ap(0, [(256, 8), (16, 16), (1, 16)]),
                                                   in0=tv.ap(0, [(32, 8), (1, 16), (0, 16)]),
                                                   in1=tv.ap(16, [(32, 8), (0, 16), (1, 16)]), op=ALU.add),
                  reads=['tva0', 'tva1', 'tvb0', 'tvb1'], writes=['cand'])
            for h0_ in range(0, 8, 2):
                pr = (h0_, h0_ + 1)
                cd2 = [cand.c(h * 256, 256) for h in pr]
                wk2 = [work.c(k * 256, 256) for k in range(2)]
                for k, h in enumerate(pr):
                    fw.op('dve', lambda e: e.max(out=cv.c(h * 16, 8), in_=cd2[k]), reads=['cand'], writes=['cva%d' % k])
                for k, h in enumerate(pr):
                    fw.op('dve', lambda e: e.max_index(out=cpos.c(h * 16, 8), in_max=cv.c(h * 16, 8), in_values=cd2[k]),
                          reads=['cand', 'cva%d' % k], writes=['cpos%d' % k])
                for k, h in enumerate(pr):
                    fw.op('dve', lambda e: e.match_replace(out=wk2[k], in_to_replace=cv.c(h * 16, 8), in_values=cd2[k], imm_value=-1e30),
                          reads=['cand', 'cva%d' % k], writes=['work%d' % k])
                for k, h in enumerate(pr):
                    fw.op('dve', lambda e: e.max(out=cv.c(h * 16 + 8, 8), in_=wk2[k]), reads=['work%d' % k], writes=['cvb%d' % k])
                for k, h in enumerate(pr):
                    fw.op('dve', lambda e: e.max_index(out=cpos.c(h * 16 + 8, 8), in_max=cv.c(h * 16 + 8, 8), in_values=wk2[k]),
                          reads=['work%d' % k, 'cvb%d' % k], writes=['cpos%d' % k])
            v3 = [(16, 8), (1, 16)]
            fw.op('dve', lambda e: e.tensor_tensor(out=gsh.ap(0, v3), in0=cv.ap(0, v3), in1=cv.ap(0, [(16, 8), (0, 16)]), op=ALU.subtract),
                  reads=['cva0', 'cva1', 'cvb0', 'cvb1'], writes=['gsh'])
            fw.op('act', lambda e: e.activation(out=gsh.c(0, 128), in_=gsh.c(0, 128), func=AF.Exp), reads=['gsh'], writes=['gsh'])
            fw.op('dve', lambda e: e.tensor_reduce(out=gs.c(0, 8), in_=gsh.ap(0, v3), axis=AX.X, op=ALU.add), reads=['gsh'], writes=['gs'])
            fw.op('dve', lambda e: e.reciprocal(gs.c(8, 8), gs.c(0, 8)), reads=['gs'], writes=['gr'])
            fw.op('dve', lambda e: e.tensor_tensor(out=gates.ap(tt * 128, v3), in0=gsh.ap(0, v3), in1=gs.ap(8, [(1, 8), (0, 16)]), op=ALU.mult),
                  reads=['gsh', 'gr'], writes=['gates'])
            fw.op('dve', lambda e: e.tensor_scalar(out=apos.c(0, 128), in0=cpos.c(0, 128), scalar1=4, scalar2=None,
                                                   op0=ALU.logical_shift_right), reads=['cpos0', 'cpos1'], writes=['apos'])
            fw.op('dve', lambda e: e.tensor_scalar(out=apos.c(128, 128), in0=cpos.c(0, 128), scalar1=15, scalar2=None,
                                                   op0=ALU.bitwise_and), reads=['cpos0', 'cpos1'], writes=['bpos'])
            fw.op('dve', lambda e: e.tensor_copy(abf.c(0, 256), apos.c(0, 256)), reads=['apos', 'bpos'], writes=['abf'])
            v4 = [(256, 8), (16, 16), (1, 16)]
            for ab in range(2):
                fw.op('dve', lambda e: e.tensor_tensor(out=cand.ap(0, v4), in0=abf.ap(ab * 128, [(16, 8), (1, 16), (0, 16)]),
                                                       in1=iota.ap(0, [(0, 8), (0, 16), (1, 16)]), op=ALU.is_equal),
                      reads=['abf', 'iota', 'cand'], writes=['cand'])
                fw.op('dve', lambda e: e.tensor_tensor(out=cand.ap(0, v4), in0=cand.ap(0, v4),
                                                       in1=tif.ap(ab * 16, [(32, 8), (0, 16), (1, 16)]), op=ALU.mult),
                      reads=['cand', 'tif'], writes=['cand'])
                fw.op('dve', lambda e: e.tensor_reduce(out=isel.ap(ab * 128, v3), in_=cand.ap(0, v4), axis=AX.X, op=ALU.add),
                      reads=['cand'], writes=['isel%d' % ab])
            fw.op('dve', lambda e: e.scalar_tensor_tensor(out=eidf.c(0, 128), in0=isel.c(0, 128), scalar=128.0, in1=isel.c(128, 128),
                                                          op0=ALU.mult, op1=ALU.add), reads=['isel0', 'isel1'], writes=['eidf'])
            fw.op('dve', lambda e: e.tensor_copy(eid.c(tt * 128, 128), eidf.c(0, 128)), reads=['eidf'], writes=['eid'])
    dump(fw, C, gates, 0, 2048, ['gates'], 10000)
    dump(fw, C, eidf, 0, 128, ['eidf'], 12048)
    free_to(C.stack, n1, fw)
    NR = 12
    uv = [alloc(nc, C.stack, 'uv%d' % i, 2048) for i in range(NR)]
    dg = [alloc(nc, C.stack, 'dg%d' % i, 128) for i in range(4)]
    wpg = alloc(nc, C.stack, 'wpg', 8192)
    wpp = alloc(nc, C.stack, 'wpp', 2048)
    gb = alloc(nc, C.stack, 'gb', 2048)
    htd = [alloc(nc, C.stack, 'htd%d' % i, 1024) for i in range(2)]
    pTt = alloc(nc, C.stack, 'pTt', 256)
    y = alloc(nc, C.stack, 'y', 1024)
    h2 = alloc(nc, C.stack, 'h2', 1024)
    h2T = alloc(nc, C.stack, 'h2T', 1024)
    dots = alloc(nc, C.stack, 'dots', 264)
    accS = alloc(nc, C.stack, 'accS', 1024)
    sgp = alloc(nc, C.stack, 'sgp', 512)
    tmp = alloc(nc, C.stack, 'tmpd', 512)
    outt = [alloc(nc, C.stack, 'outt%d' % i, 1024) for i in range(1)]
    sm = alloc(nc, C.stack, 'smdd', 32)
    for q4 in range(4):
        load(fw, wpg, q4 * 2048, d['wpg'], q4 * 2048, 2048, 'wpg', 'ld_w')
    load(fw, wpp, 0, d['wpp'], 0, 2048, 'wpp', 'ld_w')
    fw.dma('sp', lambda q: q.dma_start(out=gb.c(0, 2048), in_=bass.AP(d['ln'], 2048, [[0, 128], [1, 2048]])), 'ld_w', writes=['gb'])
    accb = [C.ps[6], C.ps[7]]
    LA = NR - 2
    total = NT * 128
    DVE_EVERY = 4
    LAST_PE = 126

    def gather(n):
        tt, ex = divmod(n, 128)
        UV = uv[n % NR]
        fw.dma('pool', lambda q: q.indirect_dma_start(out=UV.c(0, 2048), out_offset=None, in_=d['puv'].ap(),
                                                      in_offset=bass.IndirectOffsetOnAxis(ap=eid.c(tt * 128 + ex, 1), axis=0)),
               'g_' + UV.name, reads=['eid'], writes=[UV.name])

    def front(n):
        tt, ex = divmod(n, 128)
        UV = uv[n % NR]
        H = htd[tt % 2]
        if ex == 0:
            fw.dma('sp', lambda q: q.dma_start(out=H.c(0, 1024), in_=dram2(C.h1, 1024, tt * 128, 128, 0, [(1, 1024)])),
                   'ld_h%d' % (tt % 2), reads=['h1_dram%d' % tt], writes=[H.name])
        fw.op('dve', lambda e: e.scalar_tensor_tensor(out=UV.c(0, 1024), in0=UV.c(0, 1024), scalar=1.0, in1=H.c(0, 1024),
                                                      op0=ALU.mult, op1=ALU.mult, accum_out=dots.c(ex, 1)),
              reads=[UV.name, H.name], writes=[UV.name + 'u', 'dot%d' % (n % 4)])
        fw.op('act', lambda e: e.activation(out=dots.c(128 + ex, 1), in_=dots.c(ex, 1), func=AF.Gelu),
              reads=['dot%d' % (n % 4)], writes=['gel%d' % (n % 4)])
        fw.op('act', lambda e: e.activation(out=dots.c(256 + (n % 4), 1), in_=dots.c(128 + ex, 1), func=AF.Copy,
                                            scale=gates.c(tt * 128 + ex, 1)),
              reads=['gel%d' % (n % 4), 'gates'], writes=['gav%d' % (n % 4)])
        if ex % DVE_EVERY != DVE_EVERY - 1:
            DG = dg[n % 4]
            fw.op('act', lambda e: e.activation(out=DG.c(0, 128), in_=ident.c(0, 128), func=AF.Copy, scale=dots.c(256 + (n % 4), 1)),
                  reads=['ident', 'gav%d' % (n % 4)], writes=[DG.name])

    def back(n):
        tt, ex = divmod(n, 128)
        UV = uv[n % NR]
        if ex % DVE_EVERY == DVE_EVERY - 1:
            if ex == DVE_EVERY - 1:
                fw.op('dve', lambda e: e.tensor_scalar(out=accS.c(0, 1024), in0=UV.c(1024, 1024), scalar1=dots.c(256 + (n % 4), 1),
                                                       scalar2=None, op0=ALU.mult),
                      reads=[UV.name, 'gav%d' % (n % 4)], writes=['accS'])
            else:
                fw.op('dve', lambda e: e.scalar_tensor_tensor(out=accS.c(0, 1024), in0=UV.c(1024, 1024), scalar=dots.c(256 + (n % 4), 1),
                                                              in1=accS.c(0, 1024), op0=ALU.mult, op1=ALU.add),
                      reads=[UV.name, 'gav%d' % (n % 4), 'accS'], writes=['accS'])
            return
        DG = dg[n % 4]
        for hf in range(2):
            fw.op('pe', lambda e: e.matmul(accb[hf].c(0, 512), DG.c(0, 128), UV.c(1024 + hf * 512, 512),
                                           start=(ex == 0), stop=(ex == LAST_PE)),
                  reads=[DG.name, UV.name], writes=[accb[hf].name], signal=(hf == 1))

    for n in range(LA):
        gather(n)
    for n in range(total + 1):
        if n + LA < total:
            gather(n + LA)
        if n < total:
            front(n)
        if n >= 1:
            back(n - 1)
        if n < 1 or (n - 1) % 128 != 127:
            continue
        tt = (n - 1) // 128
        H = htd[tt % 2]
        fw.dma('sp', lambda q: q.dma_start(out=pTt.ap(0, [(128, 2), (1, 128)]),
                                           in_=dram2(d['pT'], 4096, 0, 128, tt * 128, [(2048, 2), (1, 128)])),
               'ld_p', writes=['pTt'])
        for hf in range(2):
            fw.op('dve', lambda e: e.scalar_tensor_tensor(out=y.c(hf * 512, 512), in0=H.c(hf * 512, 512), scalar=ALPHA,
                                                          in1=accb[hf].c(0, 512), op0=ALU.mult, op1=ALU.add),
                  reads=[H.name, accb[hf].name], writes=['y'])
        fw.op('dve', lambda e: e.tensor_tensor(out=y.c(0, 1024), in0=y.c(0, 1024), in1=accS.c(0, 1024), op=ALU.add),
              reads=['y', 'accS'], writes=['y'])
        layer_norm_tile(fw, C, y, gb, 0, h2, sm, 'd', eng='dve')
        for a in range(2):
            ps = C.ps[a]
            for k4 in range(4):
                kt = a * 4 + k4
                fw.op('pe', lambda e: e.transpose(ps.c(k4 * 128, 128), h2.c(kt * 128, 128), ident.c(0, 128)),
                      reads=['h2', 'ident'], writes=[ps.name], signal=(k4 == 3))
            vcopy(fw, 'act', h2T.c(a * 512, 512), ps.c(0, 512), [ps.name], ['h2T%d' % a])
        O = outt[0]
        for hf in range(2):
            psg = C.ps[2 + hf]
            psp = C.ps[4 + hf]
            for kt in range(8):
                fw.op('pe', lambda e: e.matmul(psg.c(0, 512), h2T.c(kt * 128, 128), wpg.c(kt * 1024 + hf * 512, 512),
                                               start=(kt == 0), stop=(kt == 7)),
                      reads=['h2T0', 'h2T1', 'wpg'], writes=[psg.name], signal=(kt == 7))
            for kt in range(2):
                fw.op('pe', lambda e: e.matmul(psp.c(0, 512), pTt.c(kt * 128, 128), wpp.c(kt * 1024 + hf * 512, 512),
                                               start=(kt == 0), stop=(kt == 1)),
                      reads=['pTt', 'wpp'], writes=[psp.name], signal=(kt == 1))
            fw.op('act', lambda e: e.activation(out=sgp.c(0, 512), in_=psg.c(0, 512), func=AF.Sigmoid), reads=[psg.name], writes=['sgp'])
            fw.op('dve', lambda e: e.tensor_tensor(out=tmp.c(0, 512), in0=sgp.c(0, 512), in1=psp.c(0, 512), op=ALU.mult),
                  reads=['sgp', psp.name], writes=['tmpd'])
            fw.op('dve', lambda e: e.tensor_tensor(out=O.c(hf * 512, 512), in0=tmp.c(0, 512), in1=h2.c(hf * 512, 512), op=ALU.add),
                  reads=['tmpd', 'h2'], writes=[O.name])
        fw.dma('sp', lambda q: q.dma_start(out=dram2(C.out, 1024, tt * 128, 128, 0, [(1, 1024)]), in_=O.c(0, 1024)),
               'st_out', reads=[O.name])
    free_to(C.stack, n0, fw)


def build(stage=99, debug=False):
    nc = bass.Bass("TRN2", target_bir_lowering=False)
    C = Ctx()
    C.d = {k: nc.dram_tensor(k, shp, F32, kind="ExternalInput") for k, shp in IN_SHAPES.items()}
    dbg_kind = "ExternalOutput" if debug else "Internal"
    C.onT = nc.dram_tensor("onT", [128, 8192], F32, kind=dbg_kind)
    C.odT = nc.dram_tensor("odT", [128, 8192], F32, kind=dbg_kind)
    C.h1 = nc.dram_tensor("h1", [2048, 1024], F32, kind=dbg_kind)
    C.out = nc.dram_tensor("out", [2048, 1024], F32, kind="ExternalOutput")
    fw = FW(nc)
    C.dbg = nc.dram_tensor("dbg", [128, 32768], F32, kind="ExternalOutput") if debug else None
    C.stack = []
    C.ps = [alloc(nc, C.stack, 'ps%d' % i, 512, psum=True) for i in range(8)]
    C.ident = alloc(nc, C.stack, 'ident', 128)
    C.qscale = alloc(nc, C.stack, 'qscale', 8)
    load(fw, C.ident, 0, C.d['ident'], 0, 128, 'ident', 'ld_w')
    load(fw, C.qscale, 0, C.d['qscale'], 0, 8, 'qscale', 'ld_w')
    phase_A(fw, C)
    if stage >= 2:
        phase_B(fw, C)
    if stage >= 3:
        phase_C(fw, C)
    if stage >= 4:
        phase_D(fw, C)
    for s in list(fw.sems):
        if fw.cnt[s] > 0 and s != 'E_sp':
            fw.eng['sp'].wait_ge(fw.sems[s], fw.cnt[s])
    free_to(C.stack, 0)
    fw.close()
    print("instructions:", fw.n_inst)
    return nc


def kernel(**inputs):
    inp = {k: np.asarray(v) for k, v in inputs.items()}
    nc = build(stage=99, debug=False)
    in_maps = []
    for b in range(8):
        dmap = pack_inputs(inp, b)
        in_maps.append({k: np.ascontiguousarray(dmap[k], dtype=np.float32) for k in IN_SHAPES})
    res = run_bass_kernel_spmd(nc, in_maps, core_ids=list(range(8)))
    out = np.stack([np.asarray(res.results[b]['out'], dtype=np.float32) for b in range(8)], axis=0)
    return out
```

```python
import numpy as np
import concourse.bass as bass
import concourse.mybir as mybir
from concourse.bass_utils import run_bass_kernel_spmd

F32 = mybir.dt.float32
U32 = mybir.dt.uint32
AF = mybir.ActivationFunctionType
ALU = mybir.AluOpType
AX = mybir.AxisListType

S = 2048
D = 1024
NT = 16
BIGS = 1.0e5
SLOPE_N = [2.0 ** (-(h + 1)) for h in range(8)]
SLOPE_D = [2.0 ** (-2 * (h + 1)) for h in range(4)]
LAM_INIT = 0.2
ALPHA = 2.0 ** 0.25


class FW:
    def __init__(self, nc):
        self.nc = nc
        self.eng = {'pe': nc.tensor, 'dve': nc.vector, 'act': nc.scalar,
                    'pool': nc.gpsimd, 'sp': nc.sync}
        self.sems = {}
        self.cnt = {}
        self.waited = {e: {} for e in self.eng}
        self.state = {}
        self._cms = []
        for e in self.eng:
            self._mksem('E_' + e)
        self.n_inst = 0
        self._rr = {}

    def _mksem(self, name):
        cm = self.nc.semaphore(name)
        s = cm.__enter__()
        self._cms.append(cm)
        self.sems[name] = s
        self.cnt[name] = 0
        return s

    def close(self):
        for cm in reversed(self._cms):
            cm.__exit__(None, None, None)

    def _deps(self, reads, writes):
        deps = {}

        def add(d):
            if d is None:
                return
            s, v = d
            if deps.get(s, 0) < v:
                deps[s] = v
        for k in reads:
            st = self.state.get(k)
            if st:
                add(st['w'])
        for k in writes:
            st = self.state.get(k)
            if st:
                add(st['w'])
                for s, v in st['r'].items():
                    add((s, v))
        return deps

    def _wait(self, e, deps):
        own = 'E_' + e
        for s, v in deps.items():
            if e == 'pe' and s == own:
                continue
            if self.waited[e].get(s, 0) >= v:
                continue
            self.eng[e].wait_ge(self.sems[s], v)
            self.waited[e][s] = v

    def _record(self, reads, writes, tok):
        for k in reads:
            st = self.state.setdefault(k, {'w': None, 'r': {}})
            s, v = tok
            if st['r'].get(s, 0) < v:
                st['r'][s] = v
        for k in writes:
            self.state[k] = {'w': tok, 'r': {}}

    def op(self, e, fn, reads=(), writes=(), signal=True):
        deps = self._deps(reads, writes)
        self._wait(e, deps)
        inst = fn(self.eng[e])
        own = 'E_' + e
        if signal:
            self.cnt[own] += 1
            inst.then_inc(self.sems[own], 1)
            tok = (own, self.cnt[own])
        else:
            tok = (own, self.cnt[own] + 1)
        self._record(reads, writes, tok)
        self.n_inst += 1
        return inst

    POOLS = {'ld_w': 8}

    def dma(self, q, fn, sem, reads=(), writes=()):
        if sem in self.POOLS:
            n = self._rr.get(sem, 0)
            self._rr[sem] = n + 1
            sem = '%s_%d' % (sem, n % self.POOLS[sem])
        if sem not in self.sems:
            self._mksem(sem)
        deps = self._deps(reads, writes)
        if self.cnt[sem] > 0:
            deps[sem] = max(deps.get(sem, 0), self.cnt[sem])
        self._wait(q, deps)
        inst = fn(self.eng[q])
        self.cnt[sem] += 16
        inst.then_inc(self.sems[sem], 16)
        self._record(reads, writes, (sem, self.cnt[sem]))
        self.n_inst += 1
        return inst

    def wait_all(self, e, keys):
        self._wait(e, self._deps(keys, keys))

    def barrier(self):
        for e in self.eng:
            own = 'E_' + e
            for s, v in self.cnt.items():
                if v == 0 or (s == own and e in ('pe', 'sp')):
                    continue
                if self.waited[e].get(s, 0) >= v:
                    continue
                self.eng[e].wait_ge(self.sems[s], v)
                self.waited[e][s] = v


class T:
    def __init__(self, t, ncols, name):
        self.t, self.n, self.name = t, ncols, name

    def ap(self, col=0, dims=None, p0=0, np_=128):
        if dims is None:
            dims = [(1, self.n - col)]
        return bass.AP(self.t, p0 * self.n + col, [[self.n, np_]] + [[s, n] for s, n in dims])

    def c(self, col, n, p0=0, np_=128):
        return self.ap(col, [(1, n)], p0, np_)


def dram2(t, rowlen, row0, nrows, col, dims):
    return bass.AP(t, row0 * rowlen + col, [[rowlen, nrows]] + [[s, n] for s, n in dims])


def pack_fm(w, cols):
    sub = w[:, cols]
    M = sub.shape[1]
    return np.ascontiguousarray(sub.reshape(8, 128, M).transpose(1, 0, 2).reshape(128, 8 * M))


def pack_rows(w, kt):
    M = w.shape[1]
    return np.ascontiguousarray(w.reshape(kt, 128, M).transpose(1, 0, 2).reshape(128, kt * M))


def host_consts():
    c = {}
    c['ident'] = np.eye(128, dtype=np.float32)
    kl = np.arange(128)[:, None]
    m = np.arange(2048)[None, :]
    v = m - kl
    c['tstrip'] = np.where(v >= 0, -v, -BIGS).astype(np.float32)
    m2 = np.arange(1024)[None, :]
    v2 = m2 - kl
    c['wstrip'] = np.where((v2 >= 0) & (v2 < 512), -v2, -BIGS).astype(np.float32)
    cend = 16 * np.arange(128)[:, None] + 31
    vq = np.arange(2048)[None, :] - cend
    cb = np.where(vq >= 0, -vq, -BIGS).astype(np.float32)
    cb[127, :] = -BIGS
    c['cmpbias'] = cb
    e = np.zeros((32, 2048), np.float32)
    e[np.arange(2048) // 64, np.arange(2048)] = BIGS
    c['eblk'] = e
    t = (np.arange(16)[None, :, None] * 128 + np.arange(128)[:, None, None])
    jb = np.arange(32)[None, None, :]
    cur = t // 64
    allowed = jb <= cur
    forced = (jb == 0) | (jb == cur) | (jb == cur - 1)
    A = (allowed & ~forced).astype(np.float32)
    B = np.where(forced, 1.0e4, np.where(allowed, 0.0, -1.0e4)).astype(np.float32)
    c['selA'] = np.ascontiguousarray(A.reshape(128, 512))
    c['selB'] = np.ascontiguousarray(B.reshape(128, 512))
    c0 = np.arange(128)[:, None] * 16
    s0 = np.arange(32)[None, :] * 64
    ov = np.minimum(c0 + 32, s0 + 64) - np.maximum(c0, s0)
    sw = (np.clip(ov, 0, None) / 32.0).astype(np.float32)
    sw[127, :] = 0
    c['selw'] = sw
    qs = np.zeros((128, 8), np.float32)
    for i in range(4):
        for p in range(128):
            qs[p, i] = 0.125 / SLOPE_N[2 * i + p // 64]
            qs[p, 4 + i] = 0.125 / SLOPE_D[2 * (i % 2) + p // 64]
    c['qscale'] = qs
    bt = np.zeros((128, 8 * 36), np.float32)
    for si in range(8):
        for mi in range(36):
            bt[:, si * 36 + mi] = (2.0 ** (-(si + 1))) * (np.arange(128) - 64 * (mi - 2))
    c['btab'] = bt
    ql_ = np.arange(128)[None, :]
    tri = np.where(kl > ql_, -BIGS, 0.0).astype(np.float32)
    anti = np.where(ql_ >= kl, -BIGS, 0.0).astype(np.float32)
    tri01 = (kl <= ql_).astype(np.float32)
    anti01 = (ql_ < kl).astype(np.float32)
    c['masks'] = np.ascontiguousarray(np.concatenate([tri, anti, tri01, anti01], axis=1))
    e01 = np.zeros((128, 2048), np.float32)
    e01[:32] = (e > 0)
    c['e01'] = e01
    c['iota16'] = np.tile(np.arange(16, dtype=np.float32)[None, :], (128, 1))
    return c


def pack_inputs(inp, b):
    w_in = inp['w_in'][0]
    x = inp['x'][b]
    d = {}
    d['xT'] = pack_rows(np.ascontiguousarray(x.T), 8)
    d['xtm'] = np.ascontiguousarray(x)
    kv0 = 512
    tiles = [np.arange(128 * i, 128 * i + 128) for i in range(4)]
    tiles.append(kv0 + np.arange(128))
    tiles.append(kv0 + 128 + np.arange(128))
    for j in (2, 4):
        for g in range(2):
            cg = kv0 + j * 128 + g * 64 + np.arange(64)
            tiles.append(np.concatenate([cg, cg]))
    d['wA_fm'] = np.concatenate([pack_fm(w_in, t) for t in tiles], axis=1)
    tmcols = np.concatenate([kv0 + 3 * 128 + np.arange(128), kv0 + 5 * 128 + np.arange(128),
                             kv0 + 768 + np.arange(24)])
    d['wA_tm'] = pack_fm(w_in, tmcols)
    for nm, w1, w2, pos in (('k', 'cmp_k_w1', 'cmp_k_w2', 'cmp_pos_k'), ('v', 'cmp_v_w1', 'cmp_v_w2', 'cmp_pos_v')):
        W1 = inp[w1][0].reshape(32, 64, 256)
        W1p = W1.transpose(1, 0, 2).reshape(64, 32 * 256)
        d['w1' + nm] = np.ascontiguousarray(np.concatenate([W1p, W1p], axis=0))
        pT = inp[pos][0].T
        d['pos' + nm] = np.ascontiguousarray(np.concatenate([pT, pT], axis=0))
    w2k = inp['cmp_k_w2'][0]
    d['w2k'] = pack_rows(np.concatenate([w2k, w2k], axis=1), 2)
    d['w2v'] = pack_rows(inp['cmp_v_w2'][0], 2)
    q0 = 512 + 768 + 24
    k0 = q0 + 512
    v0 = k0 + 512
    tb = []
    for base in (q0, k0):
        for mp in range(2):
            for pr in range(2):
                tb.append(base + mp * 256 + pr * 128 + np.arange(128))
    d['wB_fm'] = np.concatenate([pack_fm(w_in, t) for t in tb], axis=1)
    d['wB_tm'] = pack_fm(w_in, v0 + np.arange(512))
    d['lam'] = np.stack([inp['lam_q1'][0], inp['lam_k1'][0], inp['lam_q2'][0], inp['lam_k2'][0]]).reshape(1, 256)
    d['normg'] = inp['diff_norm_g'][0].reshape(1, 128)
    ga0 = v0 + 512
    d['wC_g'] = np.concatenate([pack_fm(w_in, ga0 + 128 * i + np.arange(128)) for i in range(16)], axis=1)
    d['wbn'] = pack_rows(inp['w_branch_nsa'][0], 4)
    d['wbd'] = pack_rows(inp['w_branch_diff'][0], 4)
    d['wout'] = pack_rows(inp['w_out'][0], 8)
    d['ln'] = np.stack([inp['ln1_g'][0], inp['ln1_b'][0], inp['ln2_g'][0], inp['ln2_b'][0]]).reshape(1, 4096)
    d['wq'] = pack_rows(inp['peer_wq'][0], 8)
    d['sk'] = np.ascontiguousarray(np.concatenate([inp['peer_subkeys1'][0].T, inp['peer_subkeys2'][0].T], axis=1))
    d['puv'] = np.concatenate([inp['peer_u'][0], inp['peer_v'][0]], axis=1)
    d['pT'] = pack_rows(np.ascontiguousarray(inp['p'][0, b].T), 2)
    d['wpg'] = pack_rows(inp['ple_w_gate'][0], 8)
    d['wpp'] = pack_rows(inp['ple_w_proj'][0], 2)
    d.update(host_consts())
    return d


IN_SHAPES = {
    'xT': [128, 8 * 2048], 'xtm': [2048, 1024],
    'wA_fm': [128, 10 * 1024], 'wA_tm': [128, 8 * 280],
    'w1k': [128, 32 * 256], 'w1v': [128, 32 * 256], 'posk': [128, 32], 'posv': [128, 32],
    'w2k': [128, 256], 'w2v': [128, 128],
    'wB_fm': [128, 8 * 1024], 'wB_tm': [128, 8 * 512], 'lam': [1, 256], 'normg': [1, 128],
    'wC_g': [128, 16 * 1024], 'wbn': [128, 4096], 'wbd': [128, 4096], 'wout': [128, 8192],
    'ln': [1, 4096], 'wq': [128, 8 * 2048], 'sk': [128, 256],
    'puv': [16384, 2048], 'pT': [128, 2 * 2048],
    'wpg': [128, 8192], 'wpp': [128, 2048],
    'ident': [128, 128], 'tstrip': [128, 2048], 'wstrip': [128, 1024], 'cmpbias': [128, 2048],
    'eblk': [32, 2048], 'selA': [128, 512], 'selB': [128, 512], 'selw': [128, 32], 'qscale': [128, 8], 'iota16': [128, 16], 'btab': [128, 288], 'masks': [128, 512], 'e01': [128, 2048],
}


class Ctx:
    pass


def dump(fw, C, t, col, n, keys, dcol, p0=0, np_=128):
    if C.dbg is None:
        return
    fw.dma('sp', lambda q: q.dma_start(out=dram2(C.dbg, 32768, p0, np_, dcol, [(1, n)]), in_=t.c(col, n, p0, np_)),
           'st_dbg', reads=keys)


_UID = [0]


def alloc(nc, stack, name, ncols, dtype=F32, psum=False, nparts=128):
    _UID[0] += 1
    cm = (nc.psum_tensor if psum else nc.sbuf_tensor)('s%d_%s' % (_UID[0], name), [nparts, ncols], dtype)
    t = cm.__enter__()
    stack.append(cm)
    return T(t, ncols, name)


def free_to(stack, n, fw=None):
    while len(stack) > n:
        stack.pop().__exit__(None, None, None)
    if fw is not None:
        fw.barrier()


def load(fw, dst, dcol, src, scol, ncols, key, sem, nrows=128, p0=0):
    rowlen = src.shape[1]
    fw.dma('sp', lambda q: q.dma_start(out=dst.c(dcol, ncols, p0, nrows),
                                       in_=dram2(src, rowlen, 0, nrows, scol, [(1, ncols)])),
           sem, writes=[key])


def proj_phase(fw, C, wfm, nfm, wtm, ntm, fm_evac, tm_evac):
    nc = fw.nc
    n0 = len(C.stack)
    Wf = alloc(nc, C.stack, 'Wf', nfm * 1024)
    Wt = alloc(nc, C.stack, 'Wt', 8 * ntm)
    xb = [alloc(nc, C.stack, 'xb%d' % i, 8 * 512) for i in range(2)]
    for i in range(nfm):
        load(fw, Wf, i * 1024, wfm, i * 1024, 1024, 'Wf', 'ld_w')
    load(fw, Wt, 0, wtm, 0, 8 * ntm, 'Wt', 'ld_w')
    for c in range(4):
        X = xb[c % 2]
        fw.dma('sp', lambda q: q.dma_start(out=X.ap(0, [(512, 8), (1, 512)]),
                                           in_=dram2(C.d['xT'], 8 * 2048, 0, 128, c * 512, [(2048, 8), (1, 512)])),
               'ld_x%d' % (c % 2), writes=[X.name])
        for i in range(nfm):
            ps = C.ps[6 + (i % 2)]
            for kt in range(8):
                fw.op('pe', lambda e: e.matmul(ps.c(0, 512), Wf.c(i * 1024 + kt * 128, 128), X.c(kt * 512, 512),
                                               start=(kt == 0), stop=(kt == 7)),
                      reads=['Wf', X.name], writes=[ps.name], signal=(kt == 7))
            fm_evac(i, c, ps)
        for tl in range(4):
            ps = C.ps[6 + (tl % 2)]
            for kt in range(8):
                fw.op('pe', lambda e: e.matmul(ps.c(0, ntm), X.c(kt * 512 + tl * 128, 128), Wt.c(kt * ntm, ntm),
                                               start=(kt == 0), stop=(kt == 7)),
                      reads=['Wt', X.name], writes=[ps.name], signal=(kt == 7))
            tm_evac(c * 4 + tl, ps)
    free_to(C.stack, n0, fw)


class AttnItem:
    pass


def attn_stream(fw, C, items):
    prev = None
    n = 0
    for it in items:
        sb = C.ps[n % 3]
        pb = C.pbuf[n % 3]
        mms = [(it.kT, it.qT)] + it.extra
        for mi, (l, r) in enumerate(mms):
            fw.op('pe', lambda e: e.matmul(sb.c(0, it.N), l, r, start=(mi == 0), stop=(mi == len(mms) - 1)),
                  reads=it.reads, writes=[sb.name], signal=(mi == len(mms) - 1))
        fw.op('act', lambda e: e.activation(out=pb.c(0, it.N), in_=sb.c(0, it.N), func=AF.Exp, scale=it.slope),
              reads=[sb.name], writes=[pb.name])
        it.pb = pb
        if prev is not None:
            emit_pv(fw, C, prev)
        prev = it
        n += 1
    if prev is not None:
        emit_pv(fw, C, prev)


def exp_segments(btab, si, qs, qe, j):
    w = 128 if si == 0 else (256 if si == 1 else 512)
    segs = []
    s0 = (qs // w) * w
    while s0 < qe:
        a = max(s0, qs)
        b = min(s0 + w, qe)
        m = (s0 + w // 2 - 128 * j) // 64
        segs.append((a - qs, b - a, btab.c(si * 36 + m + 2, 1)))
        s0 += w
    return segs


def attn_stream2(fw, C, items):
    LAH = 2
    n = len(items)
    ident = C.ident
    for idx in range(n + LAH):
        if idx < n:
            it = items[idx]
            sb = C.ps[idx % 3]
            pb = C.pbuf[idx % 4]
            nm = len(it.masks)
            fw.op('pe', lambda e: e.matmul(sb.c(0, it.N), it.kT, it.qT, start=True, stop=(nm == 0)),
                  reads=it.reads, writes=[sb.name], signal=(nm == 0))
            for mi, (off, map_) in enumerate(it.masks):
                fw.op('pe', lambda e: e.matmul(sb.c(off, 128), ident.c(0, 128), map_, start=False, stop=(mi == nm - 1)),
                      reads=['ident', 'masks'], writes=[sb.name], signal=(mi == nm - 1))
            for (off, wd, bias_ap) in it.exps:
                fw.op('act', lambda e: e.activation(out=pb.c(off, wd), in_=sb.c(off, wd), func=AF.Exp, scale=it.slope, bias=bias_ap),
                      reads=[sb.name, 'btab'], writes=[pb.name])
            if it.mmask is not None:
                fw.op('dve', lambda e: e.tensor_tensor(out=pb.c(0, it.N), in0=pb.c(0, it.N), in1=it.mmask, op=ALU.mult),
                      reads=[pb.name, it.mkey], writes=[pb.name])
            for (off, m01) in getattr(it, 'pmasks', []):
                fw.op('pool', lambda e: e.tensor_tensor(out=pb.c(off, 128), in0=pb.c(off, 128), in1=m01, op=ALU.mult),
                      reads=[pb.name, 'masks'], writes=[pb.name])
            it.pb = pb
        if idx >= LAH:
            it = items[idx - LAH]
            for pi, (out_ap, lhsT, start) in enumerate(it.pvs):
                fw.op('pe', lambda e: e.matmul(out_ap, lhsT, it.pb.c(0, it.N), start=start, stop=False, skip_group_check=True),
                      reads=[it.pb.name] + it.vreads, writes=it.obanks, signal=(pi == len(it.pvs) - 1))
            if it.after is not None:
                it.after()


def emit_pv(fw, C, it):
    for pi, (ps_ap, pcol, rhs, start) in enumerate(it.pv):
        fw.op('pe', lambda e: e.matmul(ps_ap, it.pb.c(pcol, 128), rhs, start=start, stop=False, skip_group_check=True),
              reads=[it.pb.name] + it.vreads, writes=it.obanks, signal=(pi == len(it.pv) - 1))
    if getattr(it, 'after', None) is not None:
        it.after()


def vcopy(fw, eng, out, in_, reads, writes):
    if eng == 'act':
        fw.op('act', lambda e: e.activation(out=out, in_=in_, func=AF.Copy), reads=reads, writes=writes)
    else:
        fw.op(eng, lambda e: e.tensor_copy(out, in_), reads=reads, writes=writes)


def transpose_out(fw, C, och, oT, dram_t, c, tag):
    for f in range(4):
        ps = C.ps[6 + (f % 2)]
        for ql in range(4):
            fw.op('pe', lambda e: e.transpose(ps.c(ql * 128, 128), och.c(ql * 512 + f * 128, 128), C.ident.c(0, 128)),
                  reads=[och.name, 'ident'], writes=[ps.name], signal=(ql == 3))
        vcopy(fw, 'act' if f % 2 else 'dve', oT.c(f * 512, 512), ps.c(0, 512), [ps.name], [oT.name + str(f)])
    fw.dma('sp', lambda q: q.dma_start(out=dram2(dram_t, 8192, 0, 128, c * 512, [(2048, 4), (1, 512)]),
                                       in_=oT.ap(0, [(512, 4), (1, 512)])),
           'st_' + tag, reads=[oT.name + str(f) for f in range(4)], writes=[tag + '_dram%d' % c])


def phase_A(fw, C):
    nc = fw.nc
    d = C.d
    n0 = len(C.stack)
    Qp = [alloc(nc, C.stack, 'Qp%d' % i, 2048) for i in range(4)]
    KsT = [alloc(nc, C.stack, 'KsT%d' % g, 2048) for g in range(2)]
    KwT = [alloc(nc, C.stack, 'KwT%d' % g, 2048) for g in range(2)]
    VsA = alloc(nc, C.stack, 'VsA', 16 * 130 + 64)
    VwA = alloc(nc, C.stack, 'VwA', 16 * 130 + 64)
    gN = alloc(nc, C.stack, 'gN', 16 * 24)
    kcT = [alloc(nc, C.stack, 'kcT%d' % g, 128) for g in range(2)]
    vcA = [alloc(nc, C.stack, 'vcA%d' % g, 97) for g in range(2)]
    n1 = len(C.stack)
    KcT = alloc(nc, C.stack, 'KcT', 2048)
    VcT = alloc(nc, C.stack, 'VcT', 2048)
    fw.op('pool', lambda e: e.memset(VsA.c(0, 16 * 130 + 64), 1.0), writes=['VsA'])
    fw.op('pool', lambda e: e.memset(VwA.c(0, 16 * 130 + 64), 1.0), writes=['VwA'])
    fmdst = [KcT, VcT, KsT[0], KsT[1], KwT[0], KwT[1]]

    def fm_evac(i, c, ps):
        if i < 4:
            fw.op('dve', lambda e: e.tensor_scalar(out=Qp[i].c(c * 512, 512), in0=ps.c(0, 512), scalar1=C.qscale.c(i, 1),
                                                   scalar2=None, op0=ALU.mult),
                  reads=[ps.name, 'qscale'], writes=[Qp[i].name])
        else:
            dst = fmdst[i - 4]
            vcopy(fw, 'act', dst.c(c * 512, 512), ps.c(0, 512), [ps.name], [dst.name])

    def tm_evac(tt, ps):
        fw.op('dve', lambda e: e.tensor_copy(VsA.ap(tt * 130, [(65, 2), (1, 64)]), ps.ap(0, [(64, 2), (1, 64)])),
              reads=[ps.name], writes=['VsA'])
        fw.op('dve', lambda e: e.tensor_copy(VwA.ap(tt * 130, [(65, 2), (1, 64)]), ps.ap(128, [(64, 2), (1, 64)])),
              reads=[ps.name], writes=['VwA'])
        fw.op('act', lambda e: e.activation(out=gN.c(tt * 24, 24), in_=ps.c(256, 24), func=AF.Sigmoid),
              reads=[ps.name], writes=['gN'])

    proj_phase(fw, C, d['wA_fm'], 10, d['wA_tm'], 280, fm_evac, tm_evac)

    dump(fw, C, Qp[0], 0, 2048, [Qp[0].name], 0)
    dump(fw, C, KsT[1], 0, 2048, [KsT[1].name], 2048)
    dump(fw, C, VsA, 0, 2080, ['VsA'], 4096)
    dump(fw, C, gN, 0, 384, ['gN'], 6176)
    dump(fw, C, KcT, 0, 2048, ['KcT'], 6560)
    n2 = len(C.stack)
    W1 = alloc(nc, C.stack, 'W1', 32 * 256)
    pos = alloc(nc, C.stack, 'pos', 32)
    w2k = alloc(nc, C.stack, 'w2k', 256)
    w2v = alloc(nc, C.stack, 'w2v', 128)
    hidT = [alloc(nc, C.stack, 'hidT%d' % h, 128) for h in range(2)]
    crow = alloc(nc, C.stack, 'crow', 256)
    onesr = alloc(nc, C.stack, 'onesr', 128)
    hid = alloc(nc, C.stack, 'hid', 256)
    fw.op('pool', lambda e: e.memset(onesr.c(0, 128), 1.0), writes=['onesr'])
    load(fw, w2k, 0, d['w2k'], 0, 256, 'w2k', 'ld_w')
    load(fw, w2v, 0, d['w2v'], 0, 128, 'w2v', 'ld_w')
    for g in range(2):
        fw.op('pool', lambda e: e.memset(kcT[g].c(0, 128), 0.0), writes=[kcT[g].name])
        fw.op('pool', lambda e: e.memset(vcA[g].c(0, 97), 0.0), writes=[vcA[g].name])
        fw.op('pool', lambda e: e.memset(vcA[g].c(64, 1), 1.0), writes=[vcA[g].name])
        load(fw, vcA[g], 65, d['selw'], 0, 32, vcA[g].name, 'ld_w')
    for kv in range(2):
        src = KcT if kv == 0 else VcT
        for q4 in range(4):
            load(fw, W1, q4 * 2048, d['w1k' if kv == 0 else 'w1v'], q4 * 2048, 2048, 'W1', 'ld_w')
        load(fw, pos, 0, d['posk' if kv == 0 else 'posv'], 0, 32, 'pos', 'ld_w')
        psc = C.ps[5]
        for l in range(32):
            fw.op('pe', lambda e: e.matmul(psc.c(0, 256, 0, 1), pos.c(l, 1, 0, 64), W1.c(l * 256, 256, 0, 64),
                                           start=(l == 0), stop=(l == 31)),
                  reads=['W1', 'pos'], writes=[psc.name], signal=(l == 31))
        vcopy(fw, 'dve', crow.c(0, 256, 0, 1), psc.c(0, 256, 0, 1), [psc.name], ['crow'])
        for g in range(2):
            ps = C.ps[3 + g]
            for l in range(32):
                fw.op('pe', lambda e: e.matmul(ps.c(0, 256, 0, 127), src.ap(l, [(16, 127)], g * 64, 64),
                                               W1.c(l * 256, 256, g * 64, 64), start=(l == 0), stop=False),
                      reads=['W1', src.name], writes=[ps.name], signal=False)
            fw.op('pe', lambda e: e.matmul(ps.c(0, 256, 0, 127), onesr.c(0, 127, 0, 1), crow.c(0, 256, 0, 1), start=False, stop=True),
                  reads=['onesr', 'crow'], writes=[ps.name])
            fw.op('act', lambda e: e.activation(out=hid.c(0, 256, 0, 127), in_=ps.c(0, 256, 0, 127), func=AF.Gelu),
                  reads=[ps.name], writes=['hid'])
            pst = C.ps[6 + g]
            for half in range(2):
                fw.op('pe', lambda e: e.transpose(pst.c(half * 128, 127), hid.c(half * 128, 128, 0, 127), C.ident.c(0, 127, 0, 127)),
                      reads=['hid', 'ident'], writes=[pst.name], signal=(half == 1))
            for half in range(2):
                vcopy(fw, 'dve', hidT[half].c(0, 127), pst.c(half * 128, 127), [pst.name], [hidT[half].name])
            if kv == 0:
                ps = C.ps[1]
                for half in range(2):
                    fw.op('pe', lambda e: e.matmul(ps.c(0, 127), w2k.c(half * 128, 128), hidT[half].c(0, 127),
                                                   start=(half == 0), stop=(half == 1)),
                          reads=['w2k', hidT[half].name], writes=[ps.name], signal=(half == 1))
                vcopy(fw, 'dve', kcT[g].c(0, 127), ps.c(0, 127), [ps.name], [kcT[g].name])
            else:
                ps = C.ps[2]
                for half in range(2):
                    fw.op('pe', lambda e: e.matmul(ps.c(0, 64, 0, 127), hidT[half].c(0, 127), w2v.c(half * 64, 64),
                                                   start=(half == 0), stop=(half == 1)),
                          reads=['w2v', hidT[half].name], writes=[ps.name], signal=(half == 1))
                vcopy(fw, 'dve', vcA[g].c(0, 64, 0, 127), ps.c(0, 64, 0, 127), [ps.name], [vcA[g].name])
    dump(fw, C, kcT[1], 0, 128, [kcT[1].name], 8608)
    dump(fw, C, vcA[1], 0, 97, [vcA[1].name], 8736)
    free_to(C.stack, n1, fw)

    cmpb = alloc(nc, C.stack, 'cmpb', 2048)
    e01 = alloc(nc, C.stack, 'e01', 2048)
    btab = alloc(nc, C.stack, 'btab', 288)
    masks = alloc(nc, C.stack, 'masks', 512)
    selA = alloc(nc, C.stack, 'selA', 512)
    selB = alloc(nc, C.stack, 'selB', 512)
    C.pbuf = [alloc(nc, C.stack, 'pb%d' % i, 512) for i in range(4)]
    och = alloc(nc, C.stack, 'och', 2048)
    oT = alloc(nc, C.stack, 'oT', 2048)
    Mst = alloc(nc, C.stack, 'Mst', 8192)
    Qz = [alloc(nc, C.stack, 'Qz%d' % i, 512) for i in range(8)]
    for i in range(8):
        fw.op('pool', lambda e: e.memset(Qz[i].c(0, 512), 0.0), writes=[Qz[i].name])
    ots = alloc(nc, C.stack, 'ots', 512)
    imp = alloc(nc, C.stack, 'imp', 128)
    score = alloc(nc, C.stack, 'score', 128)
    nm = alloc(nc, C.stack, 'nm', 128)
    negT = [alloc(nc, C.stack, 'negT%d' % g, 512) for g in range(2)]
    sm = alloc(nc, C.stack, 'sm', 32)
    load(fw, cmpb, 0, d['cmpbias'], 0, 2048, 'cmpb', 'ld_w')
    load(fw, e01, 0, d['e01'], 0, 2048, 'e01', 'ld_w')
    for g in range(2):
        fw.op('pool', lambda e: e.memset(negT[g].c(0, 512), 0.0), writes=[negT[g].name])
    load(fw, btab, 0, d['btab'], 0, 288, 'btab', 'ld_w')
    load(fw, masks, 0, d['masks'], 0, 512, 'masks', 'ld_w')
    load(fw, selA, 0, d['selA'], 0, 512, 'selA', 'ld_w')
    load(fw, selB, 0, d['selB'], 0, 512, 'selB', 'ld_w')
    ident = C.ident

    def normalize(O, W, hh, br, c, first_in_group, clamp=1e-30):
        def f():
            fw.op('dve', lambda e: e.tensor_scalar(out=sm.c(0, 4), in0=O.ap(64, [(W, 4)]), scalar1=clamp, scalar2=None,
                                                   op0=ALU.max), reads=[O.name], writes=['sm_rs'])
            fw.op('dve', lambda e: e.reciprocal(sm.c(4, 4), sm.c(0, 4)), reads=['sm_rs'], writes=['sm_rcp'])
            fw.op('dve', lambda e: e.tensor_tensor(out=sm.c(8, 4), in0=sm.c(4, 4),
                                                   in1=gN.ap(4 * c * 24 + hh * 3 + br, [(24, 4)]), op=ALU.mult),
                  reads=['sm_rcp', 'gN'], writes=['sm_rg'])
            for ql in range(4):
                dst = och.c(ql * 512 + hh * 64, 64)
                if br == 0:
                    fw.op('dve', lambda e: e.tensor_scalar(out=dst, in0=O.c(ql * W, 64), scalar1=sm.c(8 + ql, 1), scalar2=None,
                                                           op0=ALU.mult), reads=[O.name, 'sm_rg'], writes=['och'])
                else:
                    fw.op('dve', lambda e: e.scalar_tensor_tensor(out=dst, in0=O.c(ql * W, 64), scalar=sm.c(8 + ql, 1), in1=dst,
                                                                  op0=ALU.mult, op1=ALU.add),
                          reads=[O.name, 'sm_rg', 'och'], writes=['och'])
                if br == 0:
                    idst = imp.c(ql * 32, 32)
                    if first_in_group:
                        fw.op('dve', lambda e: e.tensor_scalar(out=idst, in0=O.c(ql * W + 65, 32), scalar1=sm.c(4 + ql, 1),
                                                               scalar2=None, op0=ALU.mult),
                              reads=[O.name, 'sm_rcp'], writes=['imp'])
                    else:
                        fw.op('dve', lambda e: e.scalar_tensor_tensor(out=idst, in0=O.c(ql * W + 65, 32), scalar=sm.c(4 + ql, 1),
                                                                      in1=idst, op0=ALU.mult, op1=ALU.add),
                              reads=[O.name, 'sm_rcp', 'imp'], writes=['imp'])
        return f

    On = C.ps[5]

    def after_fm(OT, hh, br, c):
        def f():
            vcopy(fw, 'act', ots.c(0, 512, 0, 65), OT.c(0, 512, 0, 65), [OT.name], ['ots'])
            for ql in range(4):
                fw.op('pe', lambda e: e.transpose(On.c(ql * 65, 65), ots.c(ql * 128, 128, 0, 65), ident.c(0, 65, 0, 65)),
                      reads=['ots', 'ident'], writes=[On.name], signal=(ql == 3))
            normalize(On, 65, hh, br, c, False, clamp=1e-37)()
        return f

    def build_items(c, br, heads):
        q0 = c * 512
        items = []
        for hh in heads:
            g = hh // 4
            i, ph = hh // 2, hh % 2
            OT = C.ps[3 + (hh % 2)]
            VA = VsA if br == 1 else VwA
            KT = KsT[g] if br == 1 else KwT[g]
            jlo = 0 if br == 1 else max(0, 4 * c - 4)
            jhi = 4 * c + 3
            first = True
            for j in range(jlo, jhi + 1):
                tlo = max(j, 4 * c)
                thi = 4 * c + 3 if br == 1 else min(j + 4, 4 * c + 3)
                qs_ = tlo * 128
                N = (thi - tlo + 1) * 128
                it = AttnItem()
                it.kT = KT.c(j * 128, 128)
                it.qT = Qz[hh].c(qs_ - q0, N)
                it.reads = [KT.name, Qz[hh].name]
                it.N = N
                it.slope = SLOPE_N[hh]
                it.masks = []
                it.pmasks = []
                for tq in range(tlo, thi + 1):
                    Dd = tq - j
                    off = (tq - tlo) * 128
                    if br == 2 and Dd == 0:
                        it.pmasks.append((off, masks.c(256, 128)))
                    if br == 2 and Dd == 4:
                        it.pmasks.append((off, masks.c(384, 128)))
                it.exps = exp_segments(btab, hh, qs_, qs_ + N, j)
                if br == 1:
                    it.mmask = Mst.c(j * 512 + qs_ - q0, N)
                    it.mkey = 'Mst%d' % j
                else:
                    it.mmask = None
                it.pvs = [(OT.c(qs_ - q0, N), VA.c(j * 130 + g * 65, 128), first)]
                first = False
                it.vreads = [VA.name]
                it.obanks = [OT.name]
                it.after = after_fm(OT, hh, br, c) if j == jhi else None
                items.append(it)
        return items

    for c in range(4):
        q0 = c * 512
        for hh in range(8):
            fw.op('pool', lambda e: e.tensor_copy(Qz[hh].c(0, 512, (hh % 2) * 64, 64), Qp[hh // 2].c(q0, 512, (hh % 2) * 64, 64)),
                  reads=[Qp[hh // 2].name], writes=[Qz[hh].name])
        for g in range(2):
            items = []
            for hl in range(4):
                hh = 4 * g + hl
                i, ph = hh // 2, hh % 2
                O = C.ps[3 + hl]
                it = AttnItem()
                it.kT = kcT[g].c(0, 128, ph * 64, 64)
                it.qT = Qp[i].c(q0, 512, ph * 64, 64)
                it.extra = [(ident.c(0, 128), cmpb.c(q0, 512))]
                it.N = 512
                it.slope = SLOPE_N[hh]
                it.pv = [(O.c(ql * 97, 97), ql * 128, vcA[g].c(0, 97), ql == 0) for ql in range(4)]
                it.reads = [kcT[g].name, Qp[i].name, 'ident', 'cmpb']
                it.vreads = [vcA[g].name]
                it.obanks = [O.name]
                it.after = normalize(O, 97, hh, 0, c, hl == 0)
                items.append(it)
            attn_stream(fw, C, items)
            fw.op('dve', lambda e: e.tensor_tensor(out=score.c(0, 128), in0=imp.c(0, 128), in1=selA.c(c * 128, 128), op=ALU.mult),
                  reads=['imp', 'selA'], writes=['score'])
            fw.op('dve', lambda e: e.tensor_tensor(out=score.c(0, 128), in0=score.c(0, 128), in1=selB.c(c * 128, 128), op=ALU.add),
                  reads=['score', 'selB'], writes=['score'])
            ps5 = C.ps[7]
            for ql in range(4):
                fw.op('dve', lambda e: e.max(out=sm.c(16, 8), in_=score.c(ql * 32, 32)), reads=['score'], writes=['sm_top'])
                fw.op('dve', lambda e: e.tensor_scalar(out=nm.c(ql * 32, 32), in0=score.c(ql * 32, 32), scalar1=sm.c(23, 1),
                                                       scalar2=None, op0=ALU.is_ge),
                      reads=['score', 'sm_top'], writes=['nm'])
                fw.op('pe', lambda e: e.transpose(ps5.c(ql * 128, 128, 0, 32), nm.c(ql * 32, 32), ident.c(0, 128)),
                      reads=['nm', 'ident'], writes=[ps5.name])
            vcopy(fw, 'dve', negT[g].c(0, 512, 0, 32), ps5.c(0, 512, 0, 32), [ps5.name], [negT[g].name])
            for j in range(4 * c + 4):
                mp_ = C.ps[6 + (j % 2)]
                fw.op('pe', lambda e: e.matmul(mp_.c(0, 512), e01.c(j * 128, 128), negT[g].c(0, 512), start=True, stop=True),
                      reads=['e01', negT[g].name], writes=[mp_.name])
                vcopy(fw, 'dve', Mst.c(j * 512, 512), mp_.c(0, 512), [mp_.name], ['Mst%d' % j])
                if j >= 4 * c:
                    sub = Mst.c(j * 512 + (j - 4 * c) * 128, 128)
                    fw.op('pool', lambda e: e.tensor_tensor(out=sub, in0=sub, in1=masks.c(256, 128), op=ALU.mult),
                          reads=['Mst%d' % j, 'masks'], writes=['Mst%d' % j])
            items = build_items(c, 1, range(4 * g, 4 * g + 4))
            attn_stream2(fw, C, items)
        attn_stream2(fw, C, build_items(c, 2, range(8)))
        transpose_out(fw, C, och, oT, C.onT, c, 'onT')
    free_to(C.stack, n0, fw)


def phase_B(fw, C):
    nc = fw.nc
    d = C.d
    n0 = len(C.stack)
    Qd = [alloc(nc, C.stack, 'Qd%d' % i, 2048) for i in range(4)]
    Kd = [alloc(nc, C.stack, 'Kd%d' % i, 2048) for i in range(4)]
    Vd = alloc(nc, C.stack, 'Vd', 16 * 512)

    def fm_evac(i, c, ps):
        if i < 4:
            fw.op('dve', lambda e: e.tensor_scalar(out=Qd[i].c(c * 512, 512), in0=ps.c(0, 512), scalar1=C.qscale.c(4 + i, 1),
                                                   scalar2=None, op0=ALU.mult),
                  reads=[ps.name, 'qscale'], writes=[Qd[i].name])
        else:
            vcopy(fw, 'act', Kd[i - 4].c(c * 512, 512), ps.c(0, 512), [ps.name], [Kd[i - 4].name])

    def tm_evac(tt, ps):
        vcopy(fw, 'dve', Vd.c(tt * 512, 512), ps.c(0, 512), [ps.name], ['Vd'])

    proj_phase(fw, C, d['wB_fm'], 8, d['wB_tm'], 512, fm_evac, tm_evac)

    btab = alloc(nc, C.stack, 'btab', 288)
    masks = alloc(nc, C.stack, 'masks', 512)
    C.pbuf = [alloc(nc, C.stack, 'pb%d' % i, 512) for i in range(4)]
    och = alloc(nc, C.stack, 'och', 2048)
    oT = alloc(nc, C.stack, 'oT', 2048)
    o1 = alloc(nc, C.stack, 'o1', 512)
    otv = alloc(nc, C.stack, 'otv', 512)
    otsr = alloc(nc, C.stack, 'otsr', 512)
    onesc = alloc(nc, C.stack, 'onesc', 128)
    Qz = [alloc(nc, C.stack, 'Qz%d' % i, 512) for i in range(8)]
    for i in range(8):
        fw.op('pool', lambda e: e.memset(Qz[i].c(0, 512), 0.0), writes=[Qz[i].name])
    junk = alloc(nc, C.stack, 'junk', 128)
    gbc = alloc(nc, C.stack, 'gbc', 128)
    lt = alloc(nc, C.stack, 'lt', 640)
    neglam = alloc(nc, C.stack, 'neglam', 1)
    sm = alloc(nc, C.stack, 'smd', 32)
    load(fw, btab, 0, d['btab'], 0, 288, 'btab', 'ld_w')
    load(fw, masks, 0, d['masks'], 0, 512, 'masks', 'ld_w')
    fw.op('pool', lambda e: e.memset(onesc.c(0, 128), 1.0), writes=['onesc'])
    ident = C.ident
    load(fw, lt, 0, d['lam'], 0, 256, 'lt', 'ld_w', nrows=1)
    fw.op('pool', lambda e: e.memset(lt.c(400, 128, 0, 1), 1.0), writes=['lt_ones'])
    fw.op('dve', lambda e: e.tensor_tensor(out=lt.ap(256, [(64, 2), (1, 64)], 0, 1), in0=lt.ap(0, [(128, 2), (1, 64)], 0, 1),
                                           in1=lt.ap(64, [(128, 2), (1, 64)], 0, 1), op=ALU.mult), reads=['lt'], writes=['lt_p'])
    fw.op('dve', lambda e: e.tensor_reduce(out=lt.c(384, 2, 0, 1), in_=lt.ap(256, [(64, 2), (1, 64)], 0, 1), axis=AX.X, op=ALU.add),
          reads=['lt_p'], writes=['lt_s'])
    fw.op('act', lambda e: e.activation(out=lt.c(386, 2, 0, 1), in_=lt.c(384, 2, 0, 1), func=AF.Exp), reads=['lt_s'], writes=['lt_e'])
    fw.op('dve', lambda e: e.tensor_tensor(out=lt.c(388, 1, 0, 1), in0=lt.c(386, 1, 0, 1), in1=lt.c(387, 1, 0, 1), op=ALU.subtract),
          reads=['lt_e'], writes=['lt_d'])
    fw.op('dve', lambda e: e.tensor_scalar(out=lt.c(389, 1, 0, 1), in0=lt.c(388, 1, 0, 1), scalar1=LAM_INIT, scalar2=-1.0,
                                           op0=ALU.add, op1=ALU.mult), reads=['lt_d'], writes=['lt_n'])
    ps7 = C.ps[7]
    fw.op('pe', lambda e: e.matmul(ps7.c(0, 1), lt.c(400, 128, 0, 1), lt.c(389, 1, 0, 1), start=True, stop=True),
          reads=['lt_n', 'lt_ones'], writes=[ps7.name])
    vcopy(fw, 'dve', neglam.c(0, 1), ps7.c(0, 1), [ps7.name], ['neglam'])
    fw.dma('sp', lambda q: q.dma_start(out=gbc.c(0, 128), in_=bass.AP(d['normg'], 0, [[0, 128], [1, 128]])), 'ld_w', writes=['gbc0'])
    fw.op('dve', lambda e: e.tensor_scalar(out=gbc.c(0, 128), in0=gbc.c(0, 128), scalar1=1.0 - LAM_INIT, scalar2=None, op0=ALU.mult),
          reads=['gbc0'], writes=['gbc'])

    OTv = [C.ps[3], C.ps[4]]
    OTs = [C.ps[5], C.ps[6]]
    On = C.ps[7]

    def after0(h, mp):
        def f():
            base = 0 if mp == 0 else 8
            vcopy(fw, 'act', otv.c(0, 512), OTv[mp].c(0, 512), [OTv[mp].name], ['otv'])
            vcopy(fw, 'dve', otsr.c(0, 512, 0, 1), OTs[mp].c(0, 512, 0, 1), [OTs[mp].name], ['otsr'])
            for ql in range(4):
                fw.op('pe', lambda e: e.transpose(On.c(ql * 128, 128), otv.c(ql * 128, 128), ident.c(0, 128)),
                      reads=['otv', 'ident'], writes=[On.name], signal=(ql == 3))
            for ql in range(4):
                fw.op('pe', lambda e: e.transpose(OTs[mp].c(ql, 1), otsr.c(ql * 128, 128, 0, 1), ident.c(0, 1, 0, 1)),
                      reads=['otsr', 'ident'], writes=[OTs[mp].name], signal=(ql == 3))
            fw.op('dve', lambda e: e.tensor_scalar(out=sm.c(base, 4), in0=OTs[mp].c(0, 4), scalar1=1e-37,
                                                   scalar2=None, op0=ALU.max), reads=[OTs[mp].name], writes=['smd_rs%d' % mp])
            fw.op('dve', lambda e: e.reciprocal(sm.c(base + 4, 4), sm.c(base, 4)), reads=['smd_rs%d' % mp], writes=['smd_rcp%d' % mp])
            if mp == 0:
                for ql in range(4):
                    fw.op('dve', lambda e: e.tensor_scalar(out=o1.c(ql * 128, 128), in0=On.c(ql * 128, 128),
                                                           scalar1=sm.c(4 + ql, 1), scalar2=None, op0=ALU.mult),
                          reads=[On.name, 'smd_rcp0'], writes=['o1'])
                return
            fw.op('dve', lambda e: e.tensor_scalar(out=sm.c(16, 4), in0=sm.c(12, 4), scalar1=neglam.c(0, 1), scalar2=None, op0=ALU.mult),
                  reads=['smd_rcp1', 'neglam'], writes=['smd_nlr'])
            fw.op('dve', lambda e: e.memset(sm.c(20, 4), 0.0), writes=['smd_ss'])
            for ql in range(4):
                fw.op('dve', lambda e: e.scalar_tensor_tensor(out=o1.c(ql * 128, 128), in0=On.c(ql * 128, 128),
                                                              scalar=sm.c(16 + ql, 1), in1=o1.c(ql * 128, 128), op0=ALU.mult, op1=ALU.add),
                      reads=[On.name, 'smd_nlr', 'o1'], writes=['o1'])
                fw.op('dve', lambda e: e.scalar_tensor_tensor(out=junk.c(0, 128), in0=o1.c(ql * 128, 128), scalar=1.0,
                                                              in1=o1.c(ql * 128, 128), op0=ALU.mult, op1=ALU.mult,
                                                              accum_out=sm.c(20 + ql, 1)),
                      reads=['o1', 'smd_ss'], writes=['junk', 'smd_ss'])
            fw.op('dve', lambda e: e.tensor_scalar(out=sm.c(24, 4), in0=sm.c(20, 4), scalar1=1.0 / 128, scalar2=1e-5,
                                                   op0=ALU.mult, op1=ALU.add), reads=['smd_ss'], writes=['smd_var'])
            fw.op('act', lambda e: e.activation(out=sm.c(24, 4), in_=sm.c(24, 4), func=AF.Sqrt), reads=['smd_var'], writes=['smd_var'])
            fw.op('dve', lambda e: e.reciprocal(sm.c(28, 4), sm.c(24, 4)), reads=['smd_var'], writes=['smd_rstd'])
            for ql in range(4):
                fw.op('dve', lambda e: e.scalar_tensor_tensor(out=och.c(ql * 512 + h * 128, 128), in0=o1.c(ql * 128, 128),
                                                              scalar=sm.c(28 + ql, 1), in1=gbc.c(0, 128), op0=ALU.mult, op1=ALU.mult),
                      reads=['o1', 'smd_rstd', 'gbc'], writes=['och'])
        return f

    for c in range(4):
        q0 = c * 512
        for h in range(4):
            for mp in range(2):
                fw.op('pool', lambda e: e.tensor_copy(Qz[h * 2 + mp].c(0, 512, (h % 2) * 64, 64),
                                                      Qd[mp * 2 + h // 2].c(q0, 512, (h % 2) * 64, 64)),
                      reads=[Qd[mp * 2 + h // 2].name], writes=[Qz[h * 2 + mp].name])
        items = []
        for h in range(4):
            ph = h % 2
            si = 2 * h + 1
            for mp in range(2):
                ti = mp * 2 + h // 2
                first = True
                jhi = 4 * c + 3
                for j in range(0, jhi + 1):
                    tlo = max(j, 4 * c)
                    thi = 4 * c + 3
                    qs_ = tlo * 128
                    N = (thi - tlo + 1) * 128
                    it = AttnItem()
                    it.kT = Kd[ti].c(j * 128, 128)
                    it.qT = Qz[h * 2 + mp].c(qs_ - q0, N)
                    it.reads = [Kd[ti].name, Qz[h * 2 + mp].name]
                    it.N = N
                    it.slope = SLOPE_D[h]
                    it.masks = []
                    it.pmasks = []
                    for tq in range(tlo, thi + 1):
                        Dd = tq - j
                        off = (tq - tlo) * 128
                        if Dd == 0:
                            it.pmasks.append((off, masks.c(256, 128)))
                    it.exps = exp_segments(btab, si, qs_, qs_ + N, j)
                    it.mmask = None
                    it.pvs = [(OTv[mp].c(qs_ - q0, N), Vd.c(j * 512 + h * 128, 128), first),
                              (OTs[mp].c(qs_ - q0, N), onesc.c(0, 128), first)]
                    first = False
                    it.vreads = ['Vd', 'onesc']
                    it.obanks = [OTv[mp].name, OTs[mp].name]
                    it.after = after0(h, mp) if j == jhi else None
                    items.append(it)
        attn_stream2(fw, C, items)
        transpose_out(fw, C, och, oT, C.odT, c, 'odT')
    free_to(C.stack, n0, fw)


def layer_norm_tile(fw, C, y, gb, goff, out, sm, tag, eng='pool'):
    for hf in range(2):
        fw.op('dve', lambda e: e.bn_stats(sm.c(hf * 6, 6), y.c(hf * 512, 512)), reads=[y.name], writes=[tag + 'st%d' % hf])
    fw.op('dve', lambda e: e.bn_aggr(sm.c(12, 2), sm.c(0, 12)), reads=[tag + 'st0', tag + 'st1'], writes=[tag + 'mv'])
    fw.op('dve', lambda e: e.tensor_scalar(out=sm.c(14, 1), in0=sm.c(13, 1), scalar1=1e-5, scalar2=None, op0=ALU.add),
          reads=[tag + 'mv'], writes=[tag + 've'])
    fw.op('act', lambda e: e.activation(out=sm.c(15, 1), in_=sm.c(14, 1), func=AF.Sqrt), reads=[tag + 've'], writes=[tag + 'sd'])
    fw.op('dve', lambda e: e.reciprocal(sm.c(16, 1), sm.c(15, 1)), reads=[tag + 'sd'], writes=[tag + 'rstd'])
    fw.op('dve', lambda e: e.tensor_scalar(out=out.c(0, 1024), in0=y.c(0, 1024), scalar1=sm.c(12, 1), scalar2=sm.c(16, 1),
                                           op0=ALU.subtract, op1=ALU.mult), reads=[y.name, tag + 'mv', tag + 'rstd'], writes=[out.name])
    fw.op(eng, lambda e: e.tensor_tensor(out=out.c(0, 1024), in0=out.c(0, 1024), in1=gb.c(goff, 1024), op=ALU.mult),
          reads=[out.name, 'gb'], writes=[out.name])
    fw.op(eng, lambda e: e.tensor_tensor(out=out.c(0, 1024), in0=out.c(0, 1024), in1=gb.c(goff + 1024, 1024), op=ALU.add),
          reads=[out.name, 'gb'], writes=[out.name])


def phase_C(fw, C):
    nc = fw.nc
    d = C.d
    n0 = len(C.stack)
    wbn = alloc(nc, C.stack, 'wbn', 4096)
    wbd = alloc(nc, C.stack, 'wbd', 4096)
    wout = alloc(nc, C.stack, 'wout', 8192)
    Wg = [alloc(nc, C.stack, 'Wg%d' % i, 2048) for i in range(2)]
    onc = alloc(nc, C.stack, 'onc', 2048)
    odc = alloc(nc, C.stack, 'odc', 2048)
    xc = alloc(nc, C.stack, 'xc', 4096)
    mT = alloc(nc, C.stack, 'mT', 4096)
    sg = [alloc(nc, C.stack, 'sg%d' % i, 512) for i in range(2)]
    tmp = alloc(nc, C.stack, 'tmpc', 512)
    gb = alloc(nc, C.stack, 'gb', 2048)
    xt = alloc(nc, C.stack, 'xt', 1024)
    y = alloc(nc, C.stack, 'y', 1024)
    ho = [alloc(nc, C.stack, 'ho%d' % i, 1024) for i in range(2)]
    sm = alloc(nc, C.stack, 'smc', 32)
    for q4 in range(2):
        load(fw, wbn, q4 * 2048, d['wbn'], q4 * 2048, 2048, 'wbn', 'ld_w')
        load(fw, wbd, q4 * 2048, d['wbd'], q4 * 2048, 2048, 'wbd', 'ld_w')
    for q4 in range(4):
        load(fw, wout, q4 * 2048, d['wout'], q4 * 2048, 2048, 'wout', 'ld_w')
    fw.dma('sp', lambda q: q.dma_start(out=gb.c(0, 2048), in_=bass.AP(d['ln'], 0, [[0, 128], [1, 2048]])), 'ld_w', writes=['gb'])
    for c in range(4):
        fw.dma('sp', lambda q: q.dma_start(out=onc.ap(0, [(512, 4), (1, 512)]),
                                           in_=dram2(C.onT, 8192, 0, 128, c * 512, [(2048, 4), (1, 512)])),
               'ld_c', reads=['onT_dram%d' % c], writes=['onc'])
        fw.dma('sp', lambda q: q.dma_start(out=odc.ap(0, [(512, 4), (1, 512)]),
                                           in_=dram2(C.odT, 8192, 0, 128, c * 512, [(2048, 4), (1, 512)])),
               'ld_c', reads=['odT_dram%d' % c], writes=['odc'])
        fw.dma('sp', lambda q: q.dma_start(out=xc.ap(0, [(512, 8), (1, 512)]),
                                           in_=dram2(d['xT'], 8 * 2048, 0, 128, c * 512, [(2048, 8), (1, 512)])),
               'ld_c', writes=['xc'])
        for m in range(8):
            W = Wg[m % 2]
            load(fw, W, 0, d['wC_g'], m * 1024, 1024, W.name, 'ld_g%d' % (m % 2))
            load(fw, W, 1024, d['wC_g'], (8 + m) * 1024, 1024, W.name, 'ld_g%d' % (m % 2))
            pb = [C.ps[(m % 2) * 4 + i] for i in range(4)]
            for gi in range(2):
                for kt in range(8):
                    fw.op('pe', lambda e: e.matmul(pb[gi].c(0, 512), W.c(gi * 1024 + kt * 128, 128), xc.c(kt * 512, 512),
                                                   start=(kt == 0), stop=(kt == 7)),
                          reads=[W.name, 'xc'], writes=[pb[gi].name], signal=(kt == 7))
            for bi, (wb, oc) in enumerate(((wbn, onc), (wbd, odc))):
                for ft in range(4):
                    fw.op('pe', lambda e: e.matmul(pb[2 + bi].c(0, 512), wb.c(ft * 1024 + m * 128, 128), oc.c(ft * 512, 512),
                                                   start=(ft == 0), stop=(ft == 3)),
                          reads=[wb.name, oc.name], writes=[pb[2 + bi].name], signal=(ft == 3))
            for gi in range(2):
                fw.op('act', lambda e: e.activation(out=sg[gi].c(0, 512), in_=pb[gi].c(0, 512), func=AF.Sigmoid),
                      reads=[pb[gi].name], writes=[sg[gi].name])
            fw.op('dve', lambda e: e.tensor_tensor(out=mT.c(m * 512, 512), in0=sg[0].c(0, 512), in1=pb[2].c(0, 512), op=ALU.mult),
                  reads=[sg[0].name, pb[2].name], writes=['mT%d' % m])
            fw.op('dve', lambda e: e.tensor_tensor(out=tmp.c(0, 512), in0=sg[1].c(0, 512), in1=pb[3].c(0, 512), op=ALU.mult),
                  reads=[sg[1].name, pb[3].name], writes=['tmpc'])
            fw.op('pool', lambda e: e.tensor_tensor(out=mT.c(m * 512, 512), in0=mT.c(m * 512, 512), in1=tmp.c(0, 512), op=ALU.add),
                  reads=['mT%d' % m, 'tmpc'], writes=['mT%d' % m])
        for tl in range(4):
            tt = c * 4 + tl
            fw.dma('sp', lambda q: q.dma_start(out=xt.c(0, 1024), in_=dram2(d['xtm'], 1024, tt * 128, 128, 0, [(1, 1024)])),
                   'ld_xt', writes=['xt'])
            for hf in range(2):
                ps = C.ps[hf]
                for ft in range(8):
                    fw.op('pe', lambda e: e.matmul(ps.c(0, 512), mT.c(ft * 512 + tl * 128, 128), wout.c(ft * 1024 + hf * 512, 512),
                                                   start=(ft == 0), stop=(ft == 7)),
                          reads=['mT%d' % ft, 'wout'], writes=[ps.name], signal=(ft == 7))
                fw.op('dve', lambda e: e.scalar_tensor_tensor(out=y.c(hf * 512, 512), in0=xt.c(hf * 512, 512), scalar=ALPHA,
                                                              in1=ps.c(0, 512), op0=ALU.mult, op1=ALU.add),
                      reads=['xt', ps.name], writes=['y'])
            h = ho[tt % 2]
            layer_norm_tile(fw, C, y, gb, 0, h, sm, 'c')
            fw.dma('sp', lambda q: q.dma_start(out=dram2(C.h1, 1024, tt * 128, 128, 0, [(1, 1024)]), in_=h.c(0, 1024)),
                   'st_h1', reads=[h.name], writes=['h1_dram%d' % tt])
    free_to(C.stack, n0, fw)


def phase_D(fw, C):
    nc = fw.nc
    d = C.d
    n0 = len(C.stack)
    ident = C.ident
    eid = alloc(nc, C.stack, 'eid', 2048, dtype=U32)
    gates = alloc(nc, C.stack, 'gates', 2048)
    n1 = len(C.stack)
    wq = alloc(nc, C.stack, 'wq', 8 * 2048)
    sk = alloc(nc, C.stack, 'sk', 256)
    iota = alloc(nc, C.stack, 'iota', 16)
    h1T = alloc(nc, C.stack, 'h1T', 4096)
    qT = alloc(nc, C.stack, 'qT', 16 * 512)
    ht = [alloc(nc, C.stack, 'ht%d' % i, 1024) for i in range(2)]
    scbs = [alloc(nc, C.stack, 'scb%s' % x, 2048) for x in ('A', 'B')]
    work = alloc(nc, C.stack, 'work', 512)
    tv = alloc(nc, C.stack, 'tv', 256)
    ti = alloc(nc, C.stack, 'ti', 256, dtype=U32)
    tif = alloc(nc, C.stack, 'tif', 256)
    cand = alloc(nc, C.stack, 'cand', 2048)
    cv = alloc(nc, C.stack, 'cv', 128)
    cpos = alloc(nc, C.stack, 'cpos', 128, dtype=U32)
    apos = alloc(nc, C.stack, 'apos', 256, dtype=U32)
    abf = alloc(nc, C.stack, 'abf', 256)
    isel = alloc(nc, C.stack, 'isel', 256)
    eidf = alloc(nc, C.stack, 'eidf', 128)
    gsh = alloc(nc, C.stack, 'gsh', 128)
    gs = alloc(nc, C.stack, 'gs', 16)
    for q8 in range(8):
        load(fw, wq, q8 * 2048, d['wq'], q8 * 2048, 2048, 'wq', 'ld_w')
    load(fw, sk, 0, d['sk'], 0, 256, 'sk', 'ld_w')
    load(fw, iota, 0, d['iota16'], 0, 16, 'iota', 'ld_w')
    for c in range(4):
        for tl in range(4):
            tt = c * 4 + tl
            H = ht[tl % 2]
            fw.dma('sp', lambda q: q.dma_start(out=H.c(0, 1024), in_=dram2(C.h1, 1024, tt * 128, 128, 0, [(1, 1024)])),
                   'ld_h%d' % (tl % 2), reads=['h1_dram%d' % tt], writes=[H.name])
            for a in range(2):
                ps = C.ps[a]
                for k4 in range(4):
                    kt = a * 4 + k4
                    fw.op('pe', lambda e: e.transpose(ps.c(k4 * 128, 128), H.c(kt * 128, 128), ident.c(0, 128)),
                          reads=[H.name, 'ident'], writes=[ps.name], signal=(k4 == 3))
                vcopy(fw, 'act', h1T.ap(a * 4 * 512 + tl * 128, [(512, 4), (1, 128)]), ps.ap(0, [(128, 4), (1, 128)]),
                      [ps.name], ['h1T'])
        for n in range(16):
            ps = C.ps[2 + (n % 2)]
            for kt in range(8):
                fw.op('pe', lambda e: e.matmul(ps.c(0, 512), wq.c(kt * 2048 + n * 128, 128), h1T.c(kt * 512, 512),
                                               start=(kt == 0), stop=(kt == 7)),
                      reads=['wq', 'h1T'], writes=[ps.name], signal=(kt == 7))
            vcopy(fw, 'act', qT.c(n * 512, 512), ps.c(0, 512), [ps.name], ['qT'])
        for tl in range(4):
            tt = c * 4 + tl
            for nb in range(4):
                ps = C.ps[4 + (nb % 2)]
                for ni in range(4):
                    n = nb * 4 + ni
                    fw.op('pe', lambda e: e.matmul(ps.c(ni * 128, 128), qT.c(n * 512 + tl * 128, 128), sk.c((n % 2) * 128, 128),
                                                   start=True, stop=True, skip_group_check=True),
                          reads=['qT', 'sk'], writes=[ps.name], signal=(ni == 3))
                vcopy(fw, 'act', scbs[tl % 2].c(nb * 512, 512), ps.c(0, 512), [ps.name], ['scb%d_%d' % (tl % 2, nb)])
            for n0_ in range(0, 16, 2):
                pr = (n0_, n0_ + 1)
                sc2 = [scbs[tl % 2].c(n * 128, 128) for n in pr]
                rk2 = [['scb%d_%d' % (tl % 2, n // 4)] for n in pr]
                wk2 = [work.c(k * 128, 128) for k in range(2)]
                for k, n in enumerate(pr):
                    fw.op('dve', lambda e: e.max(out=tv.c(n * 16, 8), in_=sc2[k]), reads=rk2[k], writes=['tva%d' % k])
                for k, n in enumerate(pr):
                    fw.op('dve', lambda e: e.max_index(out=ti.c(n * 16, 8), in_max=tv.c(n * 16, 8), in_values=sc2[k]),
                          reads=rk2[k] + ['tva%d' % k], writes=['ti%d' % k])
                for k, n in enumerate(pr):
                    fw.op('dve', lambda e: e.match_replace(out=wk2[k], in_to_replace=tv.c(n * 16, 8), in_values=sc2[k], imm_value=-1e30),
                          reads=rk2[k] + ['tva%d' % k], writes=['work%d' % k])
                for k, n in enumerate(pr):
                    fw.op('dve', lambda e: e.max(out=tv.c(n * 16 + 8, 8), in_=wk2[k]), reads=['work%d' % k], writes=['tvb%d' % k])
                for k, n in enumerate(pr):
                    fw.op('dve', lambda e: e.max_index(out=ti.c(n * 16 + 8, 8), in_max=tv.c(n * 16 + 8, 8), in_values=wk2[k]),
                          reads=['work%d' % k, 'tvb%d' % k], writes=['ti%d' % k])
            fw.op('dve', lambda e: e.tensor_copy(tif.c(0, 256), ti.c(0, 256)), reads=['ti0', 'ti1'], writes=['tif'])
            fw.op('dve', lambda e: e.tensor_tensor(out=cand.ap(0, [(256, 8), (16, 16), (1, 16)]),
                                                   in0=tv.ap(0, [(32, 8), (1, 16), (0, 16)]),
                                                   in1=tv.ap(16, [(32, 8), (0, 16), (1, 16)]), op=ALU.add),
                  reads=['tva0', 'tva1', 'tvb0', 'tvb1'], writes=['cand'])
            for h0_ in range(0, 8, 2):
                pr = (h0_, h0_ + 1)
                cd2 = [cand.c(h * 256, 256) for h in pr]
                wk2 = [work.c(k * 256, 256) for k in range(2)]
                for k, h in enumerate(pr):
                    fw.op('dve', lambda e: e.max(out=cv.c(h * 16, 8), in_=cd2[k]), reads=['cand'], writes=['cva%d' % k])
                for k, h in enumerate(pr):
                    fw.op('dve', lambda e: e.max_index(out=cpos.c(h * 16, 8), in_max=cv.c(h * 16, 8), in_values=cd2[k]),
                          reads=['cand', 'cva%d' % k], writes=['cpos%d' % k])
                for k, h in enumerate(pr):
                    fw.op('dve', lambda e: e.match_replace(out=wk2[k], in_to_replace=cv.c(h * 16, 8), in_values=cd2[k], imm_value=-1e30),
                          reads=['cand', 'cva%d' % k], writes=['work%d' % k])
                for k, h in enumerate(pr):
                    fw.op('dve', lambda e: e.max(out=cv.c(h * 16 + 8, 8), in_=wk2[k]), reads=['work%d' % k], writes=['cvb%d' % k])
                for k, h in enumerate(pr):
                    fw.op('dve', lambda e: e.max_index(out=cpos.c(h * 16 + 8, 8), in_max=cv.c(h * 16 + 8, 8), in_values=wk2[k]),
                          reads=['work%d' % k, 'cvb%d' % k], writes=['cpos%d' % k])
            v3 = [(16, 8), (1, 16)]
            fw.op('dve', lambda e: e.tensor_tensor(out=gsh.ap(0, v3), in0=cv.ap(0, v3), in1=cv.ap(0, [(16, 8), (0, 16)]), op=ALU.subtract),
                  reads=['cva0', 'cva1', 'cvb0', 'cvb1'], writes=['gsh'])
            fw.op('act', lambda e: e.activation(out=gsh.c(0, 128), in_=gsh.c(0, 128), func=AF.Exp), reads=['gsh'], writes=['gsh'])
            fw.op('dve', lambda e: e.tensor_reduce(out=gs.c(0, 8), in_=gsh.ap(0, v3), axis=AX.X, op=ALU.add), reads=['gsh'], writes=['gs'])
            fw.op('dve', lambda e: e.reciprocal(gs.c(8, 8), gs.c(0, 8)), reads=['gs'], writes=['gr'])
            fw.op('dve', lambda e: e.tensor_tensor(out=gates.ap(tt * 128, v3), in0=gsh.ap(0, v3), in1=gs.ap(8, [(1, 8), (0, 16)]), op=ALU.mult),
                  reads=['gsh', 'gr'], writes=['gates'])
            fw.op('dve', lambda e: e.tensor_scalar(out=apos.c(0, 128), in0=cpos.c(0, 128), scalar1=4, scalar2=None,
                                                   op0=ALU.logical_shift_right), reads=['cpos0', 'cpos1'], writes=['apos'])
            fw.op('dve', lambda e: e.tensor_scalar(out=apos.c(128, 128), in0=cpos.c(0, 128), scalar1=15, scalar2=None,
                                                   op0=ALU.bitwise_and), reads=['cpos0', 'cpos1'], writes=['bpos'])
            fw.op('dve', lambda e: e.tensor_copy(abf.c(0, 256), apos.c(0, 256)), reads=['apos', 'bpos'], writes=['abf'])
            v4 = [(256, 8), (16, 16), (1, 16)]
            for ab in range(2):
                fw.op('dve', lambda e: e.tensor_tensor(out=cand.ap(0, v4), in0=abf.ap(ab * 128, [(16, 8), (1, 16), (0, 16)]),
                                                       in1=iota.ap(0, [(0, 8), (0, 16), (1, 16)]), op=ALU.is_equal),
                      reads=['abf', 'iota', 'cand'], writes=['cand'])
                fw.op('dve', lambda e: e.tensor_tensor(out=cand.ap(0, v4), in0=cand.ap(0, v4),
                                                       in1=tif.ap(ab * 16, [(32, 8), (0, 16), (1, 16)]), op=ALU.mult),
                      reads=['cand', 'tif'], writes=['cand'])
                fw.op('dve', lambda e: e.tensor_reduce(out=isel.ap(ab * 128, v3), in_=cand.ap(0, v4), axis=AX.X, op=ALU.add),
                      reads=['cand'], writes=['isel%d' % ab])
            fw.op('dve', lambda e: e.scalar_tensor_tensor(out=eidf.c(0, 128), in0=isel.c(0, 128), scalar=128.0, in1=isel.c(128, 128),
                                                          op0=ALU.mult, op1=ALU.add), reads=['isel0', 'isel1'], writes=['eidf'])
            fw.op('dve', lambda e: e.tensor_copy(eid.c(tt * 128, 128), eidf.c(0, 128)), reads=['eidf'], writes=['eid'])
    dump(fw, C, gates, 0, 2048, ['gates'], 10000)
    dump(fw, C, eidf, 0, 128, ['eidf'], 12048)
    free_to(C.stack, n1, fw)
    NR = 12
    uv = [alloc(nc, C.stack, 'uv%d' % i, 2048) for i in range(NR)]
    dg = [alloc(nc, C.stack, 'dg%d' % i, 128) for i in range(4)]
    wpg = alloc(nc, C.stack, 'wpg', 8192)
    wpp = alloc(nc, C.stack, 'wpp', 2048)
    gb = alloc(nc, C.stack, 'gb', 2048)
    htd = [alloc(nc, C.stack, 'htd%d' % i, 1024) for i in range(2)]
    pTt = alloc(nc, C.stack, 'pTt', 256)
    y = alloc(nc, C.stack, 'y', 1024)
    h2 = alloc(nc, C.stack, 'h2', 1024)
    h2T = alloc(nc, C.stack, 'h2T', 1024)
    dots = alloc(nc, C.stack, 'dots', 264)
    accS = alloc(nc, C.stack, 'accS', 1024)
    sgp = alloc(nc, C.stack, 'sgp', 512)
    tmp = alloc(nc, C.stack, 'tmpd', 512)
    outt = [alloc(nc, C.stack, 'outt%d' % i, 1024) for i in range(1)]
    sm = alloc(nc, C.stack, 'smdd', 32)
    for q4 in range(4):
        load(fw, wpg, q4 * 2048, d['wpg'], q4 * 2048, 2048, 'wpg', 'ld_w')
    load(fw, wpp, 0, d['wpp'], 0, 2048, 'wpp', 'ld_w')
    fw.dma('sp', lambda q: q.dma_start(out=gb.c(0, 2048), in_=bass.AP(d['ln'], 2048, [[0, 128], [1, 2048]])), 'ld_w', writes=['gb'])
    accb = [C.ps[6], C.ps[7]]
    LA = NR - 2
    total = NT * 128
    DVE_EVERY = 4
    LAST_PE = 126

    def gather(n):
        tt, ex = divmod(n, 128)
        UV = uv[n % NR]
        fw.dma('pool', lambda q: q.indirect_dma_start(out=UV.c(0, 2048), out_offset=None, in_=d['puv'].ap(),
                                                      in_offset=bass.IndirectOffsetOnAxis(ap=eid.c(tt * 128 + ex, 1), axis=0)),
               'g_' + UV.name, reads=['eid'], writes=[UV.name])

    def front(n):
        tt, ex = divmod(n, 128)
        UV = uv[n % NR]
        H = htd[tt % 2]
        if ex == 0:
            fw.dma('sp', lambda q: q.dma_start(out=H.c(0, 1024), in_=dram2(C.h1, 1024, tt * 128, 128, 0, [(1, 1024)])),
                   'ld_h%d' % (tt % 2), reads=['h1_dram%d' % tt], writes=[H.name])
        fw.op('dve', lambda e: e.scalar_tensor_tensor(out=UV.c(0, 1024), in0=UV.c(0, 1024), scalar=1.0, in1=H.c(0, 1024),
                                                      op0=ALU.mult, op1=ALU.mult, accum_out=dots.c(ex, 1)),
              reads=[UV.name, H.name], writes=[UV.name + 'u', 'dot%d' % (n % 4)])
        fw.op('act', lambda e: e.activation(out=dots.c(128 + ex, 1), in_=dots.c(ex, 1), func=AF.Gelu),
              reads=['dot%d' % (n % 4)], writes=['gel%d' % (n % 4)])
        fw.op('act', lambda e: e.activation(out=dots.c(256 + (n % 4), 1), in_=dots.c(128 + ex, 1), func=AF.Copy,
                                            scale=gates.c(tt * 128 + ex, 1)),
              reads=['gel%d' % (n % 4), 'gates'], writes=['gav%d' % (n % 4)])
        if ex % DVE_EVERY != DVE_EVERY - 1:
            DG = dg[n % 4]
            fw.op('act', lambda e: e.activation(out=DG.c(0, 128), in_=ident.c(0, 128), func=AF.Copy, scale=dots.c(256 + (n % 4), 1)),
                  reads=['ident', 'gav%d' % (n % 4)], writes=[DG.name])

    def back(n):
        tt, ex = divmod(n, 128)
        UV = uv[n % NR]
        if ex % DVE_EVERY == DVE_EVERY - 1:
            if ex == DVE_EVERY - 1:
                fw.op('dve', lambda e: e.tensor_scalar(out=accS.c(0, 1024), in0=UV.c(1024, 1024), scalar1=dots.c(256 + (n % 4), 1),
                                                       scalar2=None, op0=ALU.mult),
                      reads=[UV.name, 'gav%d' % (n % 4)], writes=['accS'])
            else:
                fw.op('dve', lambda e: e.scalar_tensor_tensor(out=accS.c(0, 1024), in0=UV.c(1024, 1024), scalar=dots.c(256 + (n % 4), 1),
                                                              in1=accS.c(0, 1024), op0=ALU.mult, op1=ALU.add),
                      reads=[UV.name, 'gav%d' % (n % 4), 'accS'], writes=['accS'])
            return
        DG = dg[n % 4]
        for hf in range(2):
            fw.op('pe', lambda e: e.matmul(accb[hf].c(0, 512), DG.c(0, 128), UV.c(1024 + hf * 512, 512),
                                           start=(ex == 0), stop=(ex == LAST_PE)),
                  reads=[DG.name, UV.name], writes=[accb[hf].name], signal=(hf == 1))

    for n in range(LA):
        gather(n)
    for n in range(total + 1):
        if n + LA < total:
            gather(n + LA)
        if n < total:
            front(n)
        if n >= 1:
            back(n - 1)
        if n < 1 or (n - 1) % 128 != 127:
            continue
        tt = (n - 1) // 128
        H = htd[tt % 2]
        fw.dma('sp', lambda q: q.dma_start(out=pTt.ap(0, [(128, 2), (1, 128)]),
                                           in_=dram2(d['pT'], 4096, 0, 128, tt * 128, [(2048, 2), (1, 128)])),
               'ld_p', writes=['pTt'])
        for hf in range(2):
            fw.op('dve', lambda e: e.scalar_tensor_tensor(out=y.c(hf * 512, 512), in0=H.c(hf * 512, 512), scalar=ALPHA,
                                                          in1=accb[hf].c(0, 512), op0=ALU.mult, op1=ALU.add),
                  reads=[H.name, accb[hf].name], writes=['y'])
        fw.op('dve', lambda e: e.tensor_tensor(out=y.c(0, 1024), in0=y.c(0, 1024), in1=accS.c(0, 1024), op=ALU.add),
              reads=['y', 'accS'], writes=['y'])
        layer_norm_tile(fw, C, y, gb, 0, h2, sm, 'd', eng='dve')
        for a in range(2):
            ps = C.ps[a]
            for k4 in range(4):
                kt = a * 4 + k4
                fw.op('pe', lambda e: e.transpose(ps.c(k4 * 128, 128), h2.c(kt * 128, 128), ident.c(0, 128)),
                      reads=['h2', 'ident'], writes=[ps.name], signal=(k4 == 3))
            vcopy(fw, 'act', h2T.c(a * 512, 512), ps.c(0, 512), [ps.name], ['h2T%d' % a])
        O = outt[0]
        for hf in range(2):
            psg = C.ps[2 + hf]
            psp = C.ps[4 + hf]
            for kt in range(8):
                fw.op('pe', lambda e: e.matmul(psg.c(0, 512), h2T.c(kt * 128, 128), wpg.c(kt * 1024 + hf * 512, 512),
                                               start=(kt == 0), stop=(kt == 7)),
                      reads=['h2T0', 'h2T1', 'wpg'], writes=[psg.name], signal=(kt == 7))
            for kt in range(2):
                fw.op('pe', lambda e: e.matmul(psp.c(0, 512), pTt.c(kt * 128, 128), wpp.c(kt * 1024 + hf * 512, 512),
                                               start=(kt == 0), stop=(kt == 1)),
                      reads=['pTt', 'wpp'], writes=[psp.name], signal=(kt == 1))
            fw.op('act', lambda e: e.activation(out=sgp.c(0, 512), in_=psg.c(0, 512), func=AF.Sigmoid), reads=[psg.name], writes=['sgp'])
            fw.op('dve', lambda e: e.tensor_tensor(out=tmp.c(0, 512), in0=sgp.c(0, 512), in1=psp.c(0, 512), op=ALU.mult),
                  reads=['sgp', psp.name], writes=['tmpd'])
            fw.op('dve', lambda e: e.tensor_tensor(out=O.c(hf * 512, 512), in0=tmp.c(0, 512), in1=h2.c(hf * 512, 512), op=ALU.add),
                  reads=['tmpd', 'h2'], writes=[O.name])
        fw.dma('sp', lambda q: q.dma_start(out=dram2(C.out, 1024, tt * 128, 128, 0, [(1, 1024)]), in_=O.c(0, 1024)),
               'st_out', reads=[O.name])
    free_to(C.stack, n0, fw)


def build(stage=99, debug=False):
    nc = bass.Bass("TRN2", target_bir_lowering=False)
    C = Ctx()
    C.d = {k: nc.dram_tensor(k, shp, F32, kind="ExternalInput") for k, shp in IN_SHAPES.items()}
    dbg_kind = "ExternalOutput" if debug else "Internal"
    C.onT = nc.dram_tensor("onT", [128, 8192], F32, kind=dbg_kind)
    C.odT = nc.dram_tensor("odT", [128, 8192], F32, kind=dbg_kind)
    C.h1 = nc.dram_tensor("h1", [2048, 1024], F32, kind=dbg_kind)
    C.out = nc.dram_tensor("out", [2048, 1024], F32, kind="ExternalOutput")
    fw = FW(nc)
    C.dbg = nc.dram_tensor("dbg", [128, 32768], F32, kind="ExternalOutput") if debug else None
    C.stack = []
    C.ps = [alloc(nc, C.stack, 'ps%d' % i, 512, psum=True) for i in range(8)]
    C.ident = alloc(nc, C.stack, 'ident', 128)
    C.qscale = alloc(nc, C.stack, 'qscale', 8)
    load(fw, C.ident, 0, C.d['ident'], 0, 128, 'ident', 'ld_w')
    load(fw, C.qscale, 0, C.d['qscale'], 0, 8, 'qscale', 'ld_w')
    phase_A(fw, C)
    if stage >= 2:
        phase_B(fw, C)
    if stage >= 3:
        phase_C(fw, C)
    if stage >= 4:
        phase_D(fw, C)
    for s in list(fw.sems):
        if fw.cnt[s] > 0 and s != 'E_sp':
            fw.eng['sp'].wait_ge(fw.sems[s], fw.cnt[s])
    free_to(C.stack, 0)
    fw.close()
    print("instructions:", fw.n_inst)
    return nc


def kernel(**inputs):
    inp = {k: np.asarray(v) for k, v in inputs.items()}
    nc = build(stage=99, debug=False)
    in_maps = []
    for b in range(8):
        dmap = pack_inputs(inp, b)
        in_maps.append({k: np.ascontiguousarray(dmap[k], dtype=np.float32) for k in IN_SHAPES})
    res = run_bass_kernel_spmd(nc, in_maps, core_ids=list(range(8)))
    out = np.stack([np.asarray(res.results[b]['out'], dtype=np.float32) for b in range(8)], axis=0)
    return out
```
